# Optimizing a Trainium2 kernel written in Bass

```python
import jax, jax.numpy as jnp
from jax import lax
import numpy as np

D_MODEL = 2048
BATCH = 2
SEQ = 16384
DEPTH = 1
DEC_BATCH = 8
DEC_SEQ = 64
PAST_LEN = 4096

CHUNK = 64
N_META = 16
HGRN_HEADS = 8
HGRN_DK = 128
HGRN_DV = 128
HGRN_WIDTH = HGRN_HEADS * HGRN_DK
HGRN_BLOCK = CHUNK // 4
CONV_WIDTH = D_MODEL // 2
CONV_K = 31
D_FF = 4 * D_MODEL
EPS = 1e-6
SPLITS = [HGRN_WIDTH, 2 * HGRN_WIDTH, 3 * HGRN_WIDTH, 4 * HGRN_WIDTH,
          4 * HGRN_WIDTH + CONV_WIDTH, 4 * HGRN_WIDTH + 2 * CONV_WIDTH,
          4 * HGRN_WIDTH + 2 * CONV_WIDTH + D_MODEL]
N_IN = 4 * HGRN_WIDTH + 2 * CONV_WIDTH + 2 * D_MODEL

kernel_name = "hgrn2_conformer_gated_stream_step"


def rmsnorm(x, g):
    xf = x.astype(jnp.float32)
    y = xf * lax.rsqrt(jnp.mean(xf * xf, axis=-1, keepdims=True) + EPS)
    return (y * g.astype(jnp.float32)).astype(x.dtype)


def layernorm(x, g, b):
    xf = x.astype(jnp.float32)
    mu = jnp.mean(xf, axis=-1, keepdims=True)
    var = jnp.mean(jnp.square(xf - mu), axis=-1, keepdims=True)
    y = (xf - mu) * lax.rsqrt(var + EPS)
    return (y * g.astype(jnp.float32) + b.astype(jnp.float32)).astype(x.dtype)


def hgrn2_chunkwise(q, k, v, log_f, s0):
    B, T, H, DK = q.shape
    DV = v.shape[-1]
    L = HGRN_BLOCK
    pad = (-T) % L
    if pad:
        pw = ((0, 0), (0, pad), (0, 0), (0, 0))
        q = jnp.pad(q, pw); k = jnp.pad(k, pw); v = jnp.pad(v, pw); log_f = jnp.pad(log_f, pw)
    N = (T + pad) // L

    def blocks(a):
        return a.reshape(B, N, L, H, a.shape[-1]).transpose(1, 0, 3, 2, 4)

    mask = jnp.tril(jnp.ones((L, L), dtype=bool))

    def step(S, xs):
        qb, kb, vb, lfb = xs
        b = jnp.cumsum(lfb, axis=2)
        r = b[:, :, L // 2 - 1:L // 2]
        g = b[:, :, L - 1:]
        scores = jnp.einsum('bhtd,bhsd->bhts', qb * jnp.exp(b - r), kb * jnp.exp(r - b))
        scores = jnp.where(mask, scores, 0.0)
        o = (jnp.einsum('bhts,bhsv->bhtv', scores, vb)
             + jnp.einsum('bhtd,bhdv->bhtv', qb * jnp.exp(b), S))
        S = (jnp.exp(g)[:, :, 0, :, None] * S
             + jnp.einsum('bhsd,bhsv->bhdv', kb * jnp.exp(g - b), vb))
        return S, o

    S, o = lax.scan(step, s0, (blocks(q), blocks(k), blocks(v), blocks(log_f)))
    o = o.transpose(1, 0, 3, 2, 4).reshape(B, N * L, H, DV)[:, :T]
    return o, S


def trunk_layer(x, s0, conv_buf, lb, norm1_g, w_in, hgrn_norm_g, w_proj_h, dw_kernel, dw_bias,
                conv_ln_g, conv_ln_b, w_proj_c, w_out, norm2_g, w_ff1, w_ff2):
    B, T, _ = x.shape
    f32 = jnp.float32
    h = rmsnorm(x, norm1_g)
    z = jnp.einsum('btd,dn->btn', h, w_in)
    q, f_logit, val, og, ca, cb, gh, gc = jnp.split(z, SPLITS, axis=-1)

    f = lb + (1.0 - lb) * jax.nn.sigmoid(f_logit.astype(f32)).reshape(B, T, HGRN_HEADS, HGRN_DK)
    log_f = jnp.log(f)
    k = 1.0 - f
    qh = jax.nn.silu(q.astype(f32)).reshape(B, T, HGRN_HEADS, HGRN_DK)
    vh = val.astype(f32).reshape(B, T, HGRN_HEADS, HGRN_DV)
    o, s_new = hgrn2_chunkwise(qh, k, vh, log_f, s0.astype(f32))
    o = rmsnorm(o, hgrn_norm_g) * jax.nn.silu(og.astype(f32)).reshape(B, T, HGRN_HEADS, HGRN_DV)
    o = o.reshape(B, T, HGRN_HEADS * HGRN_DV).astype(x.dtype)
    branch_h = jnp.einsum('btc,cd->btd', o, w_proj_h)

    u = ca * jax.nn.sigmoid(cb)
    uc = jnp.concatenate([conv_buf.astype(u.dtype), u], axis=1)
    dw = lax.conv_general_dilated(uc, dw_kernel[:, None, :].astype(uc.dtype), window_strides=(1,),
                                  padding='VALID', dimension_numbers=('NWC', 'WIO', 'NWC'),
                                  feature_group_count=CONV_WIDTH) + dw_bias
    new_buf = uc[:, uc.shape[1] - (CONV_K - 1):]
    c = jax.nn.silu(layernorm(dw, conv_ln_g, conv_ln_b))
    branch_c = jnp.einsum('btc,cd->btd', c, w_proj_c)

    merged = jax.nn.sigmoid(gh) * branch_h + jax.nn.sigmoid(gc) * branch_c
    x = x + jnp.einsum('btd,de->bte', merged, w_out)

    hf = jnp.einsum('btd,df->btf', rmsnorm(x, norm2_g), w_ff1)
    x = x + jnp.einsum('btf,fd->btd', jnp.square(jax.nn.relu(hf)), w_ff2)
    return x, s_new, new_buf


def setup_inputs(seed: int = 0) -> dict:
    key = jax.random.key(seed)
    ks = jax.random.split(key, 20)
    nrm = jax.random.normal
    f32 = jnp.float32
    return {
        "x_prompt": nrm(ks[0], (BATCH, SEQ, D_MODEL), f32),
        "x_sample": nrm(ks[1], (DEC_BATCH, DEC_SEQ, D_MODEL), f32),
        "state_hgrn": 0.5 * nrm(ks[2], (DEPTH, DEC_BATCH, HGRN_HEADS, HGRN_DK, HGRN_DV), f32),
        "state_conv": nrm(ks[3], (DEPTH, DEC_BATCH, CONV_K - 1, CONV_WIDTH), f32),
        "meta_tokens": nrm(ks[4], (N_META, D_MODEL), f32),
        "norm1_g": 1.0 + 0.02 * nrm(ks[5], (DEPTH, D_MODEL), f32),
        "w_in": nrm(ks[6], (DEPTH, D_MODEL, N_IN), f32) * D_MODEL ** -0.5,
        "lb_logits": 0.1 * nrm(ks[7], (DEPTH + 1, HGRN_WIDTH), f32),
        "hgrn_norm_g": 1.0 + 0.02 * nrm(ks[8], (DEPTH, HGRN_DV), f32),
        "w_proj_h": nrm(ks[9], (DEPTH, HGRN_HEADS * HGRN_DV, D_MODEL), f32) * (HGRN_HEADS * HGRN_DV) ** -0.5,
        "dw_kernel": nrm(ks[10], (DEPTH, CONV_K, CONV_WIDTH), f32) * CONV_K ** -0.5,
        "dw_bias": 0.02 * nrm(ks[11], (DEPTH, CONV_WIDTH), f32),
        "conv_ln_g": 1.0 + 0.02 * nrm(ks[12], (DEPTH, CONV_WIDTH), f32),
        "conv_ln_b": 0.02 * nrm(ks[13], (DEPTH, CONV_WIDTH), f32),
        "w_proj_c": nrm(ks[14], (DEPTH, CONV_WIDTH, D_MODEL), f32) * CONV_WIDTH ** -0.5,
        "w_out": nrm(ks[15], (DEPTH, D_MODEL, D_MODEL), f32) * D_MODEL ** -0.5,
        "norm2_g": 1.0 + 0.02 * nrm(ks[16], (DEPTH, D_MODEL), f32),
        "w_ff1": nrm(ks[17], (DEPTH, D_MODEL, D_FF), f32) * D_MODEL ** -0.5,
        "w_ff2": nrm(ks[18], (DEPTH, D_FF, D_MODEL), f32) * D_FF ** -0.5,
        "final_norm_g": 1.0 + 0.02 * nrm(ks[19], (D_MODEL,), f32),
    }


def reference(x_prompt, x_sample, state_hgrn, state_conv, meta_tokens, norm1_g, w_in, lb_logits,
              hgrn_norm_g, w_proj_h, dw_kernel, dw_bias, conv_ln_g, conv_ln_b, w_proj_c, w_out,
              norm2_g, w_ff1, w_ff2, final_norm_g):
    lb_all = jnp.cumsum(jax.nn.softmax(lb_logits.astype(jnp.float32), axis=0), axis=0)

    xp = jnp.concatenate([jnp.broadcast_to(meta_tokens[None].astype(x_prompt.dtype), (BATCH, N_META, D_MODEL)),
                          x_prompt], axis=1)
    xs = x_sample
    hp_list, cp_list, hs_list, cs_list = [], [], [], []
    for l in range(DEPTH):
        lb = lb_all[l].reshape(HGRN_HEADS, HGRN_DK)
        wts = (lb, norm1_g[l], w_in[l], hgrn_norm_g[l], w_proj_h[l], dw_kernel[l], dw_bias[l],
               conv_ln_g[l], conv_ln_b[l], w_proj_c[l], w_out[l], norm2_g[l], w_ff1[l], w_ff2[l])
        s0p = jnp.zeros((BATCH, HGRN_HEADS, HGRN_DK, HGRN_DV), jnp.float32)
        c0p = jnp.zeros((BATCH, CONV_K - 1, CONV_WIDTH), xp.dtype)
        xp, sp, cp = trunk_layer(xp, s0p, c0p, *wts)
        xs, ss, cs = trunk_layer(xs, state_hgrn[l], state_conv[l], *wts)
        hp_list.append(sp.astype(state_hgrn.dtype)); cp_list.append(cp.astype(state_conv.dtype))
        hs_list.append(ss.astype(state_hgrn.dtype)); cs_list.append(cs.astype(state_conv.dtype))

    y_prompt = rmsnorm(xp, final_norm_g)[:, N_META:]
    y_sample = rmsnorm(xs, final_norm_g)
    new_hgrn_prompt = jnp.stack(hp_list, axis=0)
    new_conv_prompt = jnp.stack(cp_list, axis=0)
    new_hgrn_sample = jnp.stack(hs_list, axis=0)
    new_conv_sample = jnp.stack(cs_list, axis=0)
    return (y_prompt, y_sample, new_hgrn_prompt, new_conv_prompt, new_hgrn_sample, new_conv_sample)
```

```python
import contextlib
import numpy as np
import ml_dtypes
import concourse.bass as bass
import concourse.mybir as mybir
from concourse.bass_utils import run_bass_kernel_spmd

F32 = mybir.dt.float32
BF16 = mybir.dt.bfloat16
AF = mybir.ActivationFunctionType
ALU = mybir.AluOpType

D = 2048
NH = 8
HW = 1024
CW = 1024
CK = 31
DFF = 8192
NIN = 10240
EPS = 1e-6
N_META = 16
SEQ = 16384
CHUNK_TOK = 4096
DEC_SEQ = 64
KC = D // 128

OFF_Q, OFF_F, OFF_V, OFF_OG, OFF_CA, OFF_CB, OFF_GH, OFF_GC = 0, 1024, 2048, 3072, 4096, 5120, 6144, 8192

C_ID = 0
C_SU = 128
C_M128 = 256
C_M64 = 384
C_RG128 = 512
C_RG64 = 514
C_O128 = 516
C_OC = 644
C_MASK = 772
NCONST = C_MASK + 8 * 128


def make_consts():
    c = np.zeros((128, NCONST), np.float32)
    s = np.arange(128)[:, None]
    t = np.arange(128)[None, :]
    c[:, C_ID:C_ID + 128] = (s == t)
    c[:, C_SU:C_SU + 128] = (s > t)
    c[:, C_M128:C_M128 + 128] = (s <= t).astype(np.float32) - (s <= 63).astype(np.float32)
    c[:, C_M64:C_M64 + 128] = (s <= t).astype(np.float32) - (s <= 31).astype(np.float32)
    c[:, C_RG128] = (np.arange(128) <= 63)
    c[:, C_RG128 + 1] = 1.0
    c[:, C_RG64] = (np.arange(128) <= 31)
    c[:, C_RG64 + 1] = 1.0
    c[:, C_O128:C_O128 + 128] = 1.0 / 128.0
    c[:, C_OC:C_OC + 128] = 1.0 / 1024.0
    for h in range(8):
        c[:, C_MASK + h * 128:C_MASK + (h + 1) * 128] = (s <= t)
    return c


class Sched:
    ENG = ("pe", "act", "dve", "pool", "sp")

    def __init__(self):
        self.ops = {e: [] for e in self.ENG}
        self.lastw = {}
        self.readers = {}
        self.chan_count = {}
        self.alias = {}

    def _exp(self, keys):
        out = []
        for k in keys:
            a = self.alias.get(k)
            if a is None:
                out.append(k)
            else:
                out.extend(a)
        return out

    def add(self, eng, fn, reads=(), writes=(), chan=None):
        reads = self._exp(reads)
        writes = self._exp(writes)
        deps = []
        for k in reads:
            ev = self.lastw.get(k)
            if ev is not None:
                deps.append(ev)
        for k in writes:
            ev = self.lastw.get(k)
            if ev is not None:
                deps.append(ev)
            r = self.readers.get(k)
            if r:
                deps.extend((kk[0], kk[1], v) for kk, v in r.items())
        if chan is None:
            ev = ("E", eng, len(self.ops[eng]))
        else:
            c = self.chan_count.get(chan, 0)
            self.chan_count[chan] = c + 1
            ev = ("D", chan, c)
        self.ops[eng].append({"fn": fn, "deps": deps, "ev": ev, "chan": chan, "sig": False, "eng": eng})
        for k in reads:
            r = self.readers.setdefault(k, {})
            kk = (ev[0], ev[1])
            if r.get(kk, -1) < ev[2]:
                r[kk] = ev[2]
        for k in writes:
            self.lastw[k] = ev
            self.readers[k] = {}
        return ev

    def finalize(self):
        for e in self.ENG:
            known = {}
            for op in self.ops[e]:
                need = {}
                for (kind, ident, idx) in op["deps"]:
                    if kind == "E" and ident == e and e in ("pe", "sp"):
                        continue
                    kk = (kind, ident)
                    if known.get(kk, -1) >= idx:
                        continue
                    if need.get(kk, -1) < idx:
                        need[kk] = idx
                for kk, idx in need.items():
                    known[kk] = idx
                    if kk[0] == "E":
                        self.ops[kk[1]][idx]["sig"] = True
                op["need"] = need
        self.cnt = {}
        for e in self.ENG:
            c = 0
            arr = []
            for op in self.ops[e]:
                if op["sig"] and op["chan"] is None:
                    c += 1
                arr.append(c)
            self.cnt[e] = arr

    def emit(self, e, engine, sems, chan_sems):
        for op in self.ops[e]:
            for (kind, ident), idx in op["need"].items():
                if kind == "E":
                    engine.wait_ge(sems[ident], self.cnt[ident][idx])
                else:
                    engine.wait_ge(chan_sems[ident], 16 * (idx + 1))
            ins = op["fn"](engine)
            if op["chan"] is not None:
                ins.then_inc(chan_sems[op["chan"]], 16)
            elif op["sig"]:
                ins.then_inc(sems[e], 1)


class Cfg:
    def __init__(self, n_hist, n_main, ns=1, nslot=5, sample=True):
        self.n_hist = n_hist
        self.n_main = n_main
        self.ns = ns
        self.T = ns * 128
        self.nslot = nslot
        self.sample = sample
        self.rows = (n_hist + n_main) * self.T


def build_program(cfg):
    nc = bass.Bass("TRN2", target_bir_lowering=False)
    S = Sched()
    T = cfg.T
    NS = cfg.ns
    es = contextlib.ExitStack()

    def din(name, shape, dt=F32):
        return nc.dram_tensor(name, list(shape), dt, kind="ExternalInput").ap()

    def dout(name, shape, dt=F32):
        return nc.dram_tensor(name, list(shape), dt, kind="ExternalOutput").ap()

    xrows = din("xrows", [cfg.rows, D])
    xs_in = din("xs", [DEC_SEQ, D])
    s0_in = din("s0", [NH, 128, 128])
    c0_in = din("c0", [CK - 1, CW])
    w_in = din("w_in", [D, NIN])
    w_ph = din("w_proj_h", [HW, D])
    w_pc = din("w_proj_c", [CW, D])
    w_out = din("w_out", [D, D])
    w_ff1 = din("w_ff1", [D, DFF])
    w_ff2 = din("w_ff2", [DFF, D])
    g1_in = din("norm1_g", [1, D])
    g2_in = din("norm2_g", [1, D])
    gf_in = din("final_norm_g", [1, D])
    lbl_in = din("lb_logits", [2, HW])
    hg_in = din("hgrn_norm_g", [128, 1])
    dwk_in = din("dw_kernel", [CK, CW])
    dwb_in = din("dw_bias", [CW, 1])
    lng_in = din("conv_ln_g", [CW, 1])
    lnb_in = din("conv_ln_b", [CW, 1])
    consts_in = din("consts", [128, NCONST])

    y_main = dout("y_main", [cfg.n_main * T, D])
    y_s = dout("y_s", [DEC_SEQ, D])
    sp_out = dout("s_p", [NH, 128, 128])
    cp_out = dout("c_p", [CK - 1, CW])
    ss_out = dout("s_s", [NH, 128, 128])
    cs_out = dout("c_s", [CK - 1, CW])

    dbg = None
    if getattr(cfg, "debug", False):
        dbg = {"x1": dout("dbg_x1", [128, D]), "m": dout("dbg_m", [128, KC * T], BF16),
               "on": dout("dbg_on", [128, NH * T], BF16), "c": dout("dbg_c", [128, 8 * T], BF16),
               "x2": dout("dbg_x2", [128, D])}
    dbg_done = set()

    def dbg_dump(name, src_ap, rkeys):
        if dbg is None or name in dbg_done:
            return
        dbg_done.add(name)
        S.add("sp", lambda e: e.dma_start(out=dbg[name][:, :], in_=src_ap), reads=rkeys,
              writes=[("yout", "dbg" + name)], chan=("dbg", name))

    NP_IN = NIN // 256
    NP_PH = D // 512
    NP_OUT = D // 256
    NP_FF1 = DFF // 256
    NP_FF2 = 4 * (D // 256)
    wb_in = nc.dram_tensor("wb_in", [NP_IN, 128, 4096], BF16).ap()
    wb_ph = nc.dram_tensor("wb_ph", [NP_PH, 128, 4096], BF16).ap()
    wb_pc = nc.dram_tensor("wb_pc", [NP_PH, 128, 4096], BF16).ap()
    wb_out = nc.dram_tensor("wb_out", [NP_OUT, 128, 4096], BF16).ap()
    wb_ff1 = nc.dram_tensor("wb_ff1", [NP_FF1, 128, 4096], BF16).ap()
    wb_ff2 = nc.dram_tensor("wb_ff2", [NP_FF2, 128, 4096], BF16).ap()

    def sb(name, shape, dt=F32):
        return es.enter_context(nc.sbuf_tensor("s_" + name, list(shape), dt))

    consts = sb("consts", [128, NCONST])
    constb = sb("constb", [128, 128], BF16)
    grow = sb("grow", [128, D])
    lbrow = sb("lbrow", [128, HW])
    omlrow = sb("omlrow", [128, HW])
    hgcol = sb("hgcol", [128, 1])
    epscol = sb("epscol", [128, 1])
    dwk = sb("dwk", [128, 8, CK])
    dwb = sb("dwb", [128, 8])
    lng = sb("lng", [128, 8])
    lnb = sb("lnb", [128, 8])
    xin = sb("xin", [128, NS, D])
    hbuf0_ = sb("hbuf0", [128, D], BF16)
    hbufs = [hbuf0_, hbuf0_]
    stat = sb("stat", [128, 16])
    rgs = sb("rgs", [128, NH, 2])
    erg = sb("erg", [128, NH, 2])
    Sst = sb("Sst", [128, NH, 128])
    Sbf = sb("Sbf", [128, NH, 128], BF16)
    histtmp = sb("histtmp", [128, 8, 30])
    ftmp = [sb(f"ftmp{j}", [128, T]) for j in range(2)]

    u = T / 32.0
    ARENA_KB = int(np.ceil(max(3 * u + 36 + u / 2, 5 * u, 4.5 * u + 2)))
    arena = sb("arena", [128, ARENA_KB * 256])
    names = {}

    def regrange(key, lo, hi):
        S.alias[key] = [("pg", p) for p in range(int(lo // 1024), int((hi + 1023) // 1024))]

    def av(name, off_kb, shape, dt=F32):
        esz = 4 if dt == F32 else 2
        nel = int(np.prod(shape[1:]))
        lo = int(round(off_kb * 1024))
        assert lo % 32 == 0 and lo + nel * esz <= ARENA_KB * 1024, (name, lo, nel * esz)
        v = arena[:, lo // 4:(lo + nel * esz + 3) // 4]
        if dt == BF16:
            v = v.bitcast(BF16)
        if len(shape) == 3:
            v = v.rearrange("p (a b) -> p a b", b=shape[2])
        regrange((name,), lo, lo + nel * esz)
        names[id(v)] = name
        v_off[name] = lo
        return v
    v_off = {}

    hT = av("hT", 0, [128, KC, T], BF16)
    h2T = av("h2T", 0, [128, KC, T], BF16)
    qs = av("qs", u, [128, NH, T])
    sog = av("sog", 2 * u, [128, NH, T])
    b0 = 3 * u
    tma = av("tma", b0, [128, HW])
    tml = av("tml", b0 + 4, [128, HW])
    tmk = av("tmk", b0 + 8, [128, HW])
    ktb = av("ktb", b0 + 12, [128, HW], BF16)
    vb = av("vb", b0 + 14, [128, HW], BF16)
    e1 = av("e1", b0 + 16, [128, NH, 128])
    e2 = av("e2", b0 + 20, [128, NH, 128])
    qp = av("qp", b0 + 24, [128, NH, 128], BF16)
    kp = av("kp", b0 + 26, [128, NH, 128], BF16)
    qh = av("qh", b0 + 28, [128, NH, 128], BF16)
    stb = av("stb", b0 + 30, [128, NH, 128], BF16)
    onT = av("onT", b0 + 36, [128, NH, T], BF16)
    for j in range(4):
        regrange(("tma", j), v_off["tma"] + j * 1024, v_off["tma"] + (j + 1) * 1024)
        regrange(("vb", j), v_off["vb"] + j * 512, v_off["vb"] + (j + 1) * 512)
    for h in range(NH):
        S.alias[("qh", h)] = S.alias[("qh",)]
    lbtmp = tma
    dwkT = tml
    cbuf = tmk
    osq = e1
    rstd_o = e2
    uc = av("uc", u, [128, 8, 30 + T])
    dwa = av("dwa", 2 * u + 1, [128, 8, T])
    for c in range(8):
        regrange(("dwa", c), v_off["dwa"] + c * T * 4, v_off["dwa"] + (c + 1) * T * 4)
    sqt = [av("sq0", 3 * u + 1, [128, T])]
    cT = av("cT", 3 * u + 1 + u / 8, [128, 8, T], BF16)
    d0 = 3 * u + 1 + u / 8 + u / 2
    mu = av("mu", d0, [128, T])
    musq = av("musq", d0 + u / 8, [128, T])
    rstd_c = musq
    tmp1 = sqt[0]
    tmp3 = av("tmp3", d0 + 2 * u / 8, [128, T])
    tmp4 = av("tmp4", d0 + 3 * u / 8, [128, T])
    mT = av("mT", d0 + 4 * u / 8, [128, KC, T], BF16)
    assert d0 + 4 * u / 8 + u <= b0 + 36 or NS < 4
    for dch_ in range(KC):
        regrange(("mT", dch_), v_off["mT"] + dch_ * T * 2, v_off["mT"] + (dch_ + 1) * T * 2)
    hfT = av("hfT", u, [128, DFF // 128, T], BF16)
    for fc in range(DFF // 128):
        regrange(("hfT", fc), v_off["hfT"] + fc * T * 2, v_off["hfT"] + (fc + 1) * T * 2)
    wslots = [sb(f"ws{i}", [128, 4096], BF16) for i in range(cfg.nslot)]

    psum = [es.enter_context(nc.psum_tensor(f"ps{i}", [128, 1024], F32)) for i in range(4)]

    class PS:
        n1 = 0
        n2 = 0

    def ps1():
        i = PS.n1 % 8
        PS.n1 += 1
        return psum[i // 2][:, (i % 2) * 512:(i % 2) * 512 + 512], [("ps", i)]

    def ps2():
        i = PS.n2 % 4
        PS.n2 += 1
        PS.n1 = max(PS.n1, 0)
        return psum[i], [("ps", 2 * i), ("ps", 2 * i + 1)]

    class WS:
        n = 0

    def wload(src_ap, src_key, live=()):
        s = WS.n % cfg.nslot
        while ("ws", s) in live:
            WS.n += 1
            s = WS.n % cfg.nslot
        WS.n += 1
        S.add("sp", lambda e, s=s, src_ap=src_ap: e.dma_start(out=wslots[s][:], in_=src_ap),
              reads=[src_key], writes=[("ws", s)], chan=("ws", s))
        return wslots[s], ("ws", s)

    conv_q = []

    def convert(name, wb, pieces, src_fn, chan):
        pieces = list(pieces)
        for n, p in enumerate(pieces):
            conv_q.append((name, wb, p, src_fn(p), chan, pieces if n == len(pieces) - 1 else None))

    def emit_conv(n):
        for _ in range(n):
            if not conv_q:
                return
            name, wb, p, src, chan, fin = conv_q.pop(0)
            ev = S.add("pool", lambda e, p=p, src=src, wb=wb: e.dma_start(
                out=wb[p].rearrange("q (a b) -> q a b", b=src.shape[-1]), in_=src),
                writes=[(name, p)], chan=("cv", chan))
            if fin is not None:
                for pp in fin:
                    S.lastw[(name, pp)] = ev

    w_in_v = w_in.rearrange("(kc p) n -> p kc n", p=128)
    w_ph_v = w_ph.rearrange("(kc p) n -> p kc n", p=128)
    w_pc_v = w_pc.rearrange("(kc p) n -> p kc n", p=128)
    w_out_v = w_out.rearrange("(kc p) n -> p kc n", p=128)
    w_ff1_v = w_ff1.rearrange("(kc p) n -> p kc n", p=128)
    w_ff2_v = w_ff2.rearrange("(fb fc p) n -> p fb fc n", p=128, fc=16)

    def cload(dst, src, eng="sp", slow=False):
        return S.add(eng, lambda e: e.dma_start(out=dst, in_=src, allow_slow_non_contiguous=slow),
                     writes=[], chan=("const",))

    cevs = []
    cevs.append(cload(consts[:], consts_in[:, :]))
    cevs.append(cload(lbrow[:], lbl_in[0:1, :].to_broadcast([128, HW])))
    cevs.append(cload(lbtmp[:], lbl_in[1:2, :].to_broadcast([128, HW])))
    cevs.append(cload(hgcol[:], hg_in[:, :]))
    cevs.append(cload(dwkT[0:CK, :], dwk_in[:, :]))
    cevs.append(cload(dwb[:], dwb_in.rearrange("(c p) o -> p (c o)", p=128), slow=True))
    cevs.append(cload(lng[:], lng_in.rearrange("(c p) o -> p (c o)", p=128), slow=True))
    cevs.append(cload(lnb[:], lnb_in.rearrange("(c p) o -> p (c o)", p=128), slow=True))
    CK_ALL = ("constk",)
    S.lastw[CK_ALL] = cevs[-1]
    RC = [CK_ALL]

    in_src = lambda p: w_in_v[:, :, p * 256:(p + 1) * 256]
    PF_, PV_, PQ_, POG_, PCA_, PCB_, PGH_, PGC_ = (o // 256 for o in (OFF_F, OFF_V, OFF_Q, OFF_OG, OFF_CA, OFF_CB,
                                                                      OFF_GH, OFF_GC))
    convert("wb_in", wb_in, list(range(PCA_, PCA_ + 4)) + list(range(PCB_, PCB_ + 4)), in_src, "in_c")
    convert("wb_in", wb_in, list(range(PQ_, PQ_ + 4)) + list(range(POG_, POG_ + 4)), in_src, "in_q")
    convert("wb_in", wb_in, list(range(PF_, PF_ + 4)) + list(range(PV_, PV_ + 4)), in_src, "in_fv")
    convert("wb_in", wb_in, list(range(PGH_, PGH_ + 8)) + list(range(PGC_, PGC_ + 8)), in_src, "in_g")
    convert("wb_ph", wb_ph, range(NP_PH), lambda p: w_ph_v[:, :, p * 512:(p + 1) * 512], "ph")
    convert("wb_pc", wb_pc, range(NP_PH), lambda p: w_pc_v[:, :, p * 512:(p + 1) * 512], "pc")
    convert("wb_out", wb_out, range(NP_OUT), lambda p: w_out_v[:, :, p * 256:(p + 1) * 256], "out")
    convert("wb_ff1", wb_ff1, range(0, 16), lambda p: w_ff1_v[:, :, p * 256:(p + 1) * 256], "ff1a")
    convert("wb_ff1", wb_ff1, range(16, 32), lambda p: w_ff1_v[:, :, p * 256:(p + 1) * 256], "ff1b")
    ff2_src = lambda p: w_ff2_v[:, p // 8, :, (p % 8) * 256:(p % 8) * 256 + 256]
    convert("wb_ff2", wb_ff2, [fb * 8 + d for d in range(0, 4) for fb in range(4)], ff2_src, "ff2a")
    convert("wb_ff2", wb_ff2, [fb * 8 + d for d in range(4, 8) for fb in range(4)], ff2_src, "ff2b")

    S.add("dve", lambda e: e.tensor_tensor(out=lbtmp[:], in0=lbrow[:], in1=lbtmp[:], op=ALU.subtract),
          reads=RC, writes=[("tma", 0), ("tma", 1), ("tma", 2), ("tma", 3)])
    S.add("act", lambda e: e.activation(out=lbrow[:], in_=lbtmp[:], func=AF.Sigmoid),
          reads=[("tma", 0), ("tma", 1), ("tma", 2), ("tma", 3)], writes=[("lbrow",)])
    S.add("dve", lambda e: e.tensor_scalar(out=omlrow[:], in0=lbrow[:], scalar1=-1.0, scalar2=1.0,
                                           op0=ALU.mult, op1=ALU.add),
          reads=[("lbrow",)], writes=[("omlrow",)])
    S.add("pool", lambda e: e.memset(epscol[:], EPS), writes=[("epscol",)])
    S.add("dve", lambda e: e.tensor_copy(out=constb[:], in_=consts[:, C_ID:C_ID + 128]),
          reads=RC, writes=[("constb",)])
    RC = RC + [("epscol",)]
    RC2 = RC + [("lbrow",), ("omlrow",), ("constb",)]
    for c in range(8):
        pt, pk = ps1()
        S.add("pe", lambda e, c=c, pt=pt: e.matmul(pt[:, 0:CK], lhsT=dwkT[0:CK, c * 128:(c + 1) * 128],
                                                    rhs=consts[0:CK, C_ID:C_ID + CK], start=True, stop=True),
              reads=RC + [("tml",)], writes=pk)
        S.add("dve", lambda e, c=c, pt=pt: e.tensor_copy(out=dwk[:, c, :], in_=pt[:, 0:CK]),
              reads=pk, writes=[("dwk",)])
    RC2 = RC2 + [("dwk",)]

    ident_f = consts[:, C_ID:C_ID + 128]

    def key_of(t):
        return (names[id(t)],)

    class GR:
        cur = None

    def load_g(which):
        if GR.cur == which:
            return
        GR.cur = which
        src = {"g1": g1_in, "g2": g2_in, "gf": gf_in}[which]
        S.add("sp", lambda e: e.dma_start(out=grow[:], in_=src[0:1, :].to_broadcast([128, D])),
              writes=[("grow",)], chan=("grow",))

    def norm_to_hT(i, st, g_row, dstT, tok0):
        xk = ("xin", i)
        par = i % 2
        hbuf = hbufs[par]
        junk = hbuf
        hk = ("hbuf", 0)
        c0_, c1_, c2_ = 3 * par, 3 * par + 1, 3 * par + 2
        S.add("pool", lambda e: e.memset(stat[:, c0_:c0_ + 1], 0.0), writes=[("stat", c0_)])
        S.add("act", lambda e: e.activation(out=junk[:st, :], in_=xin[:st, i, :], func=AF.Square,
                                            accum_out=stat[:st, c0_:c0_ + 1]),
              reads=[xk, ("stat", c0_)], writes=[hk, ("stat", c0_)])
        S.add("act", lambda e: e.activation(out=stat[:st, c1_:c1_ + 1], in_=stat[:st, c0_:c0_ + 1], func=AF.Ln,
                                            scale=1.0 / D, bias=epscol[:st, 0:1]),
              reads=[("stat", c0_)] + RC, writes=[("stat", c1_)])
        S.add("act", lambda e: e.activation(out=stat[:st, c2_:c2_ + 1], in_=stat[:st, c1_:c1_ + 1], func=AF.Exp,
                                            scale=-0.5),
              reads=[("stat", c1_)], writes=[("stat", c2_)])
        S.add("dve", lambda e: e.scalar_tensor_tensor(out=hbuf[:st, :], in0=xin[:st, i, :], scalar=stat[:st, c2_:c2_ + 1],
                                                      in1=grow[:st, :], op0=ALU.mult, op1=ALU.mult),
              reads=[xk, ("stat", c2_), ("grow",)] + RC, writes=[hk])
        for half in range(2):
            pt, pk = ps1()
            ptb = pt.bitcast(BF16)
            for j in range(8):
                kc = half * 8 + j
                S.add("pe", lambda e, kc=kc, j=j, ptb=ptb: e.transpose(
                    ptb[:, j * 128:j * 128 + st], hbuf[:st, kc * 128:(kc + 1) * 128], constb[:st, :st]),
                    reads=[hk] + RC2, writes=pk)
            eng = "act" if half == 0 else "dve"
            src = ptb.rearrange("p (j t) -> p j t", t=128)[:, :, 0:st]
            dst = dstT[:, half * 8:(half + 1) * 8, tok0:tok0 + st]
            if eng == "act":
                S.add("act", lambda e, src=src, dst=dst: e.activation(out=dst, in_=src, func=AF.Copy),
                      reads=pk, writes=[key_of(dstT)])
            else:
                S.add("dve", lambda e, src=src, dst=dst: e.tensor_copy(out=dst, in_=src),
                      reads=pk, writes=[key_of(dstT)])

    def proj_fm(piece_src, piece_key, col_chunks, rhsT, ntok, consume):
        ws, wk = wload(piece_src, piece_key)
        wv = ws[:].rearrange("p (k c) -> p k c", c=256)
        for ch in range(col_chunks):
            pt, pk = ps1()
            for kc in range(KC):
                S.add("pe", lambda e, kc=kc, ch=ch, pt=pt, wv=wv: e.matmul(
                    pt[:, 0:ntok], lhsT=wv[:, kc, ch * 128:(ch + 1) * 128], rhs=rhsT[:, kc, 0:ntok],
                    start=(kc == 0), stop=(kc == KC - 1)),
                    reads=[wk, key_of(rhsT)], writes=pk)
            consume(ch, pt, pk)

    def hgrn_tm(i, st, tok0, piece0_f, piece0_v):
        for p4 in range(4):
            ws, wk = wload(wb_in[piece0_f + p4], ("wb_in", piece0_f + p4))
            wv = ws[:].rearrange("p (k c) -> p k c", c=256)
            pt, pk = ps1()
            for kc in range(KC):
                S.add("pe", lambda e, kc=kc, pt=pt, wv=wv: e.matmul(
                    pt[:st, 0:256], lhsT=hT[:, kc, tok0:tok0 + st], rhs=wv[:, kc, :],
                    start=(kc == 0), stop=(kc == KC - 1)),
                    reads=[wk, key_of(hT)], writes=pk)
            S.add("act", lambda e, pt=pt, p4=p4: e.activation(out=tma[:st, p4 * 256:(p4 + 1) * 256],
                                                               in_=pt[:st, 0:256], func=AF.Sigmoid),
                  reads=pk, writes=[("tma", p4)])
        for p4 in range(4):
            ws, wk = wload(wb_in[piece0_v + p4], ("wb_in", piece0_v + p4))
            wv = ws[:].rearrange("p (k c) -> p k c", c=256)
            pt, pk = ps1()
            for kc in range(KC):
                S.add("pe", lambda e, kc=kc, pt=pt, wv=wv: e.matmul(
                    pt[:st, 0:256], lhsT=hT[:, kc, tok0:tok0 + st], rhs=wv[:, kc, :],
                    start=(kc == 0), stop=(kc == KC - 1)),
                    reads=[wk, key_of(hT)], writes=pk)
            S.add("dve", lambda e, pt=pt, p4=p4: e.tensor_copy(out=vb[:st, p4 * 256:(p4 + 1) * 256],
                                                                in_=pt[:st, 0:256]),
                  reads=pk, writes=[("vb", p4)])
        TMA = [("tma", j) for j in range(4)]
        S.add("dve", lambda e: e.tensor_tensor(out=tma[:st, :], in0=tma[:st, :], in1=omlrow[:st, :], op=ALU.mult),
              reads=TMA + RC2, writes=TMA)
        S.add("pool", lambda e: e.tensor_tensor(out=tma[:st, :], in0=tma[:st, :], in1=lbrow[:st, :], op=ALU.add),
              reads=TMA + RC2, writes=TMA)
        S.add("act", lambda e: e.activation(out=tml[:st, :], in_=tma[:st, :], func=AF.Ln),
              reads=TMA, writes=[("tml",)])
        S.add("dve", lambda e: e.tensor_scalar(out=tmk[:st, :], in0=tma[:st, :], scalar1=-1.0, scalar2=1.0,
                                               op0=ALU.mult, op1=ALU.add),
              reads=TMA, writes=[("tmk",)])
        pp, ppk = ps2()
        for half in range(2):
            S.add("pe", lambda e, half=half, pp=pp: e.matmul(
                pp[:st, half * 512:(half + 1) * 512], lhsT=consts[:st, C_SU:C_SU + st],
                rhs=tml[:st, half * 512:(half + 1) * 512], start=True, stop=True),
                reads=[("tml",)] + RC, writes=ppk)
        S.add("act", lambda e, pp=pp: e.activation(out=tma[:st, :], in_=pp[:st, :], func=AF.Exp),
              reads=ppk + TMA, writes=TMA)
        S.add("dve", lambda e: e.tensor_tensor(out=ktb[:st, :], in0=tmk[:st, :], in1=tma[:st, :], op=ALU.mult),
              reads=TMA + [("tmk",)], writes=[("ktb",)])

    def hgrn_state_update(st):
        pp, ppk = ps2()
        for h in range(NH):
            S.add("pe", lambda e, h=h, pp=pp: e.matmul(
                pp[:, h * 128:(h + 1) * 128], lhsT=ktb[:st, h * 128:(h + 1) * 128],
                rhs=vb[:st, h * 128:(h + 1) * 128], start=True, stop=True),
                reads=[("ktb",)] + [("vb", j) for j in range(4)], writes=ppk)
        ppv = pp[:].rearrange("p (h v) -> p h v", v=128)
        for h in range(NH):
            S.add("dve", lambda e, h=h, ppv=ppv: e.scalar_tensor_tensor(
                out=Sst[:, h, :], in0=Sst[:, h, :], scalar=erg[:, h, 1:2], in1=ppv[:, h, :],
                op0=ALU.mult, op1=ALU.add),
                reads=ppk + [("erg",), ("Sst",)], writes=[("Sst",)])
        S.add("pool", lambda e: e.tensor_copy(out=Sbf[:], in_=Sst[:]), reads=[("Sst",)], writes=[("Sbf",)])

    def hgrn_decay_cols(st):
        crg = C_RG128 if st == 128 else C_RG64
        pt, pk = ps1()
        for h in range(NH):
            S.add("pe", lambda e, h=h, pt=pt: e.matmul(
                pt[:, h * 2:h * 2 + 2], lhsT=tml[:st, h * 128:(h + 1) * 128], rhs=consts[:st, crg:crg + 2],
                start=True, stop=True),
                reads=[("tml",)] + RC, writes=pk)
        S.add("act", lambda e, pt=pt: e.activation(out=erg[:].rearrange("p h c -> p (h c)"), in_=pt[:, 0:16],
                                                    func=AF.Exp),
              reads=pk, writes=[("erg",)])

    def hgrn_fm(i, st, tok0):
        cm = C_M128 if st == 128 else C_M64
        pb, pbk = ps2()
        pkT, pkTk = ps2()
        for h in range(NH):
            S.add("pe", lambda e, h=h, pb=pb: e.matmul(
                pb[:, h * 128:h * 128 + st], lhsT=tml[:st, h * 128:(h + 1) * 128], rhs=consts[:st, cm:cm + st],
                start=True, stop=True),
                reads=[("tml",)] + RC, writes=pbk)
        for h in range(NH):
            S.add("pe", lambda e, h=h, pkT=pkT: e.matmul(
                pkT[:, h * 128:h * 128 + st], lhsT=tmk[:st, h * 128:(h + 1) * 128], rhs=consts[:st, C_ID:C_ID + st],
                start=True, stop=True),
                reads=[("tmk",)] + RC, writes=pkTk)
        pbv = pb[:].rearrange("p (h t) -> p h t", t=128)[:, :, 0:st]
        pkTv = pkT[:].rearrange("p (h t) -> p h t", t=128)[:, :, 0:st]
        S.add("act", lambda e: e.activation(out=e1[:, :, 0:st], in_=pbv, func=AF.Exp),
              reads=pbk, writes=[("e1",)])
        S.add("act", lambda e: e.activation(out=e2[:, :, 0:st], in_=pbv, func=AF.Exp, scale=-1.0),
              reads=pbk, writes=[("e2",)])
        S.add("dve", lambda e: e.tensor_tensor(out=qp[:, :, 0:st], in0=qs[:, :, tok0:tok0 + st], in1=e1[:, :, 0:st],
                                               op=ALU.mult),
              reads=[("qs",), ("e1",)], writes=[("qp",)])
        S.add("dve", lambda e: e.tensor_tensor(out=kp[:, :, 0:st], in0=pkTv, in1=e2[:, :, 0:st], op=ALU.mult),
              reads=pkTk + [("e2",)], writes=[("kp",)])
        for h in range(NH):
            eng = "dve"
            S.add(eng, lambda e, h=h: e.scalar_tensor_tensor(
                out=qh[:, h, 0:st], in0=qs[:, h, tok0:tok0 + st], scalar=erg[:, h, 0:1], in1=e1[:, h, 0:st],
                op0=ALU.mult, op1=ALU.mult),
                reads=[("qs",), ("e1",), ("erg",)], writes=[("qh", h)])
        psc, psck = ps2()
        for h in range(NH):
            S.add("pe", lambda e, h=h, psc=psc: e.matmul(
                psc[:st, h * 128:h * 128 + st], lhsT=kp[:, h, 0:st], rhs=qp[:, h, 0:st], start=True, stop=True),
                reads=[("kp",), ("qp",)], writes=psck)
        pscv = psc[:].rearrange("p (h t) -> p h t", t=128)[:st, :, 0:st]
        maskv = consts[:, C_MASK:C_MASK + 1024].rearrange("p (h t) -> p h t", t=128)[:st, :, 0:st]
        S.add("dve", lambda e: e.tensor_tensor(out=stb[:st, :, 0:st], in0=pscv, in1=maskv, op=ALU.mult),
              reads=psck + RC, writes=[("stb",)])
        po, pok = ps2()
        for h in range(NH):
            S.add("pe", lambda e, h=h, po=po: e.matmul(
                po[:, h * 128:h * 128 + st], lhsT=vb[:st, h * 128:(h + 1) * 128], rhs=stb[:st, h, 0:st],
                start=True, stop=False),
                reads=[("stb",)] + [("vb", j) for j in range(4)], writes=pok)
            S.add("pe", lambda e, h=h, po=po: e.matmul(
                po[:, h * 128:h * 128 + st], lhsT=Sbf[:, h, :], rhs=qh[:, h, 0:st], start=False, stop=True),
                reads=[("Sbf",), ("qh", h)], writes=pok)
        pov = po[:].rearrange("p (h t) -> p h t", t=128)[:, :, 0:st]
        S.add("act", lambda e: e.activation(out=osq[:, :, 0:st], in_=pov, func=AF.Square),
              reads=pok, writes=[("e1",)])
        pm, pmk = ps2()
        for h2 in range(2):
            for hh in range(4):
                h = h2 * 4 + hh
                S.add("pe", lambda e, h=h, pm=pm: e.matmul(
                    pm[:, h * 128:h * 128 + st], lhsT=consts[:, C_O128:C_O128 + 128], rhs=osq[:, h, 0:st],
                    start=True, stop=True),
                    reads=[("e1",)] + RC, writes=pmk)
        pmv = pm[:].rearrange("p (h t) -> p h t", t=128)[:, :, 0:st]
        S.add("act", lambda e: e.activation(out=rstd_o[:, :, 0:st], in_=pmv, func=AF.Ln, bias=epscol[:, 0:1]),
              reads=pmk + RC, writes=[("e2",)])
        S.add("act", lambda e: e.activation(out=rstd_o[:, :, 0:st], in_=rstd_o[:, :, 0:st], func=AF.Exp, scale=-0.5),
              reads=[("e2",)], writes=[("e2",)])
        S.add("dve", lambda e: e.tensor_tensor(out=osq[:, :, 0:st], in0=pov, in1=rstd_o[:, :, 0:st], op=ALU.mult),
              reads=pok + [("e2",), ("e1",)], writes=[("e1",)])
        S.add("dve", lambda e: e.scalar_tensor_tensor(
            out=onT[:, :, tok0:tok0 + st], in0=osq[:, :, 0:st], scalar=hgcol[:, 0:1], in1=sog[:, :, tok0:tok0 + st],
            op0=ALU.mult, op1=ALU.mult),
            reads=[("e1",), ("sog",)] + RC, writes=[("onT",)])

    def load_x(i, st, src_rows):
        S.add("sp", lambda e: e.dma_start(out=xin[:st, i, :], in_=src_rows), writes=[("xin", i)], chan=("x", i))

    S.add("pool", lambda e: e.memset(Sst[:], 0.0), writes=[("Sst",)])
    S.add("pool", lambda e: e.memset(Sbf[:], 0.0), writes=[("Sbf",)])
    S.add("pool", lambda e: e.memset(histtmp[:], 0.0), writes=[("histtmp",)])

    P_F = OFF_F // 256
    P_V = OFF_V // 256
    P_Q = OFF_Q // 256
    P_OG = OFF_OG // 256
    P_CA = OFF_CA // 256
    P_CB = OFF_CB // 256
    P_GH = OFF_GH // 256
    P_GC = OFF_GC // 256

    def glu_chunks(ntok, c_list=range(8), hTb=None):
        hTb = hT if hTb is None else hTb
        S.add("pool", lambda e: e.tensor_copy(out=uc[:, :, 0:30], in_=histtmp[:]),
              reads=[("histtmp",)], writes=[("uc",)])
        for cp in range(4):
            if not any((2 * cp + ch) in c_list for ch in range(2)):
                continue
            wsa, wka = wload(wb_in[P_CA + cp], ("wb_in", P_CA + cp))
            wsb, wkb = wload(wb_in[P_CB + cp], ("wb_in", P_CB + cp))
            wva = wsa[:].rearrange("p (k c) -> p k c", c=256)
            wvb = wsb[:].rearrange("p (k c) -> p k c", c=256)
            for ch in range(2):
                c = 2 * cp + ch
                pa, pak = ps1()
                pbb, pbk = ps1()
                for kc in range(KC):
                    S.add("pe", lambda e, kc=kc, ch=ch, pa=pa, wva=wva: e.matmul(
                        pa[:, 0:ntok], lhsT=wva[:, kc, ch * 128:(ch + 1) * 128], rhs=hTb[:, kc, 0:ntok],
                        start=(kc == 0), stop=(kc == KC - 1)), reads=[wka, key_of(hTb)], writes=pak)
                for kc in range(KC):
                    S.add("pe", lambda e, kc=kc, ch=ch, pbb=pbb, wvb=wvb: e.matmul(
                        pbb[:, 0:ntok], lhsT=wvb[:, kc, ch * 128:(ch + 1) * 128], rhs=hTb[:, kc, 0:ntok],
                        start=(kc == 0), stop=(kc == KC - 1)), reads=[wkb, key_of(hTb)], writes=pbk)
                S.add("act", lambda e, pbb=pbb: e.activation(out=tmp1[:, 0:ntok], in_=pbb[:, 0:ntok], func=AF.Sigmoid),
                      reads=pbk, writes=[("sq0",)])
                S.add("dve", lambda e, c=c, pa=pa: e.tensor_tensor(out=uc[:, c, 30:30 + ntok], in0=pa[:, 0:ntok],
                                                                   in1=tmp1[:, 0:ntok], op=ALU.mult),
                      reads=pak + [("sq0",)], writes=[("uc",)])

    def shift_hist(ntok):
        S.add("pool", lambda e: e.tensor_copy(out=histtmp[:], in_=uc[:, :, ntok:ntok + 30]),
              reads=[("uc",)], writes=[("histtmp",)])

    if cfg.n_hist > 0:
        assert ARENA_KB >= 88 or NS < 4
        hTs = [hT, av("hT_B", 2 * u, [128, KC, T], BF16)]
        tmB0 = u
        sets = [
            dict(tma=tma, tml=tml, tmk=tmk, ktb=ktb, vb=vb, erg=erg, k="A"),
            dict(tma=av("tmaB", tmB0, [128, HW]), tml=av("tmlB", tmB0 + 4, [128, HW]),
                 tmk=av("tmkB", tmB0 + 8, [128, HW]), ktb=av("ktbB", tmB0 + 12, [128, HW], BF16),
                 vb=av("vbB", tmB0 + 14, [128, HW], BF16), erg=sb("ergB", [128, NH, 2]), k="B"),
        ]
        skeys = [
            dict(tma=[("tma", j) for j in range(4)], tml=[("tml",)], tmk=[("tmk",)], ktb=[("ktb",)],
                 vb=[("vb", j) for j in range(4)], erg=[("erg",)]),
            dict(tma=[("tmaB",)], tml=[("tmlB",)], tmk=[("tmkB",)], ktb=[("ktbB",)], vb=[("vbB",)],
                 erg=[("ergB",)]),
        ]
        res_w = []
        res_k = []
        hw_ev = None
        for n in range(8):
            if n < cfg.nslot:
                dst = wslots[n][:].rearrange("p (k c) -> p k c", c=256)
                kk = ("ws", n)
            else:
                dst = av(f"resw{n}", b0 + 16 + (n - cfg.nslot) * 8, [128, KC, 256], BF16)
                kk = (f"resw{n}",)
            p = (PF_ + n) if n < 4 else (PV_ + n - 4)
            hw_ev = S.add("pool", lambda e, dst=dst, p=p: e.dma_start(out=dst, in_=w_in_v[:, :, p * 256:(p + 1) * 256]),
                          writes=[kk], chan=("hw",))
            res_w.append(dst)
            res_k.append(kk)
        for kk in res_k:
            for pk_ in S._exp([kk]):
                S.lastw[pk_] = hw_ev

        def h_load(t):
            for i in range(NS):
                r0 = t * T + i * 128
                load_x(i, 128, xrows[r0:r0 + 128, :])

        def h_norm(t, hTbuf):
            load_g("g1")
            for i in range(NS):
                norm_to_hT(i, 128, None, hTbuf, i * 128)
            if t + 1 < cfg.n_hist:
                h_load(t + 1)

        def h_p1(g):
            t, i = divmod(g, NS)
            B, K_ = sets[g % 2], skeys[g % 2]
            hTb = hTs[t % 2]
            pf, pfk = ps2()
            for p4 in range(4):
                for kc in range(KC):
                    S.add("pe", lambda e, kc=kc, p4=p4, pf=pf, hTb=hTb, i=i: e.matmul(
                        pf[:, p4 * 256:(p4 + 1) * 256], lhsT=hTb[:, kc, i * 128:(i + 1) * 128], rhs=res_w[p4][:, kc, :],
                        start=(kc == 0), stop=(kc == KC - 1)),
                        reads=[res_k[p4], key_of(hTb)], writes=pfk)
            S.add("act", lambda e, pf=pf, B=B: e.activation(out=B["tma"][:, :], in_=pf[:, :], func=AF.Sigmoid),
                  reads=pfk, writes=K_["tma"])
            S.add("dve", lambda e, B=B: e.tensor_tensor(out=B["tma"][:, :], in0=B["tma"][:, :], in1=omlrow[:, :],
                                                        op=ALU.mult), reads=K_["tma"] + RC2, writes=K_["tma"])
            S.add("pool", lambda e, B=B: e.tensor_tensor(out=B["tma"][:, :], in0=B["tma"][:, :], in1=lbrow[:, :],
                                                         op=ALU.add), reads=K_["tma"] + RC2, writes=K_["tma"])
            S.add("act", lambda e, B=B: e.activation(out=B["tml"][:, :], in_=B["tma"][:, :], func=AF.Ln),
                  reads=K_["tma"], writes=K_["tml"])
            S.add("dve", lambda e, B=B: e.tensor_scalar(out=B["tmk"][:, :], in0=B["tma"][:, :], scalar1=-1.0,
                                                        scalar2=1.0, op0=ALU.mult, op1=ALU.add),
                  reads=K_["tma"], writes=K_["tmk"])

        def h_p2(g):
            t, i = divmod(g, NS)
            B, K_ = sets[g % 2], skeys[g % 2]
            hTb = hTs[t % 2]
            pv, pvk = ps2()
            for p4 in range(4):
                for kc in range(KC):
                    S.add("pe", lambda e, kc=kc, p4=p4, pv=pv, hTb=hTb, i=i: e.matmul(
                        pv[:, p4 * 256:(p4 + 1) * 256], lhsT=hTb[:, kc, i * 128:(i + 1) * 128],
                        rhs=res_w[4 + p4][:, kc, :], start=(kc == 0), stop=(kc == KC - 1)),
                        reads=[res_k[4 + p4], key_of(hTb)], writes=pvk)
            S.add("dve", lambda e, pv=pv, B=B: e.tensor_copy(out=B["vb"][:, :], in_=pv[:, :]),
                  reads=pvk, writes=K_["vb"])

        def h_c1(g):
            B, K_ = sets[g % 2], skeys[g % 2]
            pg, pgk = ps2()
            for half in range(2):
                S.add("pe", lambda e, half=half, pg=pg, B=B: e.matmul(
                    pg[:, half * 512:(half + 1) * 512], lhsT=consts[:, C_SU:C_SU + 128],
                    rhs=B["tml"][:, half * 512:(half + 1) * 512], start=True, stop=True),
                    reads=K_["tml"] + RC, writes=pgk)
            pr, prk = ps1()
            for h in range(NH):
                S.add("pe", lambda e, h=h, pr=pr, B=B: e.matmul(
                    pr[:, h * 2:h * 2 + 2], lhsT=B["tml"][:, h * 128:(h + 1) * 128],
                    rhs=consts[:, C_RG128:C_RG128 + 2], start=True, stop=True),
                    reads=K_["tml"] + RC, writes=prk)
            S.add("act", lambda e, pg=pg, B=B: e.activation(out=B["tma"][:, :], in_=pg[:, :], func=AF.Exp),
                  reads=pgk + K_["tma"], writes=K_["tma"])
            S.add("act", lambda e, pr=pr, B=B: e.activation(out=B["erg"][:].rearrange("p h c -> p (h c)"),
                                                            in_=pr[:, 0:16], func=AF.Exp),
                  reads=prk, writes=K_["erg"])
            S.add("dve", lambda e, B=B: e.tensor_tensor(out=B["ktb"][:, :], in0=B["tmk"][:, :], in1=B["tma"][:, :],
                                                        op=ALU.mult),
                  reads=K_["tma"] + K_["tmk"], writes=K_["ktb"])

        def h_c2(g):
            B, K_ = sets[g % 2], skeys[g % 2]
            pS, pSk = ps2()
            for h in range(NH):
                S.add("pe", lambda e, h=h, pS=pS, B=B: e.matmul(
                    pS[:, h * 128:(h + 1) * 128], lhsT=B["ktb"][:, h * 128:(h + 1) * 128],
                    rhs=B["vb"][:, h * 128:(h + 1) * 128], start=True, stop=True),
                    reads=K_["ktb"] + K_["vb"], writes=pSk)
            pSv = pS[:].rearrange("p (h v) -> p h v", v=128)
            for h in range(NH):
                S.add("dve", lambda e, h=h, pSv=pSv, B=B: e.scalar_tensor_tensor(
                    out=Sst[:, h, :], in0=Sst[:, h, :], scalar=B["erg"][:, h, 1:2], in1=pSv[:, h, :],
                    op0=ALU.mult, op1=ALU.add),
                    reads=pSk + K_["erg"] + [("Sst",)], writes=[("Sst",)])

        NG = cfg.n_hist * NS
        per_tile = (len(conv_q) + cfg.n_hist - 1) // max(cfg.n_hist, 1) + 1
        h_load(0)
        h_norm(0, hTs[0])
        h_p1(0)
        h_p2(0)
        for g in range(NG):
            t, i = divmod(g, NS)
            if i == 2:
                emit_conv(per_tile)
            if i == 1 and t + 1 < cfg.n_hist:
                h_norm(t + 1, hTs[(t + 1) % 2])
            if g + 1 < NG:
                h_p1(g + 1)
            h_c1(g)
            if g + 1 < NG:
                h_p2(g + 1)
            h_c2(g)
        emit_conv(len(conv_q))
        S.add("pool", lambda e: e.tensor_copy(out=Sbf[:], in_=Sst[:]), reads=[("Sst",)], writes=[("Sbf",)])
        glu_chunks(T, hTb=hTs[(cfg.n_hist - 1) % 2])
        shift_hist(T)
    else:
        emit_conv(len(conv_q))

    def full_tile(ntok, nsub, st, x_src_fn, y_dst_fn, emit_conv_out=None, emit_state_out=None):
        for i in range(nsub):
            load_x(i, st, x_src_fn(i))
            load_g("g1")
            norm_to_hT(i, st, None, hT, i * 128)
        for p in range(4):
            def cons_q(ch, pt, pk, p=p):
                h = 2 * p + ch
                S.add("act", lambda e: e.activation(out=qs[:, h, 0:ntok], in_=pt[:, 0:ntok], func=AF.Silu),
                      reads=pk, writes=[("qs",)])
            proj_fm(wb_in[P_Q + p], ("wb_in", P_Q + p), 2, hT, ntok, cons_q)
        for p in range(4):
            def cons_og(ch, pt, pk, p=p):
                h = 2 * p + ch
                S.add("act", lambda e: e.activation(out=sog[:, h, 0:ntok], in_=pt[:, 0:ntok], func=AF.Silu),
                      reads=pk, writes=[("sog",)])
            proj_fm(wb_in[P_OG + p], ("wb_in", P_OG + p), 2, hT, ntok, cons_og)
        for i in range(nsub):
            hgrn_tm(i, st, i * 128, P_F, P_V)
            hgrn_decay_cols(st)
            hgrn_fm(i, st, i * 128)
            hgrn_state_update(st)
        if emit_state_out is not None:
            emit_state_out()
        glu_chunks(ntok)
        def part1_chunks():
            for dp in range(4):
                wsh, wkh = wload(wb_ph[dp], ("wb_ph", dp))
                wvh = wsh[:].rearrange("p (k c) -> p k c", c=512)
                for half in range(2):
                    wsg, wkg = wload(wb_in[P_GH + dp * 2 + half], ("wb_in", P_GH + dp * 2 + half), live=(wkh,))
                    wvg = wsg[:].rearrange("p (k c) -> p k c", c=256)
                    for ch in range(2):
                        dch = dp * 4 + half * 2 + ch
                        c0 = (half * 2 + ch) * 128
                        pbh, pbhk = ps1()
                        pgh, pghk = ps1()
                        for c in range(8):
                            S.add("pe", lambda e, c=c, c0=c0, pbh=pbh, wvh=wvh: e.matmul(
                                pbh[:, 0:ntok], lhsT=wvh[:, c, c0:c0 + 128], rhs=onT[:, c, 0:ntok],
                                start=(c == 0), stop=(c == 7)), reads=[wkh, ("onT",)], writes=pbhk)
                        for kc in range(KC):
                            S.add("pe", lambda e, kc=kc, ch=ch, pgh=pgh, wvg=wvg: e.matmul(
                                pgh[:, 0:ntok], lhsT=wvg[:, kc, ch * 128:(ch + 1) * 128], rhs=hT[:, kc, 0:ntok],
                                start=(kc == 0), stop=(kc == KC - 1)), reads=[wkg, key_of(hT)], writes=pghk)
                        tg, tgk = (tmp3, ("tmp3",)) if dch % 2 == 0 else (tmp4, ("tmp4",))
                        S.add("act", lambda e, pgh=pgh, tg=tg: e.activation(out=tg[:, 0:ntok], in_=pgh[:, 0:ntok],
                                                                            func=AF.Sigmoid), reads=pghk, writes=[tgk])
                        yield (dch, pbh, pbhk, tg, tgk)

        def part1_evac(item):
            dch, pbh, pbhk, tg, tgk = item
            S.add("dve", lambda e: e.tensor_tensor(out=mT[:, dch, 0:ntok], in0=pbh[:, 0:ntok], in1=tg[:, 0:ntok],
                                                   op=ALU.mult),
                  reads=pbhk + [tgk], writes=[("mT", dch)])

        p1 = part1_chunks()
        pending = []
        ntap = 0
        for j in range(CK):
            for c in range(8):
                if j == 0:
                    S.add("dve", lambda e, c=c: e.tensor_scalar(out=dwa[:, c, 0:ntok], in0=uc[:, c, 0:ntok],
                                                                scalar1=dwk[:, c, 0:1], scalar2=dwb[:, c:c + 1],
                                                                op0=ALU.mult, op1=ALU.add),
                          reads=[("uc",)] + RC2, writes=[("dwa", c)])
                else:
                    S.add("dve", lambda e, c=c, j=j: e.scalar_tensor_tensor(
                        out=dwa[:, c, 0:ntok], in0=uc[:, c, j:j + ntok], scalar=dwk[:, c, j:j + 1],
                        in1=dwa[:, c, 0:ntok], op0=ALU.mult, op1=ALU.add),
                        reads=[("uc",), ("dwa", c)] + RC2, writes=[("dwa", c)])
                ntap += 1
                if ntap % 15 == 0:
                    while len(pending) < 2:
                        it = next(p1, None)
                        if it is None:
                            break
                        pending.append(it)
                    if pending and ntap >= 30:
                        part1_evac(pending.pop(0))
        for it in p1:
            pending.append(it)
            if len(pending) >= 2:
                part1_evac(pending.pop(0))
        for it in pending:
            part1_evac(it)
        if emit_conv_out is not None:
            emit_conv_out(ntok)
        shift_hist(ntok)
        DWA = [("dwa", c) for c in range(8)]
        pmu, pmuk = ps1()
        pm2, pm2k = ps1()
        for c in range(8):
            S.add("pe", lambda e, c=c: e.matmul(pmu[:, 0:ntok], lhsT=consts[:, C_OC:C_OC + 128], rhs=dwa[:, c, 0:ntok],
                                                start=(c == 0), stop=(c == 7)), reads=[("dwa", c)] + RC, writes=pmuk)
        for c in range(8):
            sq = sqt[0]
            sqk = ("sq0",)
            S.add("act", lambda e, c=c, sq=sq: e.activation(out=sq[:, 0:ntok], in_=dwa[:, c, 0:ntok], func=AF.Square),
                  reads=[("dwa", c)], writes=[sqk])
            S.add("pe", lambda e, c=c, sq=sq: e.matmul(pm2[:, 0:ntok], lhsT=consts[:, C_OC:C_OC + 128],
                                                       rhs=sq[:, 0:ntok], start=(c == 0), stop=(c == 7)),
                  reads=[sqk] + RC, writes=pm2k)
        S.add("act", lambda e: e.activation(out=mu[:, 0:ntok], in_=pmu[:, 0:ntok], func=AF.Copy),
              reads=pmuk, writes=[("mu",)])
        S.add("dve", lambda e: e.tensor_tensor(out=musq[:, 0:ntok], in0=mu[:, 0:ntok], in1=mu[:, 0:ntok], op=ALU.mult),
              reads=[("mu",)], writes=[("musq",)])
        S.add("dve", lambda e: e.tensor_tensor(out=musq[:, 0:ntok], in0=pm2[:, 0:ntok], in1=musq[:, 0:ntok],
                                               op=ALU.subtract),
              reads=pm2k + [("musq",)], writes=[("musq",)])
        S.add("act", lambda e: e.activation(out=rstd_c[:, 0:ntok], in_=musq[:, 0:ntok], func=AF.Ln,
                                            bias=epscol[:, 0:1]),
              reads=[("musq",)] + RC, writes=[("musq",)])
        S.add("act", lambda e: e.activation(out=rstd_c[:, 0:ntok], in_=rstd_c[:, 0:ntok], func=AF.Exp, scale=-0.5),
              reads=[("musq",)], writes=[("musq",)])
        for c in range(8):
            eng = "pool" if c % 2 else "dve"
            S.add(eng, lambda e, c=c: e.tensor_tensor(out=dwa[:, c, 0:ntok], in0=dwa[:, c, 0:ntok], in1=mu[:, 0:ntok],
                                                      op=ALU.subtract),
                  reads=[("dwa", c), ("mu",)], writes=[("dwa", c)])
            S.add(eng, lambda e, c=c: e.tensor_tensor(out=dwa[:, c, 0:ntok], in0=dwa[:, c, 0:ntok],
                                                      in1=rstd_c[:, 0:ntok], op=ALU.mult),
                  reads=[("dwa", c), ("musq",)], writes=[("dwa", c)])
            S.add("act", lambda e, c=c: e.activation(out=cT[:, c, 0:ntok], in_=dwa[:, c, 0:ntok], func=AF.Silu,
                                                     scale=lng[:, c:c + 1], bias=lnb[:, c:c + 1]),
                  reads=[("dwa", c)] + RC, writes=[("cT",)])
        for dp in range(4):
            wsc, wkc = wload(wb_pc[dp], ("wb_pc", dp))
            wvc = wsc[:].rearrange("p (k c) -> p k c", c=512)
            for half in range(2):
                wsq, wkq = wload(wb_in[P_GC + dp * 2 + half], ("wb_in", P_GC + dp * 2 + half), live=(wkc,))
                wvq = wsq[:].rearrange("p (k c) -> p k c", c=256)
                for ch in range(2):
                    dch = dp * 4 + half * 2 + ch
                    c0 = (half * 2 + ch) * 128
                    pbc, pbck = ps1()
                    pgc, pgck = ps1()
                    for c in range(8):
                        S.add("pe", lambda e, c=c, c0=c0, pbc=pbc, wvc=wvc: e.matmul(
                            pbc[:, 0:ntok], lhsT=wvc[:, c, c0:c0 + 128], rhs=cT[:, c, 0:ntok],
                            start=(c == 0), stop=(c == 7)), reads=[wkc, ("cT",)], writes=pbck)
                    for kc in range(KC):
                        S.add("pe", lambda e, kc=kc, ch=ch, pgc=pgc, wvq=wvq: e.matmul(
                            pgc[:, 0:ntok], lhsT=wvq[:, kc, ch * 128:(ch + 1) * 128], rhs=hT[:, kc, 0:ntok],
                            start=(kc == 0), stop=(kc == KC - 1)), reads=[wkq, key_of(hT)], writes=pgck)
                    tg, tgk = (tmp3, ("tmp3",)) if dch % 2 == 0 else (tmp4, ("tmp4",))
                    S.add("act", lambda e, pgc=pgc, tg=tg: e.activation(out=tg[:, 0:ntok], in_=pgc[:, 0:ntok],
                                                                        func=AF.Sigmoid), reads=pgck, writes=[tgk])
                    S.add("dve", lambda e, pbc=pbc, tg=tg: e.tensor_tensor(out=tg[:, 0:ntok], in0=pbc[:, 0:ntok],
                                                                           in1=tg[:, 0:ntok], op=ALU.mult),
                          reads=pbck + [tgk], writes=[tgk])
                    S.add("dve", lambda e, dch=dch, tg=tg: e.tensor_tensor(out=mT[:, dch, 0:ntok],
                                                                           in0=mT[:, dch, 0:ntok], in1=tg[:, 0:ntok],
                                                                           op=ALU.add),
                          reads=[tgk, ("mT", dch)], writes=[("mT", dch)])
        for dpc in range(8):
            ws, wk = wload(wb_out[dpc], ("wb_out", dpc))
            wv = ws[:].rearrange("p (k c) -> p k c", c=256)
            for i in range(nsub):
                pt, pk = ps1()
                for kc in range(KC):
                    S.add("pe", lambda e, kc=kc, i=i, pt=pt, wv=wv: e.matmul(
                        pt[:st, 0:256], lhsT=mT[:, kc, i * 128:i * 128 + st], rhs=wv[:, kc, :],
                        start=(kc == 0), stop=(kc == KC - 1)), reads=[wk, key_of(mT)], writes=pk)
                S.add("dve", lambda e, i=i, dpc=dpc, pt=pt: e.tensor_tensor(
                    out=xin[:st, i, dpc * 256:(dpc + 1) * 256], in0=xin[:st, i, dpc * 256:(dpc + 1) * 256],
                    in1=pt[:st, 0:256], op=ALU.add),
                    reads=pk + [("xin", i)], writes=[("xin", i)])
        dbg_dump("on", onT[:].rearrange("p h t -> p (h t)"), [("onT",)])
        dbg_dump("c", cT[:].rearrange("p h t -> p (h t)"), [("cT",)])
        dbg_dump("m", mT[:].rearrange("p h t -> p (h t)"), [key_of(mT)])
        dbg_dump("x1", xin[:, 0, :], [("xin", 0)])
        for i in range(nsub):
            load_g("g2")
            norm_to_hT(i, st, None, h2T, i * 128)
        for p in range(NP_FF1):
            def cons_ff(ch, pt, pk, p=p):
                fc = 2 * p + ch
                ft = ftmp[fc % 2]
                ftk = ("ftmp", fc % 2)
                S.add("act", lambda e: e.activation(out=ft[:, 0:ntok], in_=pt[:, 0:ntok], func=AF.Relu),
                      reads=pk, writes=[ftk])
                S.add("pool", lambda e: e.tensor_tensor(out=hfT[:, fc, 0:ntok], in0=ft[:, 0:ntok],
                                                        in1=ft[:, 0:ntok], op=ALU.mult),
                      reads=[ftk], writes=[("hfT", fc)])
            proj_fm(wb_ff1[p], ("wb_ff1", p), 2, h2T, ntok, cons_ff)
        for dpc in range(8):
            pts = [ps1() for _ in range(nsub)]
            for fb in range(4):
                ws, wk = wload(wb_ff2[fb * 8 + dpc], ("wb_ff2", fb * 8 + dpc))
                wv = ws[:].rearrange("p (k c) -> p k c", c=256)
                for fc in range(16):
                    for i in range(nsub):
                        pt, pk = pts[i]
                        S.add("pe", lambda e, fb=fb, fc=fc, i=i, pt=pt, wv=wv: e.matmul(
                            pt[:st, 0:256], lhsT=hfT[:, fb * 16 + fc, i * 128:i * 128 + st], rhs=wv[:, fc, :],
                            start=(fb == 0 and fc == 0), stop=(fb == 3 and fc == 15)),
                            reads=[wk, ("hfT", fb * 16 + fc)], writes=pk)
            for i in range(nsub):
                pt, pk = pts[i]
                S.add("dve", lambda e, i=i, dpc=dpc, pt=pt: e.tensor_tensor(
                    out=xin[:st, i, dpc * 256:(dpc + 1) * 256], in0=xin[:st, i, dpc * 256:(dpc + 1) * 256],
                    in1=pt[:st, 0:256], op=ALU.add),
                    reads=pk + [("xin", i)], writes=[("xin", i)])
        dbg_dump("x2", xin[:, 0, :], [("xin", 0)])
        load_g("gf")
        for i in range(nsub):
            xk = ("xin", i)
            fb_ = 8 + 3 * (i % 2)
            S.add("pool", lambda e, fb_=fb_: e.memset(stat[:, fb_:fb_ + 1], 0.0), writes=[("stat", fb_)])
            S.add("act", lambda e, i=i, fb_=fb_: e.activation(out=hbufs[i % 2][:st, :], in_=xin[:st, i, :],
                                                              func=AF.Square, accum_out=stat[:st, fb_:fb_ + 1]),
                  reads=[xk, ("stat", fb_)], writes=[("hbuf", 0), ("stat", fb_)])
            S.add("act", lambda e, fb_=fb_: e.activation(out=stat[:st, fb_ + 1:fb_ + 2], in_=stat[:st, fb_:fb_ + 1],
                                                         func=AF.Ln, scale=1.0 / D, bias=epscol[:st, 0:1]),
                  reads=[("stat", fb_)] + RC, writes=[("stat", fb_ + 1)])
            S.add("act", lambda e, fb_=fb_: e.activation(out=stat[:st, fb_ + 2:fb_ + 3], in_=stat[:st, fb_ + 1:fb_ + 2],
                                                         func=AF.Exp, scale=-0.5),
                  reads=[("stat", fb_ + 1)], writes=[("stat", fb_ + 2)])
            S.add("dve", lambda e, i=i, fb_=fb_: e.scalar_tensor_tensor(out=xin[:st, i, :], in0=xin[:st, i, :],
                                                                        scalar=stat[:st, fb_ + 2:fb_ + 3],
                                                                        in1=grow[:st, :], op0=ALU.mult, op1=ALU.mult),
                  reads=[xk, ("stat", fb_ + 2), ("grow",)] + RC, writes=[xk])
            dst = y_dst_fn(i)
            S.add("sp", lambda e, i=i, dst=dst: e.dma_start(out=dst, in_=xin[:st, i, :]),
                  reads=[xk], writes=[("yout", len(S.ops["sp"]))], chan=("x", i))

    def conv_out_emitter(dst_ap):
        def f(ntok):
            pp, ppk = ps2()
            for c in range(8):
                S.add("pe", lambda e, c=c, pp=pp: e.matmul(
                    pp[0:30, c * 128:(c + 1) * 128], lhsT=uc[:, c, ntok:ntok + 30], rhs=ident_f,
                    start=True, stop=True), reads=[("uc",)] + RC, writes=ppk)
            S.add("act", lambda e, pp=pp: e.activation(out=cbuf[0:30, :], in_=pp[0:30, :], func=AF.Copy),
                  reads=ppk, writes=[("tmk",)])
            S.add("sp", lambda e: e.dma_start(out=dst_ap, in_=cbuf[0:30, :]),
                  reads=[("tmk",)], writes=[("cout", id(dst_ap))], chan=("cb",))
        return f

    def state_out_emitter(dst_ap):
        def f():
            S.add("sp", lambda e: e.dma_start(out=dst_ap.rearrange("h d v -> d h v"), in_=Sst[:]),
                  reads=[("Sst",)], writes=[("sout", id(dst_ap))], chan=("so",))
        return f

    for t in range(cfg.n_main):
        base = (cfg.n_hist + t) * T
        last = (t == cfg.n_main - 1)
        full_tile(T, NS, 128,
                  lambda i, base=base: xrows[base + i * 128:base + (i + 1) * 128, :],
                  lambda i, t=t: y_main[t * T + i * 128:t * T + (i + 1) * 128, :],
                  emit_conv_out=conv_out_emitter(cp_out[:, :]) if last else None,
                  emit_state_out=state_out_emitter(sp_out) if last else None)

    if cfg.sample:
        S.add("sp", lambda e: e.dma_start(out=Sst[:], in_=s0_in.rearrange("h d v -> d h v")),
              reads=[], writes=[("Sst",)], chan=("so",))
        S.add("pool", lambda e: e.tensor_copy(out=Sbf[:], in_=Sst[:]), reads=[("Sst",)], writes=[("Sbf",)])
        S.add("sp", lambda e: e.dma_start(out=cbuf[0:30, :], in_=c0_in[:, :]), reads=[], writes=[("tmk",)],
              chan=("cb",))
        for c in range(8):
            pt, pk = ps1()
            S.add("pe", lambda e, c=c, pt=pt: e.matmul(pt[:, 0:30], lhsT=cbuf[0:30, c * 128:(c + 1) * 128],
                                                        rhs=consts[0:30, C_ID:C_ID + 30], start=True, stop=True),
                  reads=[("tmk",)] + RC, writes=pk)
            S.add("dve", lambda e, c=c, pt=pt: e.tensor_copy(out=histtmp[:, c, :], in_=pt[:, 0:30]),
                  reads=pk, writes=[("histtmp",)])
        full_tile(DEC_SEQ, 1, DEC_SEQ,
                  lambda i: xs_in[:, :],
                  lambda i: y_s[:, :],
                  emit_conv_out=conv_out_emitter(cs_out[:, :]),
                  emit_state_out=state_out_emitter(ss_out))

    outkeys = [k for k in S.lastw if k[0] in ("yout", "cout", "sout")]
    S.add("sp", lambda e: e.nop(), reads=outkeys)

    S.finalize()
    sems = {e: es.enter_context(nc.semaphore(f"sem_{e}")) for e in Sched.ENG if e != "sp"}
    chan_sems = {}
    for ch in S.chan_count:
        chan_sems[ch] = es.enter_context(nc.semaphore("c_" + "_".join(str(x) for x in ch)))
    with nc.Block() as block:
        @block.tensor
        def _(e):
            S.emit("pe", e, sems, chan_sems)

        @block.scalar
        def _(e):
            S.emit("act", e, sems, chan_sems)

        @block.vector
        def _(e):
            S.emit("dve", e, sems, chan_sems)

        @block.gpsimd
        def _(e):
            S.emit("pool", e, sems, chan_sems)

        @block.sync
        def _(e):
            S.emit("sp", e, sems, chan_sems)
    es.close()
    n_ins = {e: len(S.ops[e]) for e in Sched.ENG}
    return nc, n_ins


HIST_ROWS = 12800
NS_TILE = 4


def core_inputs(cfg, xrows, xs, s0, c0, shared):
    m = dict(shared)
    m["xrows"] = np.ascontiguousarray(xrows, dtype=np.float32)
    m["xs"] = np.ascontiguousarray(xs, dtype=np.float32)
    m["s0"] = np.ascontiguousarray(s0, dtype=np.float32)
    m["c0"] = np.ascontiguousarray(c0, dtype=np.float32)
    return m


def shared_inputs(norm1_g, w_in, lb_logits, hgrn_norm_g, w_proj_h, dw_kernel, dw_bias, conv_ln_g, conv_ln_b,
                  w_proj_c, w_out, norm2_g, w_ff1, w_ff2, final_norm_g):
    f = lambda a: np.ascontiguousarray(np.asarray(a), dtype=np.float32)
    return {
        "w_in": f(w_in[0]), "w_proj_h": f(w_proj_h[0]), "w_proj_c": f(w_proj_c[0]), "w_out": f(w_out[0]),
        "w_ff1": f(w_ff1[0]), "w_ff2": f(w_ff2[0]),
        "norm1_g": f(norm1_g[0]).reshape(1, D), "norm2_g": f(norm2_g[0]).reshape(1, D),
        "final_norm_g": f(final_norm_g).reshape(1, D), "lb_logits": f(lb_logits),
        "hgrn_norm_g": f(hgrn_norm_g[0]).reshape(128, 1), "dw_kernel": f(dw_kernel[0]),
        "dw_bias": f(dw_bias[0]).reshape(CW, 1), "conv_ln_g": f(conv_ln_g[0]).reshape(CW, 1),
        "conv_ln_b": f(conv_ln_b[0]).reshape(CW, 1), "consts": make_consts(),
    }


def kernel(x_prompt, x_sample, state_hgrn, state_conv, meta_tokens, norm1_g, w_in, lb_logits,
           hgrn_norm_g, w_proj_h, dw_kernel, dw_bias, conv_ln_g, conv_ln_b, w_proj_c, w_out,
           norm2_g, w_ff1, w_ff2, final_norm_g):
    x_prompt = np.asarray(x_prompt, dtype=np.float32)
    x_sample = np.asarray(x_sample, dtype=np.float32)
    state_hgrn = np.asarray(state_hgrn, dtype=np.float32)
    state_conv = np.asarray(state_conv, dtype=np.float32)
    meta = np.asarray(meta_tokens, dtype=np.float32)
    T = NS_TILE * 128
    cfg = Cfg(n_hist=HIST_ROWS // T, n_main=CHUNK_TOK // T, ns=NS_TILE)
    shared = shared_inputs(norm1_g, w_in, lb_logits, hgrn_norm_g, w_proj_h, dw_kernel, dw_bias, conv_ln_g,
                           conv_ln_b, w_proj_c, w_out, norm2_g, w_ff1, w_ff2, final_norm_g)
    in_maps = []
    for c in range(8):
        s, j = c // 4, c % 4
        start = j * CHUNK_TOK
        rows = np.zeros((HIST_ROWS + CHUNK_TOK, D), np.float32)
        rows[HIST_ROWS:] = x_prompt[s, start:start + CHUNK_TOK]
        lo = start - HIST_ROWS
        if lo >= 0:
            rows[:HIST_ROWS] = x_prompt[s, lo:start]
        else:
            nx = start
            if nx > 0:
                rows[HIST_ROWS - nx:HIST_ROWS] = x_prompt[s, 0:start]
            rows[HIST_ROWS - nx - N_META:HIST_ROWS - nx] = meta
        in_maps.append(core_inputs(cfg, rows, x_sample[c], state_hgrn[0, c], state_conv[0, c], shared))
    nc, _ = build_program(cfg)
    res = run_bass_kernel_spmd(nc, in_maps, core_ids=list(range(8)))
    R = res.results
    y_prompt = np.stack([np.concatenate([R[s * 4 + j]["y_main"] for j in range(4)], axis=0) for s in range(2)], 0)
    y_sample = np.stack([R[c]["y_s"] for c in range(8)], 0)
    new_hp = np.stack([R[3]["s_p"], R[7]["s_p"]], 0)[None]
    new_cp = np.stack([R[3]["c_p"], R[7]["c_p"]], 0)[None]
    new_hs = np.stack([R[c]["s_s"] for c in range(8)], 0)[None]
    new_cs = np.stack([R[c]["c_s"] for c in range(8)], 0)[None]
    return (y_prompt.astype(np.float32), y_sample.astype(np.float32), new_hp.astype(np.float32),
            new_cp.astype(np.float32), new_hs.astype(np.float32), new_cs.astype(np.float32))
```

```python
import contextlib
import numpy as np
import ml_dtypes
import concourse.bass as bass
import concourse.mybir as mybir
from concourse.bass_utils import run_bass_kernel_spmd

F32 = mybir.dt.float32
BF16 = mybir.dt.bfloat16
AF = mybir.ActivationFunctionType
ALU = mybir.AluOpType

D = 2048
NH = 8
HW = 1024
CW = 1024
CK = 31
DFF = 8192
NIN = 10240
EPS = 1e-6
N_META = 16
SEQ = 16384
CHUNK_TOK = 4096
DEC_SEQ = 64
KC = D // 128

OFF_Q, OFF_F, OFF_V, OFF_OG, OFF_CA, OFF_CB, OFF_GH, OFF_GC = 0, 1024, 2048, 3072, 4096, 5120, 6144, 8192

C_ID = 0
C_SU = 128
C_M128 = 256
C_M64 = 384
C_RG128 = 512
C_RG64 = 514
C_O128 = 516
C_OC = 644
C_MASK = 772
NCONST = C_MASK + 8 * 128


def make_consts():
    c = np.zeros((128, NCONST), np.float32)
    s = np.arange(128)[:, None]
    t = np.arange(128)[None, :]
    c[:, C_ID:C_ID + 128] = (s == t)
    c[:, C_SU:C_SU + 128] = (s > t)
    c[:, C_M128:C_M128 + 128] = (s <= t).astype(np.float32) - (s <= 63).astype(np.float32)
    c[:, C_M64:C_M64 + 128] = (s <= t).astype(np.float32) - (s <= 31).astype(np.float32)
    c[:, C_RG128] = (np.arange(128) <= 63)
    c[:, C_RG128 + 1] = 1.0
    c[:, C_RG64] = (np.arange(128) <= 31)
    c[:, C_RG64 + 1] = 1.0
    c[:, C_O128:C_O128 + 128] = 1.0 / 128.0
    c[:, C_OC:C_OC + 128] = 1.0 / 1024.0
    for h in range(8):
        c[:, C_MASK + h * 128:C_MASK + (h + 1) * 128] = (s <= t)
    return c


class Sched:
    ENG = ("pe", "act", "dve", "pool", "sp")

    def __init__(self):
        self.ops = {e: [] for e in self.ENG}
        self.lastw = {}
        self.readers = {}
        self.chan_count = {}
        self.alias = {}

    def _exp(self, keys):
        out = []
        for k in keys:
            a = self.alias.get(k)
            if a is None:
                out.append(k)
            else:
                out.extend(a)
        return out

    def add(self, eng, fn, reads=(), writes=(), chan=None):
        reads = self._exp(reads)
        writes = self._exp(writes)
        deps = []
        for k in reads:
            ev = self.lastw.get(k)
            if ev is not None:
                deps.append(ev)
        for k in writes:
            ev = self.lastw.get(k)
            if ev is not None:
                deps.append(ev)
            r = self.readers.get(k)
            if r:
                deps.extend((kk[0], kk[1], v) for kk, v in r.items())
        if chan is None:
            ev = ("E", eng, len(self.ops[eng]))
        else:
            c = self.chan_count.get(chan, 0)
            self.chan_count[chan] = c + 1
            ev = ("D", chan, c)
        self.ops[eng].append({"fn": fn, "deps": deps, "ev": ev, "chan": chan, "sig": False, "eng": eng})
        for k in reads:
            r = self.readers.setdefault(k, {})
            kk = (ev[0], ev[1])
            if r.get(kk, -1) < ev[2]:
                r[kk] = ev[2]
        for k in writes:
            self.lastw[k] = ev
            self.readers[k] = {}
        return ev

    def finalize(self):
        for e in self.ENG:
            known = {}
            for op in self.ops[e]:
                need = {}
                for (kind, ident, idx) in op["deps"]:
                    if kind == "E" and ident == e and e in ("pe", "sp"):
                        continue
                    kk = (kind, ident)
                    if known.get(kk, -1) >= idx:
                        continue
                    if need.get(kk, -1) < idx:
                        need[kk] = idx
                for kk, idx in need.items():
                    known[kk] = idx
                    if kk[0] == "E":
                        self.ops[kk[1]][idx]["sig"] = True
                op["need"] = need
        self.cnt = {}
        for e in self.ENG:
            c = 0
            arr = []
            for op in self.ops[e]:
                if op["sig"] and op["chan"] is None:
                    c += 1
                arr.append(c)
            self.cnt[e] = arr

    def emit(self, e, engine, sems, chan_sems):
        for op in self.ops[e]:
            for (kind, ident), idx in op["need"].items():
                if kind == "E":
                    engine.wait_ge(sems[ident], self.cnt[ident][idx])
                else:
                    engine.wait_ge(chan_sems[ident], 16 * (idx + 1))
            ins = op["fn"](engine)
            if op["chan"] is not None:
                ins.then_inc(chan_sems[op["chan"]], 16)
            elif op["sig"]:
                ins.then_inc(sems[e], 1)


class Cfg:
    def __init__(self, n_hist, n_main, ns=1, nslot=5, sample=True):
        self.n_hist = n_hist
        self.n_main = n_main
        self.ns = ns
        self.T = ns * 128
        self.nslot = nslot
        self.sample = sample
        self.rows = (n_hist + n_main) * self.T


def build_program(cfg):
    nc = bass.Bass("TRN2", target_bir_lowering=False)
    S = Sched()
    T = cfg.T
    NS = cfg.ns
    es = contextlib.ExitStack()

    def din(name, shape, dt=F32):
        return nc.dram_tensor(name, list(shape), dt, kind="ExternalInput").ap()

    def dout(name, shape, dt=F32):
        return nc.dram_tensor(name, list(shape), dt, kind="ExternalOutput").ap()

    xrows = din("xrows", [cfg.rows, D])
    xs_in = din("xs", [DEC_SEQ, D])
    s0_in = din("s0", [NH, 128, 128])
    c0_in = din("c0", [CK - 1, CW])
    w_in = din("w_in", [D, NIN])
    w_ph = din("w_proj_h", [HW, D])
    w_pc = din("w_proj_c", [CW, D])
    w_out = din("w_out", [D, D])
    w_ff1 = din("w_ff1", [D, DFF])
    w_ff2 = din("w_ff2", [DFF, D])
    g1_in = din("norm1_g", [1, D])
    g2_in = din("norm2_g", [1, D])
    gf_in = din("final_norm_g", [1, D])
    lbl_in = din("lb_logits", [2, HW])
    hg_in = din("hgrn_norm_g", [128, 1])
    dwk_in = din("dw_kernel", [CK, CW])
    dwb_in = din("dw_bias", [CW, 1])
    lng_in = din("conv_ln_g", [CW, 1])
    lnb_in = din("conv_ln_b", [CW, 1])
    consts_in = din("consts", [128, NCONST])

    y_main = dout("y_main", [cfg.n_main * T, D])
    y_s = dout("y_s", [DEC_SEQ, D])
    sp_out = dout("s_p", [NH, 128, 128])
    cp_out = dout("c_p", [CK - 1, CW])
    ss_out = dout("s_s", [NH, 128, 128])
    cs_out = dout("c_s", [CK - 1, CW])

    dbg = None
    if getattr(cfg, "debug", False):
        dbg = {"x1": dout("dbg_x1", [128, D]), "m": dout("dbg_m", [128, KC * T], BF16),
               "on": dout("dbg_on", [128, NH * T], BF16), "c": dout("dbg_c", [128, 8 * T], BF16),
               "x2": dout("dbg_x2", [128, D])}
    dbg_done = set()

    def dbg_dump(name, src_ap, rkeys):
        if dbg is None or name in dbg_done:
            return
        dbg_done.add(name)
        S.add("sp", lambda e: e.dma_start(out=dbg[name][:, :], in_=src_ap), reads=rkeys,
              writes=[("yout", "dbg" + name)], chan=("dbg", name))

    NP_IN = NIN // 256
    NP_PH = D // 512
    NP_OUT = D // 256
    NP_FF1 = DFF // 256
    NP_FF2 = 4 * (D // 256)
    wb_in = nc.dram_tensor("wb_in", [NP_IN, 128, 4096], BF16).ap()
    wb_ph = nc.dram_tensor("wb_ph", [NP_PH, 128, 4096], BF16).ap()
    wb_pc = nc.dram_tensor("wb_pc", [NP_PH, 128, 4096], BF16).ap()
    wb_out = nc.dram_tensor("wb_out", [NP_OUT, 128, 4096], BF16).ap()
    wb_ff1 = nc.dram_tensor("wb_ff1", [NP_FF1, 128, 4096], BF16).ap()
    wb_ff2 = nc.dram_tensor("wb_ff2", [NP_FF2, 128, 4096], BF16).ap()

    def sb(name, shape, dt=F32):
        return es.enter_context(nc.sbuf_tensor("s_" + name, list(shape), dt))

    consts = sb("consts", [128, NCONST])
    constb = sb("constb", [128, 128], BF16)
    grow = sb("grow", [128, D])
    lbrow = sb("lbrow", [128, HW])
    omlrow = sb("omlrow", [128, HW])
    hgcol = sb("hgcol", [128, 1])
    epscol = sb("epscol", [128, 1])
    dwk = sb("dwk", [128, 8, CK])
    dwb = sb("dwb", [128, 8])
    lng = sb("lng", [128, 8])
    lnb = sb("lnb", [128, 8])
    xin = sb("xin", [128, NS, D])
    hbuf0_ = sb("hbuf0", [128, D], BF16)
    hbufs = [hbuf0_, hbuf0_]
    stat = sb("stat", [128, 16])
    rgs = sb("rgs", [128, NH, 2])
    erg = sb("erg", [128, NH, 2])
    Sst = sb("Sst", [128, NH, 128])
    Sbf = sb("Sbf", [128, NH, 128], BF16)
    histtmp = sb("histtmp", [128, 8, 30])
    ftmp = [sb(f"ftmp{j}", [128, T]) for j in range(2)]

    u = T / 32.0
    ARENA_KB = int(np.ceil(max(3 * u + 36 + u / 2, 5 * u, 4.5 * u + 2)))
    arena = sb("arena", [128, ARENA_KB * 256])
    names = {}

    def regrange(key, lo, hi):
        S.alias[key] = [("pg", p) for p in range(int(lo // 1024), int((hi + 1023) // 1024))]

    def av(name, off_kb, shape, dt=F32):
        esz = 4 if dt == F32 else 2
        nel = int(np.prod(shape[1:]))
        lo = int(round(off_kb * 1024))
        assert lo % 32 == 0 and lo + nel * esz <= ARENA_KB * 1024, (name, lo, nel * esz)
        v = arena[:, lo // 4:(lo + nel * esz + 3) // 4]
        if dt == BF16:
            v = v.bitcast(BF16)
        if len(shape) == 3:
            v = v.rearrange("p (a b) -> p a b", b=shape[2])
        regrange((name,), lo, lo + nel * esz)
        names[id(v)] = name
        v_off[name] = lo
        return v
    v_off = {}

    hT = av("hT", 0, [128, KC, T], BF16)
    h2T = av("h2T", 0, [128, KC, T], BF16)
    qs = av("qs", u, [128, NH, T])
    sog = av("sog", 2 * u, [128, NH, T])
    b0 = 3 * u
    tma = av("tma", b0, [128, HW])
    tml = av("tml", b0 + 4, [128, HW])
    tmk = av("tmk", b0 + 8, [128, HW])
    ktb = av("ktb", b0 + 12, [128, HW], BF16)
    vb = av("vb", b0 + 14, [128, HW], BF16)
    e1 = av("e1", b0 + 16, [128, NH, 128])
    e2 = av("e2", b0 + 20, [128, NH, 128])
    qp = av("qp", b0 + 24, [128, NH, 128], BF16)
    kp = av("kp", b0 + 26, [128, NH, 128], BF16)
    qh = av("qh", b0 + 28, [128, NH, 128], BF16)
    stb = av("stb", b0 + 30, [128, NH, 128], BF16)
    onT = av("onT", b0 + 36, [128, NH, T], BF16)
    for j in range(4):
        regrange(("tma", j), v_off["tma"] + j * 1024, v_off["tma"] + (j + 1) * 1024)
        regrange(("vb", j), v_off["vb"] + j * 512, v_off["vb"] + (j + 1) * 512)
    for h in range(NH):
        S.alias[("qh", h)] = S.alias[("qh",)]
    lbtmp = tma
    dwkT = tml
    cbuf = tmk
    osq = e1
    rstd_o = e2
    uc = av("uc", u, [128, 8, 30 + T])
    dwa = av("dwa", 2 * u + 1, [128, 8, T])
    for c in range(8):
        regrange(("dwa", c), v_off["dwa"] + c * T * 4, v_off["dwa"] + (c + 1) * T * 4)
    sqt = [av("sq0", 3 * u + 1, [128, T])]
    cT = av("cT", 3 * u + 1 + u / 8, [128, 8, T], BF16)
    d0 = 3 * u + 1 + u / 8 + u / 2
    mu = av("mu", d0, [128, T])
    musq = av("musq", d0 + u / 8, [128, T])
    rstd_c = musq
    tmp1 = sqt[0]
    tmp3 = av("tmp3", d0 + 2 * u / 8, [128, T])
    tmp4 = av("tmp4", d0 + 3 * u / 8, [128, T])
    mT = av("mT", d0 + 4 * u / 8, [128, KC, T], BF16)
    assert d0 + 4 * u / 8 + u <= b0 + 36 or NS < 4
    for dch_ in range(KC):
        regrange(("mT", dch_), v_off["mT"] + dch_ * T * 2, v_off["mT"] + (dch_ + 1) * T * 2)
    hfT = av("hfT", u, [128, DFF // 128, T], BF16)
    for fc in range(DFF // 128):
        regrange(("hfT", fc), v_off["hfT"] + fc * T * 2, v_off["hfT"] + (fc + 1) * T * 2)
    wslots = [sb(f"ws{i}", [128, 4096], BF16) for i in range(cfg.nslot)]

    psum = [es.enter_context(nc.psum_tensor(f"ps{i}", [128, 1024], F32)) for i in range(4)]

    class PS:
        n1 = 0
        n2 = 0
        hist = False

    def ps1():
        if PS.hist:
            i = 6 + PS.n1 % 2
        else:
            i = PS.n1 % 8
        PS.n1 += 1
        return psum[i // 2][:, (i % 2) * 512:(i % 2) * 512 + 512], [("ps", i)]

    def ps2():
        i = PS.n2 % (3 if PS.hist else 4)
        PS.n2 += 1
        return psum[i], [("ps", 2 * i), ("ps", 2 * i + 1)]

    class WS:
        n = 0

    def wload(src_ap, src_key, live=()):
        s = WS.n % cfg.nslot
        while ("ws", s) in live:
            WS.n += 1
            s = WS.n % cfg.nslot
        WS.n += 1
        S.add("sp", lambda e, s=s, src_ap=src_ap: e.dma_start(out=wslots[s][:], in_=src_ap),
              reads=[src_key], writes=[("ws", s)], chan=("ws", s))
        return wslots[s], ("ws", s)

    conv_q = []

    def convert(name, wb, pieces, src_fn, chan):
        pieces = list(pieces)
        for n, p in enumerate(pieces):
            conv_q.append((name, wb, p, src_fn(p), chan, pieces if n == len(pieces) - 1 else None))

    def emit_conv(n):
        for _ in range(n):
            if not conv_q:
                return
            name, wb, p, src, chan, fin = conv_q.pop(0)
            ev = S.add("pool", lambda e, p=p, src=src, wb=wb: e.dma_start(
                out=wb[p].rearrange("q (a b) -> q a b", b=src.shape[-1]), in_=src),
                writes=[(name, p)], chan=("cv", chan))
            if fin is not None:
                for pp in fin:
                    S.lastw[(name, pp)] = ev

    w_in_v = w_in.rearrange("(kc p) n -> p kc n", p=128)
    w_ph_v = w_ph.rearrange("(kc p) n -> p kc n", p=128)
    w_pc_v = w_pc.rearrange("(kc p) n -> p kc n", p=128)
    w_out_v = w_out.rearrange("(kc p) n -> p kc n", p=128)
    w_ff1_v = w_ff1.rearrange("(kc p) n -> p kc n", p=128)
    w_ff2_v = w_ff2.rearrange("(fb fc p) n -> p fb fc n", p=128, fc=16)

    def cload(dst, src, eng="sp", slow=False):
        return S.add(eng, lambda e: e.dma_start(out=dst, in_=src, allow_slow_non_contiguous=slow),
                     writes=[], chan=("const",))

    cevs = []
    cevs.append(cload(consts[:], consts_in[:, :]))
    cevs.append(cload(lbrow[:], lbl_in[0:1, :].to_broadcast([128, HW])))
    cevs.append(cload(lbtmp[:], lbl_in[1:2, :].to_broadcast([128, HW])))
    cevs.append(cload(hgcol[:], hg_in[:, :]))
    cevs.append(cload(dwkT[0:CK, :], dwk_in[:, :]))
    cevs.append(cload(dwb[:], dwb_in.rearrange("(c p) o -> p (c o)", p=128), slow=True))
    cevs.append(cload(lng[:], lng_in.rearrange("(c p) o -> p (c o)", p=128), slow=True))
    cevs.append(cload(lnb[:], lnb_in.rearrange("(c p) o -> p (c o)", p=128), slow=True))
    CK_ALL = ("constk",)
    S.lastw[CK_ALL] = cevs[-1]
    RC = [CK_ALL]

    in_src = lambda p: w_in_v[:, :, p * 256:(p + 1) * 256]
    PF_, PV_, PQ_, POG_, PCA_, PCB_, PGH_, PGC_ = (o // 256 for o in (OFF_F, OFF_V, OFF_Q, OFF_OG, OFF_CA, OFF_CB,
                                                                      OFF_GH, OFF_GC))
    convert("wb_in", wb_in, list(range(PCA_, PCA_ + 4)) + list(range(PCB_, PCB_ + 4)), in_src, "in_c")
    convert("wb_in", wb_in, list(range(PQ_, PQ_ + 4)) + list(range(POG_, POG_ + 4)), in_src, "in_q")
    convert("wb_in", wb_in, list(range(PF_, PF_ + 4)) + list(range(PV_, PV_ + 4)), in_src, "in_fv")
    convert("wb_in", wb_in, list(range(PGH_, PGH_ + 8)) + list(range(PGC_, PGC_ + 8)), in_src, "in_g")
    convert("wb_ph", wb_ph, range(NP_PH), lambda p: w_ph_v[:, :, p * 512:(p + 1) * 512], "ph")
    convert("wb_pc", wb_pc, range(NP_PH), lambda p: w_pc_v[:, :, p * 512:(p + 1) * 512], "pc")
    convert("wb_out", wb_out, range(NP_OUT), lambda p: w_out_v[:, :, p * 256:(p + 1) * 256], "out")
    convert("wb_ff1", wb_ff1, range(0, 16), lambda p: w_ff1_v[:, :, p * 256:(p + 1) * 256], "ff1a")
    convert("wb_ff1", wb_ff1, range(16, 32), lambda p: w_ff1_v[:, :, p * 256:(p + 1) * 256], "ff1b")
    ff2_src = lambda p: w_ff2_v[:, p // 8, :, (p % 8) * 256:(p % 8) * 256 + 256]
    convert("wb_ff2", wb_ff2, [fb * 8 + d for d in range(0, 4) for fb in range(4)], ff2_src, "ff2a")
    convert("wb_ff2", wb_ff2, [fb * 8 + d for d in range(4, 8) for fb in range(4)], ff2_src, "ff2b")

    S.add("dve", lambda e: e.tensor_tensor(out=lbtmp[:], in0=lbrow[:], in1=lbtmp[:], op=ALU.subtract),
          reads=RC, writes=[("tma", 0), ("tma", 1), ("tma", 2), ("tma", 3)])
    S.add("act", lambda e: e.activation(out=lbrow[:], in_=lbtmp[:], func=AF.Sigmoid),
          reads=[("tma", 0), ("tma", 1), ("tma", 2), ("tma", 3)], writes=[("lbrow",)])
    S.add("dve", lambda e: e.tensor_scalar(out=omlrow[:], in0=lbrow[:], scalar1=-1.0, scalar2=1.0,
                                           op0=ALU.mult, op1=ALU.add),
          reads=[("lbrow",)], writes=[("omlrow",)])
    S.add("pool", lambda e: e.memset(epscol[:], EPS), writes=[("epscol",)])
    S.add("dve", lambda e: e.tensor_copy(out=constb[:], in_=consts[:, C_ID:C_ID + 128]),
          reads=RC, writes=[("constb",)])
    RC = RC + [("epscol",)]
    RC2 = RC + [("lbrow",), ("omlrow",), ("constb",)]
    for c in range(8):
        pt, pk = ps1()
        S.add("pe", lambda e, c=c, pt=pt: e.matmul(pt[:, 0:CK], lhsT=dwkT[0:CK, c * 128:(c + 1) * 128],
                                                    rhs=consts[0:CK, C_ID:C_ID + CK], start=True, stop=True),
              reads=RC + [("tml",)], writes=pk)
        S.add("dve", lambda e, c=c, pt=pt: e.tensor_copy(out=dwk[:, c, :], in_=pt[:, 0:CK]),
              reads=pk, writes=[("dwk",)])
    RC2 = RC2 + [("dwk",)]

    ident_f = consts[:, C_ID:C_ID + 128]

    def key_of(t):
        return (names[id(t)],)

    class GR:
        cur = None

    def load_g(which):
        if GR.cur == which:
            return
        GR.cur = which
        src = {"g1": g1_in, "g2": g2_in, "gf": gf_in}[which]
        S.add("sp", lambda e: e.dma_start(out=grow[:], in_=src[0:1, :].to_broadcast([128, D])),
              writes=[("grow",)], chan=("grow",))

    def norm_to_hT(i, st, g_row, dstT, tok0):
        xk = ("xin", i)
        par = i % 2
        hbuf = hbufs[par]
        junk = hbuf
        hk = ("hbuf", 0)
        c0_, c1_, c2_ = 3 * par, 3 * par + 1, 3 * par + 2
        S.add("pool", lambda e: e.memset(stat[:, c0_:c0_ + 1], 0.0), writes=[("stat", c0_)])
        S.add("act", lambda e: e.activation(out=junk[:st, :], in_=xin[:st, i, :], func=AF.Square,
                                            accum_out=stat[:st, c0_:c0_ + 1]),
              reads=[xk, ("stat", c0_)], writes=[hk, ("stat", c0_)])
        S.add("act", lambda e: e.activation(out=stat[:st, c1_:c1_ + 1], in_=stat[:st, c0_:c0_ + 1], func=AF.Ln,
                                            scale=1.0 / D, bias=epscol[:st, 0:1]),
              reads=[("stat", c0_)] + RC, writes=[("stat", c1_)])
        S.add("act", lambda e: e.activation(out=stat[:st, c2_:c2_ + 1], in_=stat[:st, c1_:c1_ + 1], func=AF.Exp,
                                            scale=-0.5),
              reads=[("stat", c1_)], writes=[("stat", c2_)])
        S.add("dve", lambda e: e.scalar_tensor_tensor(out=hbuf[:st, :], in0=xin[:st, i, :], scalar=stat[:st, c2_:c2_ + 1],
                                                      in1=grow[:st, :], op0=ALU.mult, op1=ALU.mult),
              reads=[xk, ("stat", c2_), ("grow",)] + RC, writes=[hk])
        for half in range(2):
            pt, pk = ps1()
            ptb = pt.bitcast(BF16)
            for j in range(8):
                kc = half * 8 + j
                S.add("pe", lambda e, kc=kc, j=j, ptb=ptb: e.transpose(
                    ptb[:, j * 128:j * 128 + st], hbuf[:st, kc * 128:(kc + 1) * 128], constb[:st, :st]),
                    reads=[hk] + RC2, writes=pk)
            eng = "act" if half == 0 else "dve"
            src = ptb.rearrange("p (j t) -> p j t", t=128)[:, :, 0:st]
            dst = dstT[:, half * 8:(half + 1) * 8, tok0:tok0 + st]
            if eng == "act":
                S.add("act", lambda e, src=src, dst=dst: e.activation(out=dst, in_=src, func=AF.Copy),
                      reads=pk, writes=[key_of(dstT)])
            else:
                S.add("dve", lambda e, src=src, dst=dst: e.tensor_copy(out=dst, in_=src),
                      reads=pk, writes=[key_of(dstT)])

    def proj_fm(piece_src, piece_key, col_chunks, rhsT, ntok, consume):
        ws, wk = wload(piece_src, piece_key)
        wv = ws[:].rearrange("p (k c) -> p k c", c=256)
        for ch in range(col_chunks):
            pt, pk = ps1()
            for kc in range(KC):
                S.add("pe", lambda e, kc=kc, ch=ch, pt=pt, wv=wv: e.matmul(
                    pt[:, 0:ntok], lhsT=wv[:, kc, ch * 128:(ch + 1) * 128], rhs=rhsT[:, kc, 0:ntok],
                    start=(kc == 0), stop=(kc == KC - 1)),
                    reads=[wk, key_of(rhsT)], writes=pk)
            consume(ch, pt, pk)

    def hgrn_tm(i, st, tok0, piece0_f, piece0_v):
        for p4 in range(4):
            ws, wk = wload(wb_in[piece0_f + p4], ("wb_in", piece0_f + p4))
            wv = ws[:].rearrange("p (k c) -> p k c", c=256)
            pt, pk = ps1()
            for kc in range(KC):
                S.add("pe", lambda e, kc=kc, pt=pt, wv=wv: e.matmul(
                    pt[:st, 0:256], lhsT=hT[:, kc, tok0:tok0 + st], rhs=wv[:, kc, :],
                    start=(kc == 0), stop=(kc == KC - 1)),
                    reads=[wk, key_of(hT)], writes=pk)
            S.add("act", lambda e, pt=pt, p4=p4: e.activation(out=tma[:st, p4 * 256:(p4 + 1) * 256],
                                                               in_=pt[:st, 0:256], func=AF.Sigmoid),
                  reads=pk, writes=[("tma", p4)])
        for p4 in range(4):
            ws, wk = wload(wb_in[piece0_v + p4], ("wb_in", piece0_v + p4))
            wv = ws[:].rearrange("p (k c) -> p k c", c=256)
            pt, pk = ps1()
            for kc in range(KC):
                S.add("pe", lambda e, kc=kc, pt=pt, wv=wv: e.matmul(
                    pt[:st, 0:256], lhsT=hT[:, kc, tok0:tok0 + st], rhs=wv[:, kc, :],
                    start=(kc == 0), stop=(kc == KC - 1)),
                    reads=[wk, key_of(hT)], writes=pk)
            S.add("dve", lambda e, pt=pt, p4=p4: e.tensor_copy(out=vb[:st, p4 * 256:(p4 + 1) * 256],
                                                                in_=pt[:st, 0:256]),
                  reads=pk, writes=[("vb", p4)])
        TMA = [("tma", j) for j in range(4)]
        S.add("dve", lambda e: e.tensor_tensor(out=tma[:st, :], in0=tma[:st, :], in1=omlrow[:st, :], op=ALU.mult),
              reads=TMA + RC2, writes=TMA)
        S.add("pool", lambda e: e.tensor_tensor(out=tma[:st, :], in0=tma[:st, :], in1=lbrow[:st, :], op=ALU.add),
              reads=TMA + RC2, writes=TMA)
        S.add("act", lambda e: e.activation(out=tml[:st, :], in_=tma[:st, :], func=AF.Ln),
              reads=TMA, writes=[("tml",)])
        S.add("dve", lambda e: e.tensor_scalar(out=tmk[:st, :], in0=tma[:st, :], scalar1=-1.0, scalar2=1.0,
                                               op0=ALU.mult, op1=ALU.add),
              reads=TMA, writes=[("tmk",)])
        pp, ppk = ps2()
        for half in range(2):
            S.add("pe", lambda e, half=half, pp=pp: e.matmul(
                pp[:st, half * 512:(half + 1) * 512], lhsT=consts[:st, C_SU:C_SU + st],
                rhs=tml[:st, half * 512:(half + 1) * 512], start=True, stop=True),
                reads=[("tml",)] + RC, writes=ppk)
        S.add("act", lambda e, pp=pp: e.activation(out=tma[:st, :], in_=pp[:st, :], func=AF.Exp),
              reads=ppk + TMA, writes=TMA)
        S.add("dve", lambda e: e.tensor_tensor(out=ktb[:st, :], in0=tmk[:st, :], in1=tma[:st, :], op=ALU.mult),
              reads=TMA + [("tmk",)], writes=[("ktb",)])

    def hgrn_state_update(st):
        pp, ppk = ps2()
        for h in range(NH):
            S.add("pe", lambda e, h=h, pp=pp: e.matmul(
                pp[:, h * 128:(h + 1) * 128], lhsT=ktb[:st, h * 128:(h + 1) * 128],
                rhs=vb[:st, h * 128:(h + 1) * 128], start=True, stop=True),
                reads=[("ktb",)] + [("vb", j) for j in range(4)], writes=ppk)
        ppv = pp[:].rearrange("p (h v) -> p h v", v=128)
        for h in range(NH):
            S.add("dve", lambda e, h=h, ppv=ppv: e.scalar_tensor_tensor(
                out=Sst[:, h, :], in0=Sst[:, h, :], scalar=erg[:, h, 1:2], in1=ppv[:, h, :],
                op0=ALU.mult, op1=ALU.add),
                reads=ppk + [("erg",), ("Sst",)], writes=[("Sst",)])
        S.add("pool", lambda e: e.tensor_copy(out=Sbf[:], in_=Sst[:]), reads=[("Sst",)], writes=[("Sbf",)])

    def hgrn_decay_cols(st):
        crg = C_RG128 if st == 128 else C_RG64
        pt, pk = ps1()
        for h in range(NH):
            S.add("pe", lambda e, h=h, pt=pt: e.matmul(
                pt[:, h * 2:h * 2 + 2], lhsT=tml[:st, h * 128:(h + 1) * 128], rhs=consts[:st, crg:crg + 2],
                start=True, stop=True),
                reads=[("tml",)] + RC, writes=pk)
        S.add("act", lambda e, pt=pt: e.activation(out=erg[:].rearrange("p h c -> p (h c)"), in_=pt[:, 0:16],
                                                    func=AF.Exp),
              reads=pk, writes=[("erg",)])

    def hgrn_fm(i, st, tok0):
        cm = C_M128 if st == 128 else C_M64
        pb, pbk = ps2()
        pkT, pkTk = ps2()
        for h in range(NH):
            S.add("pe", lambda e, h=h, pb=pb: e.matmul(
                pb[:, h * 128:h * 128 + st], lhsT=tml[:st, h * 128:(h + 1) * 128], rhs=consts[:st, cm:cm + st],
                start=True, stop=True),
                reads=[("tml",)] + RC, writes=pbk)
        for h in range(NH):
            S.add("pe", lambda e, h=h, pkT=pkT: e.matmul(
                pkT[:, h * 128:h * 128 + st], lhsT=tmk[:st, h * 128:(h + 1) * 128], rhs=consts[:st, C_ID:C_ID + st],
                start=True, stop=True),
                reads=[("tmk",)] + RC, writes=pkTk)
        pbv = pb[:].rearrange("p (h t) -> p h t", t=128)[:, :, 0:st]
        pkTv = pkT[:].rearrange("p (h t) -> p h t", t=128)[:, :, 0:st]
        S.add("act", lambda e: e.activation(out=e1[:, :, 0:st], in_=pbv, func=AF.Exp),
              reads=pbk, writes=[("e1",)])
        S.add("act", lambda e: e.activation(out=e2[:, :, 0:st], in_=pbv, func=AF.Exp, scale=-1.0),
              reads=pbk, writes=[("e2",)])
        S.add("dve", lambda e: e.tensor_tensor(out=qp[:, :, 0:st], in0=qs[:, :, tok0:tok0 + st], in1=e1[:, :, 0:st],
                                               op=ALU.mult),
              reads=[("qs",), ("e1",)], writes=[("qp",)])
        S.add("dve", lambda e: e.tensor_tensor(out=kp[:, :, 0:st], in0=pkTv, in1=e2[:, :, 0:st], op=ALU.mult),
              reads=pkTk + [("e2",)], writes=[("kp",)])
        for h in range(NH):
            eng = "dve"
            S.add(eng, lambda e, h=h: e.scalar_tensor_tensor(
                out=qh[:, h, 0:st], in0=qs[:, h, tok0:tok0 + st], scalar=erg[:, h, 0:1], in1=e1[:, h, 0:st],
                op0=ALU.mult, op1=ALU.mult),
                reads=[("qs",), ("e1",), ("erg",)], writes=[("qh", h)])
        psc, psck = ps2()
        for h in range(NH):
            S.add("pe", lambda e, h=h, psc=psc: e.matmul(
                psc[:st, h * 128:h * 128 + st], lhsT=kp[:, h, 0:st], rhs=qp[:, h, 0:st], start=True, stop=True),
                reads=[("kp",), ("qp",)], writes=psck)
        pscv = psc[:].rearrange("p (h t) -> p h t", t=128)[:st, :, 0:st]
        maskv = consts[:, C_MASK:C_MASK + 1024].rearrange("p (h t) -> p h t", t=128)[:st, :, 0:st]
        S.add("dve", lambda e: e.tensor_tensor(out=stb[:st, :, 0:st], in0=pscv, in1=maskv, op=ALU.mult),
              reads=psck + RC, writes=[("stb",)])
        po, pok = ps2()
        for h in range(NH):
            S.add("pe", lambda e, h=h, po=po: e.matmul(
                po[:, h * 128:h * 128 + st], lhsT=vb[:st, h * 128:(h + 1) * 128], rhs=stb[:st, h, 0:st],
                start=True, stop=False),
                reads=[("stb",)] + [("vb", j) for j in range(4)], writes=pok)
            S.add("pe", lambda e, h=h, po=po: e.matmul(
                po[:, h * 128:h * 128 + st], lhsT=Sbf[:, h, :], rhs=qh[:, h, 0:st], start=False, stop=True),
                reads=[("Sbf",), ("qh", h)], writes=pok)
        pov = po[:].rearrange("p (h t) -> p h t", t=128)[:, :, 0:st]
        S.add("act", lambda e: e.activation(out=osq[:, :, 0:st], in_=pov, func=AF.Square),
              reads=pok, writes=[("e1",)])
        pm, pmk = ps2()
        for h2 in range(2):
            for hh in range(4):
                h = h2 * 4 + hh
                S.add("pe", lambda e, h=h, pm=pm: e.matmul(
                    pm[:, h * 128:h * 128 + st], lhsT=consts[:, C_O128:C_O128 + 128], rhs=osq[:, h, 0:st],
                    start=True, stop=True),
                    reads=[("e1",)] + RC, writes=pmk)
        pmv = pm[:].rearrange("p (h t) -> p h t", t=128)[:, :, 0:st]
        S.add("act", lambda e: e.activation(out=rstd_o[:, :, 0:st], in_=pmv, func=AF.Ln, bias=epscol[:, 0:1]),
              reads=pmk + RC, writes=[("e2",)])
        S.add("act", lambda e: e.activation(out=rstd_o[:, :, 0:st], in_=rstd_o[:, :, 0:st], func=AF.Exp, scale=-0.5),
              reads=[("e2",)], writes=[("e2",)])
        S.add("dve", lambda e: e.tensor_tensor(out=osq[:, :, 0:st], in0=pov, in1=rstd_o[:, :, 0:st], op=ALU.mult),
              reads=pok + [("e2",), ("e1",)], writes=[("e1",)])
        S.add("dve", lambda e: e.scalar_tensor_tensor(
            out=onT[:, :, tok0:tok0 + st], in0=osq[:, :, 0:st], scalar=hgcol[:, 0:1], in1=sog[:, :, tok0:tok0 + st],
            op0=ALU.mult, op1=ALU.mult),
            reads=[("e1",), ("sog",)] + RC, writes=[("onT",)])

    def load_x(i, st, src_rows):
        S.add("sp", lambda e: e.dma_start(out=xin[:st, i, :], in_=src_rows), writes=[("xin", i)], chan=("x", i))

    S.add("pool", lambda e: e.memset(Sst[:], 0.0), writes=[("Sst",)])
    S.add("pool", lambda e: e.memset(Sbf[:], 0.0), writes=[("Sbf",)])
    S.add("pool", lambda e: e.memset(histtmp[:], 0.0), writes=[("histtmp",)])

    P_F = OFF_F // 256
    P_V = OFF_V // 256
    P_Q = OFF_Q // 256
    P_OG = OFF_OG // 256
    P_CA = OFF_CA // 256
    P_CB = OFF_CB // 256
    P_GH = OFF_GH // 256
    P_GC = OFF_GC // 256

    def glu_chunks(ntok, c_list=range(8), hTb=None):
        hTb = hT if hTb is None else hTb
        S.add("pool", lambda e: e.tensor_copy(out=uc[:, :, 0:30], in_=histtmp[:]),
              reads=[("histtmp",)], writes=[("uc",)])
        for cp in range(4):
            if not any((2 * cp + ch) in c_list for ch in range(2)):
                continue
            wsa, wka = wload(wb_in[P_CA + cp], ("wb_in", P_CA + cp))
            wsb, wkb = wload(wb_in[P_CB + cp], ("wb_in", P_CB + cp))
            wva = wsa[:].rearrange("p (k c) -> p k c", c=256)
            wvb = wsb[:].rearrange("p (k c) -> p k c", c=256)
            for ch in range(2):
                c = 2 * cp + ch
                pa, pak = ps1()
                pbb, pbk = ps1()
                for kc in range(KC):
                    S.add("pe", lambda e, kc=kc, ch=ch, pa=pa, wva=wva: e.matmul(
                        pa[:, 0:ntok], lhsT=wva[:, kc, ch * 128:(ch + 1) * 128], rhs=hTb[:, kc, 0:ntok],
                        start=(kc == 0), stop=(kc == KC - 1)), reads=[wka, key_of(hTb)], writes=pak)
                for kc in range(KC):
                    S.add("pe", lambda e, kc=kc, ch=ch, pbb=pbb, wvb=wvb: e.matmul(
                        pbb[:, 0:ntok], lhsT=wvb[:, kc, ch * 128:(ch + 1) * 128], rhs=hTb[:, kc, 0:ntok],
                        start=(kc == 0), stop=(kc == KC - 1)), reads=[wkb, key_of(hTb)], writes=pbk)
                S.add("act", lambda e, pbb=pbb: e.activation(out=tmp1[:, 0:ntok], in_=pbb[:, 0:ntok], func=AF.Sigmoid),
                      reads=pbk, writes=[("sq0",)])
                S.add("dve", lambda e, c=c, pa=pa: e.tensor_tensor(out=uc[:, c, 30:30 + ntok], in0=pa[:, 0:ntok],
                                                                   in1=tmp1[:, 0:ntok], op=ALU.mult),
                      reads=pak + [("sq0",)], writes=[("uc",)])

    def shift_hist(ntok):
        S.add("pool", lambda e: e.tensor_copy(out=histtmp[:], in_=uc[:, :, ntok:ntok + 30]),
              reads=[("uc",)], writes=[("histtmp",)])

    if cfg.n_hist > 0:
        assert ARENA_KB >= 88 or NS < 4
        hTs = [hT, av("hT_B", 2 * u, [128, KC, T], BF16)]
        tmB0 = u
        sets = [
            dict(tma=tma, tml=tml, tmk=tmk, ktb=ktb, vb=vb, erg=erg, k="A"),
            dict(tma=av("tmaB", tmB0, [128, HW]), tml=av("tmlB", tmB0 + 4, [128, HW]),
                 tmk=av("tmkB", tmB0 + 8, [128, HW]), ktb=av("ktbB", tmB0 + 12, [128, HW], BF16),
                 vb=av("vbB", tmB0 + 14, [128, HW], BF16), erg=sb("ergB", [128, NH, 2]), k="B"),
        ]
        skeys = [
            dict(tma=[("tma", j) for j in range(4)], tml=[("tml",)], tmk=[("tmk",)], ktb=[("ktb",)],
                 vb=[("vb", j) for j in range(4)], erg=[("erg",)]),
            dict(tma=[("tmaB",)], tml=[("tmlB",)], tmk=[("tmkB",)], ktb=[("ktbB",)], vb=[("vbB",)],
                 erg=[("ergB",)]),
        ]
        res_w = []
        res_k = []
        hw_ev = None
        for n in range(8):
            if n < cfg.nslot:
                dst = wslots[n][:].rearrange("p (k c) -> p k c", c=256)
                kk = ("ws", n)
            else:
                dst = av(f"resw{n}", b0 + 16 + (n - cfg.nslot) * 8, [128, KC, 256], BF16)
                kk = (f"resw{n}",)
            p = (PF_ + n) if n < 4 else (PV_ + n - 4)
            hw_ev = S.add("pool", lambda e, dst=dst, p=p: e.dma_start(out=dst, in_=w_in_v[:, :, p * 256:(p + 1) * 256]),
                          writes=[kk], chan=("hw",))
            res_w.append(dst)
            res_k.append(kk)
        for kk in res_k:
            for pk_ in S._exp([kk]):
                S.lastw[pk_] = hw_ev

        def h_load(t):
            for i in range(NS):
                r0 = t * T + i * 128
                load_x(i, 128, xrows[r0:r0 + 128, :])

        def h_norm(t, hTbuf):
            load_g("g1")
            for i in range(NS):
                norm_to_hT(i, 128, None, hTbuf, i * 128)
            if t + 1 < cfg.n_hist:
                h_load(t + 1)

        def h_p1(g):
            t, i = divmod(g, NS)
            B, K_ = sets[g % 2], skeys[g % 2]
            hTb = hTs[t % 2]
            pf, pfk = ps2()
            for p4 in range(4):
                for kc in range(KC):
                    S.add("pe", lambda e, kc=kc, p4=p4, pf=pf, hTb=hTb, i=i: e.matmul(
                        pf[:, p4 * 256:(p4 + 1) * 256], lhsT=hTb[:, kc, i * 128:(i + 1) * 128], rhs=res_w[p4][:, kc, :],
                        start=(kc == 0), stop=(kc == KC - 1)),
                        reads=[res_k[p4], key_of(hTb)], writes=pfk)
            S.add("act", lambda e, pf=pf, B=B: e.activation(out=B["tma"][:, :], in_=pf[:, :], func=AF.Sigmoid),
                  reads=pfk, writes=K_["tma"])
            S.add("dve", lambda e, B=B: e.tensor_tensor(out=B["tma"][:, :], in0=B["tma"][:, :], in1=omlrow[:, :],
                                                        op=ALU.mult), reads=K_["tma"] + RC2, writes=K_["tma"])
            S.add("pool", lambda e, B=B: e.tensor_tensor(out=B["tma"][:, :], in0=B["tma"][:, :], in1=lbrow[:, :],
                                                         op=ALU.add), reads=K_["tma"] + RC2, writes=K_["tma"])
            S.add("act", lambda e, B=B: e.activation(out=B["tml"][:, :], in_=B["tma"][:, :], func=AF.Ln),
                  reads=K_["tma"], writes=K_["tml"])
            S.add("dve", lambda e, B=B: e.tensor_scalar(out=B["tmk"][:, :], in0=B["tma"][:, :], scalar1=-1.0,
                                                        scalar2=1.0, op0=ALU.mult, op1=ALU.add),
                  reads=K_["tma"], writes=K_["tmk"])

        def h_p2(g):
            t, i = divmod(g, NS)
            B, K_ = sets[g % 2], skeys[g % 2]
            hTb = hTs[t % 2]
            pv, pvk = ps2()
            for p4 in range(4):
                for kc in range(KC):
                    S.add("pe", lambda e, kc=kc, p4=p4, pv=pv, hTb=hTb, i=i: e.matmul(
                        pv[:, p4 * 256:(p4 + 1) * 256], lhsT=hTb[:, kc, i * 128:(i + 1) * 128],
                        rhs=res_w[4 + p4][:, kc, :], start=(kc == 0), stop=(kc == KC - 1)),
                        reads=[res_k[4 + p4], key_of(hTb)], writes=pvk)
            S.add("dve", lambda e, pv=pv, B=B: e.tensor_copy(out=B["vb"][:, :], in_=pv[:, :]),
                  reads=pvk, writes=K_["vb"])

        def h_c1(g):
            B, K_ = sets[g % 2], skeys[g % 2]
            pg, pgk = ps2()
            for half in range(2):
                S.add("pe", lambda e, half=half, pg=pg, B=B: e.matmul(
                    pg[:, half * 512:(half + 1) * 512], lhsT=consts[:, C_SU:C_SU + 128],
                    rhs=B["tml"][:, half * 512:(half + 1) * 512], start=True, stop=True),
                    reads=K_["tml"] + RC, writes=pgk)
            pr, prk = ps1()
            for h in range(NH):
                S.add("pe", lambda e, h=h, pr=pr, B=B: e.matmul(
                    pr[:, h * 2:h * 2 + 2], lhsT=B["tml"][:, h * 128:(h + 1) * 128],
                    rhs=consts[:, C_RG128:C_RG128 + 2], start=True, stop=True),
                    reads=K_["tml"] + RC, writes=prk)
            S.add("act", lambda e, pg=pg, B=B: e.activation(out=B["tma"][:, :], in_=pg[:, :], func=AF.Exp),
                  reads=pgk + K_["tma"], writes=K_["tma"])
            S.add("act", lambda e, pr=pr, B=B: e.activation(out=B["erg"][:].rearrange("p h c -> p (h c)"),
                                                            in_=pr[:, 0:16], func=AF.Exp),
                  reads=prk, writes=K_["erg"])
            S.add("dve", lambda e, B=B: e.tensor_tensor(out=B["ktb"][:, :], in0=B["tmk"][:, :], in1=B["tma"][:, :],
                                                        op=ALU.mult),
                  reads=K_["tma"] + K_["tmk"], writes=K_["ktb"])

        def h_c2(g):
            B, K_ = sets[g % 2], skeys[g % 2]
            pS, pSk = ps2()
            for h in range(NH):
                S.add("pe", lambda e, h=h, pS=pS, B=B: e.matmul(
                    pS[:, h * 128:(h + 1) * 128], lhsT=B["ktb"][:, h * 128:(h + 1) * 128],
                    rhs=B["vb"][:, h * 128:(h + 1) * 128], start=True, stop=True),
                    reads=K_["ktb"] + K_["vb"], writes=pSk)
            pSv = pS[:].rearrange("p (h v) -> p h v", v=128)
            for h in range(NH):
                S.add("dve", lambda e, h=h, pSv=pSv, B=B: e.scalar_tensor_tensor(
                    out=Sst[:, h, :], in0=Sst[:, h, :], scalar=B["erg"][:, h, 1:2], in1=pSv[:, h, :],
                    op0=ALU.mult, op1=ALU.add),
                    reads=pSk + K_["erg"] + [("Sst",)], writes=[("Sst",)])

        NG = cfg.n_hist * NS
        per_tile = (len(conv_q) + cfg.n_hist - 1) // max(cfg.n_hist, 1) + 1
        PS.hist = True
        h_load(0)
        h_norm(0, hTs[0])
        h_p1(0)
        h_p2(0)
        for g in range(NG):
            t, i = divmod(g, NS)
            if i == 2:
                emit_conv(per_tile)
            if i == 1 and t + 1 < cfg.n_hist:
                h_norm(t + 1, hTs[(t + 1) % 2])
            if g + 1 < NG:
                h_p1(g + 1)
            h_c1(g)
            if g + 1 < NG:
                h_p2(g + 1)
            h_c2(g)
        emit_conv(len(conv_q))
        PS.hist = False
        S.add("pool", lambda e: e.tensor_copy(out=Sbf[:], in_=Sst[:]), reads=[("Sst",)], writes=[("Sbf",)])
        glu_chunks(T, hTb=hTs[(cfg.n_hist - 1) % 2])
        shift_hist(T)
    else:
        emit_conv(len(conv_q))

    def full_tile(ntok, nsub, st, x_src_fn, y_dst_fn, emit_conv_out=None, emit_state_out=None):
        for i in range(nsub):
            load_x(i, st, x_src_fn(i))
            load_g("g1")
            norm_to_hT(i, st, None, hT, i * 128)
        for p in range(4):
            def cons_q(ch, pt, pk, p=p):
                h = 2 * p + ch
                S.add("act", lambda e: e.activation(out=qs[:, h, 0:ntok], in_=pt[:, 0:ntok], func=AF.Silu),
                      reads=pk, writes=[("qs",)])
            proj_fm(wb_in[P_Q + p], ("wb_in", P_Q + p), 2, hT, ntok, cons_q)
        for p in range(4):
            def cons_og(ch, pt, pk, p=p):
                h = 2 * p + ch
                S.add("act", lambda e: e.activation(out=sog[:, h, 0:ntok], in_=pt[:, 0:ntok], func=AF.Silu),
                      reads=pk, writes=[("sog",)])
            proj_fm(wb_in[P_OG + p], ("wb_in", P_OG + p), 2, hT, ntok, cons_og)
        for i in range(nsub):
            hgrn_tm(i, st, i * 128, P_F, P_V)
            hgrn_decay_cols(st)
            hgrn_fm(i, st, i * 128)
            hgrn_state_update(st)
        if emit_state_out is not None:
            emit_state_out()
        glu_chunks(ntok)
        def part1_chunks():
            for dp in range(4):
                wsh, wkh = wload(wb_ph[dp], ("wb_ph", dp))
                wvh = wsh[:].rearrange("p (k c) -> p k c", c=512)
                for half in range(2):
                    wsg, wkg = wload(wb_in[P_GH + dp * 2 + half], ("wb_in", P_GH + dp * 2 + half), live=(wkh,))
                    wvg = wsg[:].rearrange("p (k c) -> p k c", c=256)
                    for ch in range(2):
                        dch = dp * 4 + half * 2 + ch
                        c0 = (half * 2 + ch) * 128
                        pbh, pbhk = ps1()
                        pgh, pghk = ps1()
                        for c in range(8):
                            S.add("pe", lambda e, c=c, c0=c0, pbh=pbh, wvh=wvh: e.matmul(
                                pbh[:, 0:ntok], lhsT=wvh[:, c, c0:c0 + 128], rhs=onT[:, c, 0:ntok],
                                start=(c == 0), stop=(c == 7)), reads=[wkh, ("onT",)], writes=pbhk)
                        for kc in range(KC):
                            S.add("pe", lambda e, kc=kc, ch=ch, pgh=pgh, wvg=wvg: e.matmul(
                                pgh[:, 0:ntok], lhsT=wvg[:, kc, ch * 128:(ch + 1) * 128], rhs=hT[:, kc, 0:ntok],
                                start=(kc == 0), stop=(kc == KC - 1)), reads=[wkg, key_of(hT)], writes=pghk)
                        tg, tgk = (tmp3, ("tmp3",)) if dch % 2 == 0 else (tmp4, ("tmp4",))
                        S.add("act", lambda e, pgh=pgh, tg=tg: e.activation(out=tg[:, 0:ntok], in_=pgh[:, 0:ntok],
                                                                            func=AF.Sigmoid), reads=pghk, writes=[tgk])
                        yield (dch, pbh, pbhk, tg, tgk)

        def part1_evac(item):
            dch, pbh, pbhk, tg, tgk = item
            S.add("dve", lambda e: e.tensor_tensor(out=mT[:, dch, 0:ntok], in0=pbh[:, 0:ntok], in1=tg[:, 0:ntok],
                                                   op=ALU.mult),
                  reads=pbhk + [tgk], writes=[("mT", dch)])

        p1 = part1_chunks()
        pending = []
        ntap = 0
        for j in range(CK):
            for c in range(8):
                if j == 0:
                    S.add("dve", lambda e, c=c: e.tensor_scalar(out=dwa[:, c, 0:ntok], in0=uc[:, c, 0:ntok],
                                                                scalar1=dwk[:, c, 0:1], scalar2=dwb[:, c:c + 1],
                                                                op0=ALU.mult, op1=ALU.add),
                          reads=[("uc",)] + RC2, writes=[("dwa", c)])
                else:
                    S.add("dve", lambda e, c=c, j=j: e.scalar_tensor_tensor(
                        out=dwa[:, c, 0:ntok], in0=uc[:, c, j:j + ntok], scalar=dwk[:, c, j:j + 1],
                        in1=dwa[:, c, 0:ntok], op0=ALU.mult, op1=ALU.add),
                        reads=[("uc",), ("dwa", c)] + RC2, writes=[("dwa", c)])
                ntap += 1
                if ntap % 15 == 0:
                    while len(pending) < 2:
                        it = next(p1, None)
                        if it is None:
                            break
                        pending.append(it)
                    if pending and ntap >= 30:
                        part1_evac(pending.pop(0))
        for it in p1:
            pending.append(it)
            if len(pending) >= 2:
                part1_evac(pending.pop(0))
        for it in pending:
            part1_evac(it)
        if emit_conv_out is not None:
            emit_conv_out(ntok)
        shift_hist(ntok)
        DWA = [("dwa", c) for c in range(8)]
        pmu, pmuk = ps1()
        pm2, pm2k = ps1()
        for c in range(8):
            S.add("pe", lambda e, c=c: e.matmul(pmu[:, 0:ntok], lhsT=consts[:, C_OC:C_OC + 128], rhs=dwa[:, c, 0:ntok],
                                                start=(c == 0), stop=(c == 7)), reads=[("dwa", c)] + RC, writes=pmuk)
        for c in range(8):
            sq = sqt[0]
            sqk = ("sq0",)
            S.add("act", lambda e, c=c, sq=sq: e.activation(out=sq[:, 0:ntok], in_=dwa[:, c, 0:ntok], func=AF.Square),
                  reads=[("dwa", c)], writes=[sqk])
            S.add("pe", lambda e, c=c, sq=sq: e.matmul(pm2[:, 0:ntok], lhsT=consts[:, C_OC:C_OC + 128],
                                                       rhs=sq[:, 0:ntok], start=(c == 0), stop=(c == 7)),
                  reads=[sqk] + RC, writes=pm2k)
        S.add("act", lambda e: e.activation(out=mu[:, 0:ntok], in_=pmu[:, 0:ntok], func=AF.Copy),
              reads=pmuk, writes=[("mu",)])
        S.add("dve", lambda e: e.tensor_tensor(out=musq[:, 0:ntok], in0=mu[:, 0:ntok], in1=mu[:, 0:ntok], op=ALU.mult),
              reads=[("mu",)], writes=[("musq",)])
        S.add("dve", lambda e: e.tensor_tensor(out=musq[:, 0:ntok], in0=pm2[:, 0:ntok], in1=musq[:, 0:ntok],
                                               op=ALU.subtract),
              reads=pm2k + [("musq",)], writes=[("musq",)])
        S.add("act", lambda e: e.activation(out=rstd_c[:, 0:ntok], in_=musq[:, 0:ntok], func=AF.Ln,
                                            bias=epscol[:, 0:1]),
              reads=[("musq",)] + RC, writes=[("musq",)])
        S.add("act", lambda e: e.activation(out=rstd_c[:, 0:ntok], in_=rstd_c[:, 0:ntok], func=AF.Exp, scale=-0.5),
              reads=[("musq",)], writes=[("musq",)])
        for c in range(8):
            eng = "pool" if c % 2 else "dve"
            S.add(eng, lambda e, c=c: e.tensor_tensor(out=dwa[:, c, 0:ntok], in0=dwa[:, c, 0:ntok], in1=mu[:, 0:ntok],
                                                      op=ALU.subtract),
                  reads=[("dwa", c), ("mu",)], writes=[("dwa", c)])
            S.add(eng, lambda e, c=c: e.tensor_tensor(out=dwa[:, c, 0:ntok], in0=dwa[:, c, 0:ntok],
                                                      in1=rstd_c[:, 0:ntok], op=ALU.mult),
                  reads=[("dwa", c), ("musq",)], writes=[("dwa", c)])
            S.add("act", lambda e, c=c: e.activation(out=cT[:, c, 0:ntok], in_=dwa[:, c, 0:ntok], func=AF.Silu,
                                                     scale=lng[:, c:c + 1], bias=lnb[:, c:c + 1]),
                  reads=[("dwa", c)] + RC, writes=[("cT",)])
        for dp in range(4):
            wsc, wkc = wload(wb_pc[dp], ("wb_pc", dp))
            wvc = wsc[:].rearrange("p (k c) -> p k c", c=512)
            for half in range(2):
                wsq, wkq = wload(wb_in[P_GC + dp * 2 + half], ("wb_in", P_GC + dp * 2 + half), live=(wkc,))
                wvq = wsq[:].rearrange("p (k c) -> p k c", c=256)
                for ch in range(2):
                    dch = dp * 4 + half * 2 + ch
                    c0 = (half * 2 + ch) * 128
                    pbc, pbck = ps1()
                    pgc, pgck = ps1()
                    for c in range(8):
                        S.add("pe", lambda e, c=c, c0=c0, pbc=pbc, wvc=wvc: e.matmul(
                            pbc[:, 0:ntok], lhsT=wvc[:, c, c0:c0 + 128], rhs=cT[:, c, 0:ntok],
                            start=(c == 0), stop=(c == 7)), reads=[wkc, ("cT",)], writes=pbck)
                    for kc in range(KC):
                        S.add("pe", lambda e, kc=kc, ch=ch, pgc=pgc, wvq=wvq: e.matmul(
                            pgc[:, 0:ntok], lhsT=wvq[:, kc, ch * 128:(ch + 1) * 128], rhs=hT[:, kc, 0:ntok],
                            start=(kc == 0), stop=(kc == KC - 1)), reads=[wkq, key_of(hT)], writes=pgck)
                    tg, tgk = (tmp3, ("tmp3",)) if dch % 2 == 0 else (tmp4, ("tmp4",))
                    S.add("act", lambda e, pgc=pgc, tg=tg: e.activation(out=tg[:, 0:ntok], in_=pgc[:, 0:ntok],
                                                                        func=AF.Sigmoid), reads=pgck, writes=[tgk])
                    S.add("dve", lambda e, pbc=pbc, tg=tg: e.tensor_tensor(out=tg[:, 0:ntok], in0=pbc[:, 0:ntok],
                                                                           in1=tg[:, 0:ntok], op=ALU.mult),
                          reads=pbck + [tgk], writes=[tgk])
                    S.add("dve", lambda e, dch=dch, tg=tg: e.tensor_tensor(out=mT[:, dch, 0:ntok],
                                                                           in0=mT[:, dch, 0:ntok], in1=tg[:, 0:ntok],
                                                                           op=ALU.add),
                          reads=[tgk, ("mT", dch)], writes=[("mT", dch)])
        for dpc in range(8):
            ws, wk = wload(wb_out[dpc], ("wb_out", dpc))
            wv = ws[:].rearrange("p (k c) -> p k c", c=256)
            for i in range(nsub):
                pt, pk = ps1()
                for kc in range(KC):
                    S.add("pe", lambda e, kc=kc, i=i, pt=pt, wv=wv: e.matmul(
                        pt[:st, 0:256], lhsT=mT[:, kc, i * 128:i * 128 + st], rhs=wv[:, kc, :],
                        start=(kc == 0), stop=(kc == KC - 1)), reads=[wk, key_of(mT)], writes=pk)
                S.add("dve", lambda e, i=i, dpc=dpc, pt=pt: e.tensor_tensor(
                    out=xin[:st, i, dpc * 256:(dpc + 1) * 256], in0=xin[:st, i, dpc * 256:(dpc + 1) * 256],
                    in1=pt[:st, 0:256], op=ALU.add),
                    reads=pk + [("xin", i)], writes=[("xin", i)])
        dbg_dump("on", onT[:].rearrange("p h t -> p (h t)"), [("onT",)])
        dbg_dump("c", cT[:].rearrange("p h t -> p (h t)"), [("cT",)])
        dbg_dump("m", mT[:].rearrange("p h t -> p (h t)"), [key_of(mT)])
        dbg_dump("x1", xin[:, 0, :], [("xin", 0)])
        for i in range(nsub):
            load_g("g2")
            norm_to_hT(i, st, None, h2T, i * 128)
        for p in range(NP_FF1):
            def cons_ff(ch, pt, pk, p=p):
                fc = 2 * p + ch
                ft = ftmp[fc % 2]
                ftk = ("ftmp", fc % 2)
                S.add("act", lambda e: e.activation(out=ft[:, 0:ntok], in_=pt[:, 0:ntok], func=AF.Relu),
                      reads=pk, writes=[ftk])
                S.add("pool", lambda e: e.tensor_tensor(out=hfT[:, fc, 0:ntok], in0=ft[:, 0:ntok],
                                                        in1=ft[:, 0:ntok], op=ALU.mult),
                      reads=[ftk], writes=[("hfT", fc)])
            proj_fm(wb_ff1[p], ("wb_ff1", p), 2, h2T, ntok, cons_ff)
        for dpc in range(8):
            pts = [ps1() for _ in range(nsub)]
            for fb in range(4):
                ws, wk = wload(wb_ff2[fb * 8 + dpc], ("wb_ff2", fb * 8 + dpc))
                wv = ws[:].rearrange("p (k c) -> p k c", c=256)
                for fc in range(16):
                    for i in range(nsub):
                        pt, pk = pts[i]
                        S.add("pe", lambda e, fb=fb, fc=fc, i=i, pt=pt, wv=wv: e.matmul(
                            pt[:st, 0:256], lhsT=hfT[:, fb * 16 + fc, i * 128:i * 128 + st], rhs=wv[:, fc, :],
                            start=(fb == 0 and fc == 0), stop=(fb == 3 and fc == 15)),
                            reads=[wk, ("hfT", fb * 16 + fc)], writes=pk)
            for i in range(nsub):
                pt, pk = pts[i]
                S.add("dve", lambda e, i=i, dpc=dpc, pt=pt: e.tensor_tensor(
                    out=xin[:st, i, dpc * 256:(dpc + 1) * 256], in0=xin[:st, i, dpc * 256:(dpc + 1) * 256],
                    in1=pt[:st, 0:256], op=ALU.add),
                    reads=pk + [("xin", i)], writes=[("xin", i)])
        dbg_dump("x2", xin[:, 0, :], [("xin", 0)])
        load_g("gf")
        for i in range(nsub):
            xk = ("xin", i)
            fb_ = 8 + 3 * (i % 2)
            S.add("pool", lambda e, fb_=fb_: e.memset(stat[:, fb_:fb_ + 1], 0.0), writes=[("stat", fb_)])
            S.add("act", lambda e, i=i, fb_=fb_: e.activation(out=hbufs[i % 2][:st, :], in_=xin[:st, i, :],
                                                              func=AF.Square, accum_out=stat[:st, fb_:fb_ + 1]),
                  reads=[xk, ("stat", fb_)], writes=[("hbuf", 0), ("stat", fb_)])
            S.add("act", lambda e, fb_=fb_: e.activation(out=stat[:st, fb_ + 1:fb_ + 2], in_=stat[:st, fb_:fb_ + 1],
                                                         func=AF.Ln, scale=1.0 / D, bias=epscol[:st, 0:1]),
                  reads=[("stat", fb_)] + RC, writes=[("stat", fb_ + 1)])
            S.add("act", lambda e, fb_=fb_: e.activation(out=stat[:st, fb_ + 2:fb_ + 3], in_=stat[:st, fb_ + 1:fb_ + 2],
                                                         func=AF.Exp, scale=-0.5),
                  reads=[("stat", fb_ + 1)], writes=[("stat", fb_ + 2)])
            S.add("dve", lambda e, i=i, fb_=fb_: e.scalar_tensor_tensor(out=xin[:st, i, :], in0=xin[:st, i, :],
                                                                        scalar=stat[:st, fb_ + 2:fb_ + 3],
                                                                        in1=grow[:st, :], op0=ALU.mult, op1=ALU.mult),
                  reads=[xk, ("stat", fb_ + 2), ("grow",)] + RC, writes=[xk])
            dst = y_dst_fn(i)
            S.add("sp", lambda e, i=i, dst=dst: e.dma_start(out=dst, in_=xin[:st, i, :]),
                  reads=[xk], writes=[("yout", len(S.ops["sp"]))], chan=("x", i))

    def conv_out_emitter(dst_ap):
        def f(ntok):
            pp, ppk = ps2()
            for c in range(8):
                S.add("pe", lambda e, c=c, pp=pp: e.matmul(
                    pp[0:30, c * 128:(c + 1) * 128], lhsT=uc[:, c, ntok:ntok + 30], rhs=ident_f,
                    start=True, stop=True), reads=[("uc",)] + RC, writes=ppk)
            S.add("act", lambda e, pp=pp: e.activation(out=cbuf[0:30, :], in_=pp[0:30, :], func=AF.Copy),
                  reads=ppk, writes=[("tmk",)])
            S.add("sp", lambda e: e.dma_start(out=dst_ap, in_=cbuf[0:30, :]),
                  reads=[("tmk",)], writes=[("cout", id(dst_ap))], chan=("cb",))
        return f

    def state_out_emitter(dst_ap):
        def f():
            S.add("sp", lambda e: e.dma_start(out=dst_ap.rearrange("h d v -> d h v"), in_=Sst[:]),
                  reads=[("Sst",)], writes=[("sout", id(dst_ap))], chan=("so",))
        return f

    for t in range(cfg.n_main):
        base = (cfg.n_hist + t) * T
        last = (t == cfg.n_main - 1)
        full_tile(T, NS, 128,
                  lambda i, base=base: xrows[base + i * 128:base + (i + 1) * 128, :],
                  lambda i, t=t: y_main[t * T + i * 128:t * T + (i + 1) * 128, :],
                  emit_conv_out=conv_out_emitter(cp_out[:, :]) if last else None,
                  emit_state_out=state_out_emitter(sp_out) if last else None)

    if cfg.sample:
        S.add("sp", lambda e: e.dma_start(out=Sst[:], in_=s0_in.rearrange("h d v -> d h v")),
              reads=[], writes=[("Sst",)], chan=("so",))
        S.add("pool", lambda e: e.tensor_copy(out=Sbf[:], in_=Sst[:]), reads=[("Sst",)], writes=[("Sbf",)])
        S.add("sp", lambda e: e.dma_start(out=cbuf[0:30, :], in_=c0_in[:, :]), reads=[], writes=[("tmk",)],
              chan=("cb",))
        for c in range(8):
            pt, pk = ps1()
            S.add("pe", lambda e, c=c, pt=pt: e.matmul(pt[:, 0:30], lhsT=cbuf[0:30, c * 128:(c + 1) * 128],
                                                        rhs=consts[0:30, C_ID:C_ID + 30], start=True, stop=True),
                  reads=[("tmk",)] + RC, writes=pk)
            S.add("dve", lambda e, c=c, pt=pt: e.tensor_copy(out=histtmp[:, c, :], in_=pt[:, 0:30]),
                  reads=pk, writes=[("histtmp",)])
        full_tile(DEC_SEQ, 1, DEC_SEQ,
                  lambda i: xs_in[:, :],
                  lambda i: y_s[:, :],
                  emit_conv_out=conv_out_emitter(cs_out[:, :]),
                  emit_state_out=state_out_emitter(ss_out))

    outkeys = [k for k in S.lastw if k[0] in ("yout", "cout", "sout")]
    S.add("sp", lambda e: e.nop(), reads=outkeys)

    S.finalize()
    sems = {e: es.enter_context(nc.semaphore(f"sem_{e}")) for e in Sched.ENG if e != "sp"}
    chan_sems = {}
    for ch in S.chan_count:
        chan_sems[ch] = es.enter_context(nc.semaphore("c_" + "_".join(str(x) for x in ch)))
    with nc.Block() as block:
        @block.tensor
        def _(e):
            S.emit("pe", e, sems, chan_sems)

        @block.scalar
        def _(e):
            S.emit("act", e, sems, chan_sems)

        @block.vector
        def _(e):
            S.emit("dve", e, sems, chan_sems)

        @block.gpsimd
        def _(e):
            S.emit("pool", e, sems, chan_sems)

        @block.sync
        def _(e):
            S.emit("sp", e, sems, chan_sems)
    es.close()
    n_ins = {e: len(S.ops[e]) for e in Sched.ENG}
    return nc, n_ins


HIST_ROWS = 12800
NS_TILE = 4


def core_inputs(cfg, xrows, xs, s0, c0, shared):
    m = dict(shared)
    m["xrows"] = np.ascontiguousarray(xrows, dtype=np.float32)
    m["xs"] = np.ascontiguousarray(xs, dtype=np.float32)
    m["s0"] = np.ascontiguousarray(s0, dtype=np.float32)
    m["c0"] = np.ascontiguousarray(c0, dtype=np.float32)
    return m


def shared_inputs(norm1_g, w_in, lb_logits, hgrn_norm_g, w_proj_h, dw_kernel, dw_bias, conv_ln_g, conv_ln_b,
                  w_proj_c, w_out, norm2_g, w_ff1, w_ff2, final_norm_g):
    f = lambda a: np.ascontiguousarray(np.asarray(a), dtype=np.float32)
    return {
        "w_in": f(w_in[0]), "w_proj_h": f(w_proj_h[0]), "w_proj_c": f(w_proj_c[0]), "w_out": f(w_out[0]),
        "w_ff1": f(w_ff1[0]), "w_ff2": f(w_ff2[0]),
        "norm1_g": f(norm1_g[0]).reshape(1, D), "norm2_g": f(norm2_g[0]).reshape(1, D),
        "final_norm_g": f(final_norm_g).reshape(1, D), "lb_logits": f(lb_logits),
        "hgrn_norm_g": f(hgrn_norm_g[0]).reshape(128, 1), "dw_kernel": f(dw_kernel[0]),
        "dw_bias": f(dw_bias[0]).reshape(CW, 1), "conv_ln_g": f(conv_ln_g[0]).reshape(CW, 1),
        "conv_ln_b": f(conv_ln_b[0]).reshape(CW, 1), "consts": make_consts(),
    }


def kernel(x_prompt, x_sample, state_hgrn, state_conv, meta_tokens, norm1_g, w_in, lb_logits,
           hgrn_norm_g, w_proj_h, dw_kernel, dw_bias, conv_ln_g, conv_ln_b, w_proj_c, w_out,
           norm2_g, w_ff1, w_ff2, final_norm_g):
    x_prompt = np.asarray(x_prompt, dtype=np.float32)
    x_sample = np.asarray(x_sample, dtype=np.float32)
    state_hgrn = np.asarray(state_hgrn, dtype=np.float32)
    state_conv = np.asarray(state_conv, dtype=np.float32)
    meta = np.asarray(meta_tokens, dtype=np.float32)
    T = NS_TILE * 128
    cfg = Cfg(n_hist=HIST_ROWS // T, n_main=CHUNK_TOK // T, ns=NS_TILE)
    shared = shared_inputs(norm1_g, w_in, lb_logits, hgrn_norm_g, w_proj_h, dw_kernel, dw_bias, conv_ln_g,
                           conv_ln_b, w_proj_c, w_out, norm2_g, w_ff1, w_ff2, final_norm_g)
    in_maps = []
    for c in range(8):
        s, j = c // 4, c % 4
        start = j * CHUNK_TOK
        rows = np.zeros((HIST_ROWS + CHUNK_TOK, D), np.float32)
        rows[HIST_ROWS:] = x_prompt[s, start:start + CHUNK_TOK]
        lo = start - HIST_ROWS
        if lo >= 0:
            rows[:HIST_ROWS] = x_prompt[s, lo:start]
        else:
            nx = start
            if nx > 0:
                rows[HIST_ROWS - nx:HIST_ROWS] = x_prompt[s, 0:start]
            rows[HIST_ROWS - nx - N_META:HIST_ROWS - nx] = meta
        in_maps.append(core_inputs(cfg, rows, x_sample[c], state_hgrn[0, c], state_conv[0, c], shared))
    nc, _ = build_program(cfg)
    res = run_bass_kernel_spmd(nc, in_maps, core_ids=list(range(8)))
    R = res.results
    y_prompt = np.stack([np.concatenate([R[s * 4 + j]["y_main"] for j in range(4)], axis=0) for s in range(2)], 0)
    y_sample = np.stack([R[c]["y_s"] for c in range(8)], 0)
    new_hp = np.stack([R[3]["s_p"], R[7]["s_p"]], 0)[None]
    new_cp = np.stack([R[3]["c_p"], R[7]["c_p"]], 0)[None]
    new_hs = np.stack([R[c]["s_s"] for c in range(8)], 0)[None]
    new_cs = np.stack([R[c]["c_s"] for c in range(8)], 0)[None]
    return (y_prompt.astype(np.float32), y_sample.astype(np.float32), new_hp.astype(np.float32),
            new_cp.astype(np.float32), new_hs.astype(np.float32), new_cs.astype(np.float32))
```

```python
import contextlib
import numpy as np
import ml_dtypes
import concourse.bass as bass
import concourse.mybir as mybir
from concourse.bass_utils import run_bass_kernel_spmd

F32 = mybir.dt.float32
BF16 = mybir.dt.bfloat16
AF = mybir.ActivationFunctionType
ALU = mybir.AluOpType

D = 2048
NH = 8
HW = 1024
CW = 1024
CK = 31
DFF = 8192
NIN = 10240
EPS = 1e-6
N_META = 16
SEQ = 16384
CHUNK_TOK = 4096
DEC_SEQ = 64
KC = D // 128

OFF_Q, OFF_F, OFF_V, OFF_OG, OFF_CA, OFF_CB, OFF_GH, OFF_GC = 0, 1024, 2048, 3072, 4096, 5120, 6144, 8192

C_ID = 0
C_SU = 128
C_M128 = 256
C_M64 = 384
C_RG128 = 512
C_RG64 = 514
C_O128 = 516
C_OC = 644
C_MASK = 772
NCONST = C_MASK + 8 * 128


def make_consts():
    c = np.zeros((128, NCONST), np.float32)
    s = np.arange(128)[:, None]
    t = np.arange(128)[None, :]
    c[:, C_ID:C_ID + 128] = (s == t)
    c[:, C_SU:C_SU + 128] = (s > t)
    c[:, C_M128:C_M128 + 128] = (s <= t).astype(np.float32) - (s <= 63).astype(np.float32)
    c[:, C_M64:C_M64 + 128] = (s <= t).astype(np.float32) - (s <= 31).astype(np.float32)
    c[:, C_RG128] = (np.arange(128) <= 63)
    c[:, C_RG128 + 1] = 1.0
    c[:, C_RG64] = (np.arange(128) <= 31)
    c[:, C_RG64 + 1] = 1.0
    c[:, C_O128:C_O128 + 128] = 1.0 / 128.0
    c[:, C_OC:C_OC + 128] = 1.0 / 1024.0
    for h in range(8):
        c[:, C_MASK + h * 128:C_MASK + (h + 1) * 128] = (s <= t)
    return c


class Sched:
    ENG = ("pe", "act", "dve", "pool", "sp")

    def __init__(self):
        self.ops = {e: [] for e in self.ENG}
        self.lastw = {}
        self.readers = {}
        self.chan_count = {}
        self.alias = {}

    def _exp(self, keys):
        out = []
        for k in keys:
            a = self.alias.get(k)
            if a is None:
                out.append(k)
            else:
                out.extend(a)
        return out

    def add(self, eng, fn, reads=(), writes=(), chan=None):
        reads = self._exp(reads)
        writes = self._exp(writes)
        deps = []
        for k in reads:
            ev = self.lastw.get(k)
            if ev is not None:
                deps.append(ev)
        for k in writes:
            ev = self.lastw.get(k)
            if ev is not None:
                deps.append(ev)
            r = self.readers.get(k)
            if r:
                deps.extend((kk[0], kk[1], v) for kk, v in r.items())
        if chan is None:
            ev = ("E", eng, len(self.ops[eng]))
        else:
            c = self.chan_count.get(chan, 0)
            self.chan_count[chan] = c + 1
            ev = ("D", chan, c)
        self.ops[eng].append({"fn": fn, "deps": deps, "ev": ev, "chan": chan, "sig": False, "eng": eng})
        for k in reads:
            r = self.readers.setdefault(k, {})
            kk = (ev[0], ev[1])
            if r.get(kk, -1) < ev[2]:
                r[kk] = ev[2]
        for k in writes:
            self.lastw[k] = ev
            self.readers[k] = {}
        return ev

    def finalize(self):
        for e in self.ENG:
            known = {}
            for op in self.ops[e]:
                need = {}
                for (kind, ident, idx) in op["deps"]:
                    if kind == "E" and ident == e and e in ("pe", "sp"):
                        continue
                    kk = (kind, ident)
                    if known.get(kk, -1) >= idx:
                        continue
                    if need.get(kk, -1) < idx:
                        need[kk] = idx
                for kk, idx in need.items():
                    known[kk] = idx
                    if kk[0] == "E":
                        self.ops[kk[1]][idx]["sig"] = True
                op["need"] = need
        self.cnt = {}
        for e in self.ENG:
            c = 0
            arr = []
            for op in self.ops[e]:
                if op["sig"] and op["chan"] is None:
                    c += 1
                arr.append(c)
            self.cnt[e] = arr

    def emit(self, e, engine, sems, chan_sems):
        for op in self.ops[e]:
            for (kind, ident), idx in op["need"].items():
                if kind == "E":
                    engine.wait_ge(sems[ident], self.cnt[ident][idx])
                else:
                    engine.wait_ge(chan_sems[ident], 16 * (idx + 1))
            ins = op["fn"](engine)
            if op["chan"] is not None:
                ins.then_inc(chan_sems[op["chan"]], 16)
            elif op["sig"]:
                ins.then_inc(sems[e], 1)


class Cfg:
    def __init__(self, n_hist, n_main, ns=1, nslot=5, sample=True):
        self.n_hist = n_hist
        self.n_main = n_main
        self.ns = ns
        self.T = ns * 128
        self.nslot = nslot
        self.sample = sample
        self.rows = (n_hist + n_main) * self.T


def build_program(cfg):
    nc = bass.Bass("TRN2", target_bir_lowering=False)
    S = Sched()
    T = cfg.T
    NS = cfg.ns
    es = contextlib.ExitStack()

    def din(name, shape, dt=F32):
        return nc.dram_tensor(name, list(shape), dt, kind="ExternalInput").ap()

    def dout(name, shape, dt=F32):
        return nc.dram_tensor(name, list(shape), dt, kind="ExternalOutput").ap()

    xrows = din("xrows", [cfg.rows, D])
    xs_in = din("xs", [DEC_SEQ, D])
    s0_in = din("s0", [NH, 128, 128])
    c0_in = din("c0", [CK - 1, CW])
    w_in = din("w_in", [D, NIN])
    w_ph = din("w_proj_h", [HW, D])
    w_pc = din("w_proj_c", [CW, D])
    w_out = din("w_out", [D, D])
    w_ff1 = din("w_ff1", [D, DFF])
    w_ff2 = din("w_ff2", [DFF, D])
    g1_in = din("norm1_g", [1, D])
    g2_in = din("norm2_g", [1, D])
    gf_in = din("final_norm_g", [1, D])
    lbl_in = din("lb_logits", [2, HW])
    hg_in = din("hgrn_norm_g", [128, 1])
    dwk_in = din("dw_kernel", [CK, CW])
    dwb_in = din("dw_bias", [CW, 1])
    lng_in = din("conv_ln_g", [CW, 1])
    lnb_in = din("conv_ln_b", [CW, 1])
    consts_in = din("consts", [128, NCONST])

    y_main = dout("y_main", [cfg.n_main * T, D])
    y_s = dout("y_s", [DEC_SEQ, D])
    sp_out = dout("s_p", [NH, 128, 128])
    cp_out = dout("c_p", [CK - 1, CW])
    ss_out = dout("s_s", [NH, 128, 128])
    cs_out = dout("c_s", [CK - 1, CW])

    dbg = None
    if getattr(cfg, "debug", False):
        dbg = {"x1": dout("dbg_x1", [128, D]), "m": dout("dbg_m", [128, KC * T], BF16),
               "on": dout("dbg_on", [128, NH * T], BF16), "c": dout("dbg_c", [128, 8 * T], BF16),
               "x2": dout("dbg_x2", [128, D])}
    dbg_done = set()

    def dbg_dump(name, src_ap, rkeys):
        if dbg is None or name in dbg_done:
            return
        dbg_done.add(name)
        S.add("sp", lambda e: e.dma_start(out=dbg[name][:, :], in_=src_ap), reads=rkeys,
              writes=[("yout", "dbg" + name)], chan=("dbg", name))

    NP_IN = NIN // 256
    NP_PH = D // 512
    NP_OUT = D // 256
    NP_FF1 = DFF // 256
    NP_FF2 = 4 * (D // 256)
    wb_in = nc.dram_tensor("wb_in", [NP_IN, 128, 4096], BF16).ap()
    wb_ph = nc.dram_tensor("wb_ph", [NP_PH, 128, 4096], BF16).ap()
    wb_pc = nc.dram_tensor("wb_pc", [NP_PH, 128, 4096], BF16).ap()
    wb_out = nc.dram_tensor("wb_out", [NP_OUT, 128, 4096], BF16).ap()
    wb_ff1 = nc.dram_tensor("wb_ff1", [NP_FF1, 128, 4096], BF16).ap()
    wb_ff2 = nc.dram_tensor("wb_ff2", [NP_FF2, 128, 4096], BF16).ap()

    def sb(name, shape, dt=F32):
        return es.enter_context(nc.sbuf_tensor("s_" + name, list(shape), dt))

    consts = sb("consts", [128, NCONST])
    constb = sb("constb", [128, 128], BF16)
    grow = sb("grow", [128, D])
    lbrow = sb("lbrow", [128, HW])
    omlrow = sb("omlrow", [128, HW])
    hgcol = sb("hgcol", [128, 1])
    epscol = sb("epscol", [128, 1])
    dwk = sb("dwk", [128, 8, CK])
    dwb = sb("dwb", [128, 8])
    lng = sb("lng", [128, 8])
    lnb = sb("lnb", [128, 8])
    xin = sb("xin", [128, NS, D])
    hbuf0_ = sb("hbuf0", [128, D], BF16)
    hbufs = [hbuf0_, hbuf0_]
    stat = sb("stat", [128, 16])
    rgs = sb("rgs", [128, NH, 2])
    erg = sb("erg", [128, NH, 2])
    Sst = sb("Sst", [128, NH, 128])
    Sbf = sb("Sbf", [128, NH, 128], BF16)
    histtmp = sb("histtmp", [128, 8, 30])
    ftmp = [sb(f"ftmp{j}", [128, T]) for j in range(2)]

    u = T / 32.0
    ARENA_KB = int(np.ceil(max(3 * u + 36 + u / 2, 5 * u, 4.5 * u + 2)))
    arena = sb("arena", [128, ARENA_KB * 256])
    names = {}

    def regrange(key, lo, hi):
        S.alias[key] = [("pg", p) for p in range(int(lo // 1024), int((hi + 1023) // 1024))]

    def av(name, off_kb, shape, dt=F32):
        esz = 4 if dt == F32 else 2
        nel = int(np.prod(shape[1:]))
        lo = int(round(off_kb * 1024))
        assert lo % 32 == 0 and lo + nel * esz <= ARENA_KB * 1024, (name, lo, nel * esz)
        v = arena[:, lo // 4:(lo + nel * esz + 3) // 4]
        if dt == BF16:
            v = v.bitcast(BF16)
        if len(shape) == 3:
            v = v.rearrange("p (a b) -> p a b", b=shape[2])
        regrange((name,), lo, lo + nel * esz)
        names[id(v)] = name
        v_off[name] = lo
        return v
    v_off = {}

    hT = av("hT", 0, [128, KC, T], BF16)
    h2T = av("h2T", 0, [128, KC, T], BF16)
    qs = av("qs", u, [128, NH, T])
    sog = av("sog", 2 * u, [128, NH, T])
    b0 = 3 * u
    tma = av("tma", b0, [128, HW])
    tml = av("tml", b0 + 4, [128, HW])
    tmk = av("tmk", b0 + 8, [128, HW])
    ktb = av("ktb", b0 + 12, [128, HW], BF16)
    vb = av("vb", b0 + 14, [128, HW], BF16)
    e1 = av("e1", b0 + 16, [128, NH, 128])
    e2 = av("e2", b0 + 20, [128, NH, 128])
    qp = av("qp", b0 + 24, [128, NH, 128], BF16)
    kp = av("kp", b0 + 26, [128, NH, 128], BF16)
    qh = av("qh", b0 + 28, [128, NH, 128], BF16)
    stb = av("stb", b0 + 30, [128, NH, 128], BF16)
    onT = av("onT", b0 + 36, [128, NH, T], BF16)
    for j in range(4):
        regrange(("tma", j), v_off["tma"] + j * 1024, v_off["tma"] + (j + 1) * 1024)
        regrange(("vb", j), v_off["vb"] + j * 512, v_off["vb"] + (j + 1) * 512)
    for h in range(NH):
        S.alias[("qh", h)] = S.alias[("qh",)]
    lbtmp = tma
    dwkT = tml
    cbuf = tmk
    osq = e1
    rstd_o = e2
    uc = av("uc", u, [128, 8, 30 + T])
    dwa = av("dwa", 2 * u + 1, [128, 8, T])
    for c in range(8):
        regrange(("dwa", c), v_off["dwa"] + c * T * 4, v_off["dwa"] + (c + 1) * T * 4)
    sqt = [av("sq0", 3 * u + 1, [128, T])]
    cT = av("cT", 3 * u + 1 + u / 8, [128, 8, T], BF16)
    d0 = 3 * u + 1 + u / 8 + u / 2
    mu = av("mu", d0, [128, T])
    musq = av("musq", d0 + u / 8, [128, T])
    rstd_c = musq
    tmp1 = sqt[0]
    tmp3 = av("tmp3", d0 + 2 * u / 8, [128, T])
    tmp4 = av("tmp4", d0 + 3 * u / 8, [128, T])
    mT = av("mT", d0 + 4 * u / 8, [128, KC, T], BF16)
    assert d0 + 4 * u / 8 + u <= b0 + 36 or NS < 4
    for dch_ in range(KC):
        regrange(("mT", dch_), v_off["mT"] + dch_ * T * 2, v_off["mT"] + (dch_ + 1) * T * 2)
    hfT = av("hfT", u, [128, DFF // 128, T], BF16)
    for fc in range(DFF // 128):
        regrange(("hfT", fc), v_off["hfT"] + fc * T * 2, v_off["hfT"] + (fc + 1) * T * 2)
    wslots = [sb(f"ws{i}", [128, 4096], BF16) for i in range(cfg.nslot)]

    psum = [es.enter_context(nc.psum_tensor(f"ps{i}", [128, 1024], F32)) for i in range(4)]

    class PS:
        n1 = 0
        n2 = 0
        hist = False

    def ps1():
        if PS.hist:
            i = 6 + PS.n1 % 2
        else:
            i = PS.n1 % 8
        PS.n1 += 1
        return psum[i // 2][:, (i % 2) * 512:(i % 2) * 512 + 512], [("ps", i)]

    def ps2():
        i = PS.n2 % (3 if PS.hist else 4)
        PS.n2 += 1
        return psum[i], [("ps", 2 * i), ("ps", 2 * i + 1)]

    class WS:
        n = 0

    def wload(src_ap, src_key, live=()):
        s = WS.n % cfg.nslot
        while ("ws", s) in live:
            WS.n += 1
            s = WS.n % cfg.nslot
        WS.n += 1
        S.add("sp", lambda e, s=s, src_ap=src_ap: e.dma_start(out=wslots[s][:], in_=src_ap),
              reads=[src_key], writes=[("ws", s)], chan=("ws", s))
        return wslots[s], ("ws", s)

    conv_q = []

    def convert(name, wb, pieces, src_fn, chan):
        pieces = list(pieces)
        for n, p in enumerate(pieces):
            conv_q.append((name, wb, p, src_fn(p), chan, pieces if n == len(pieces) - 1 else None))

    def emit_conv(n):
        for _ in range(n):
            if not conv_q:
                return
            name, wb, p, src, chan, fin = conv_q.pop(0)
            ev = S.add("pool", lambda e, p=p, src=src, wb=wb: e.dma_start(
                out=wb[p].rearrange("q (a b) -> q a b", b=src.shape[-1]), in_=src),
                writes=[(name, p)], chan=("cv", chan))
            if fin is not None:
                for pp in fin:
                    S.lastw[(name, pp)] = ev

    w_in_v = w_in.rearrange("(kc p) n -> p kc n", p=128)
    w_ph_v = w_ph.rearrange("(kc p) n -> p kc n", p=128)
    w_pc_v = w_pc.rearrange("(kc p) n -> p kc n", p=128)
    w_out_v = w_out.rearrange("(kc p) n -> p kc n", p=128)
    w_ff1_v = w_ff1.rearrange("(kc p) n -> p kc n", p=128)
    w_ff2_v = w_ff2.rearrange("(fb fc p) n -> p fb fc n", p=128, fc=16)

    def cload(dst, src, eng="sp", slow=False):
        return S.add(eng, lambda e: e.dma_start(out=dst, in_=src, allow_slow_non_contiguous=slow),
                     writes=[], chan=("const",))

    cevs = []
    cevs.append(cload(consts[:], consts_in[:, :]))
    cevs.append(cload(lbrow[:], lbl_in[0:1, :].to_broadcast([128, HW])))
    cevs.append(cload(lbtmp[:], lbl_in[1:2, :].to_broadcast([128, HW])))
    cevs.append(cload(hgcol[:], hg_in[:, :]))
    cevs.append(cload(dwkT[0:CK, :], dwk_in[:, :]))
    cevs.append(cload(dwb[:], dwb_in.rearrange("(c p) o -> p (c o)", p=128), slow=True))
    cevs.append(cload(lng[:], lng_in.rearrange("(c p) o -> p (c o)", p=128), slow=True))
    cevs.append(cload(lnb[:], lnb_in.rearrange("(c p) o -> p (c o)", p=128), slow=True))
    CK_ALL = ("constk",)
    S.lastw[CK_ALL] = cevs[-1]
    RC = [CK_ALL]

    in_src = lambda p: w_in_v[:, :, p * 256:(p + 1) * 256]
    PF_, PV_, PQ_, POG_, PCA_, PCB_, PGH_, PGC_ = (o // 256 for o in (OFF_F, OFF_V, OFF_Q, OFF_OG, OFF_CA, OFF_CB,
                                                                      OFF_GH, OFF_GC))
    convert("wb_in", wb_in, list(range(PCA_, PCA_ + 4)) + list(range(PCB_, PCB_ + 4)), in_src, "in_c")
    convert("wb_in", wb_in, list(range(PQ_, PQ_ + 4)) + list(range(POG_, POG_ + 4)), in_src, "in_q")
    convert("wb_in", wb_in, list(range(PF_, PF_ + 4)) + list(range(PV_, PV_ + 4)), in_src, "in_fv")
    convert("wb_in", wb_in, list(range(PGH_, PGH_ + 8)) + list(range(PGC_, PGC_ + 8)), in_src, "in_g")
    convert("wb_ph", wb_ph, range(NP_PH), lambda p: w_ph_v[:, :, p * 512:(p + 1) * 512], "ph")
    convert("wb_pc", wb_pc, range(NP_PH), lambda p: w_pc_v[:, :, p * 512:(p + 1) * 512], "pc")
    convert("wb_out", wb_out, range(NP_OUT), lambda p: w_out_v[:, :, p * 256:(p + 1) * 256], "out")
    convert("wb_ff1", wb_ff1, range(0, 16), lambda p: w_ff1_v[:, :, p * 256:(p + 1) * 256], "ff1a")
    convert("wb_ff1", wb_ff1, range(16, 32), lambda p: w_ff1_v[:, :, p * 256:(p + 1) * 256], "ff1b")
    ff2_src = lambda p: w_ff2_v[:, p // 8, :, (p % 8) * 256:(p % 8) * 256 + 256]
    convert("wb_ff2", wb_ff2, [fb * 8 + d for d in range(0, 4) for fb in range(4)], ff2_src, "ff2a")
    convert("wb_ff2", wb_ff2, [fb * 8 + d for d in range(4, 8) for fb in range(4)], ff2_src, "ff2b")

    S.add("dve", lambda e: e.tensor_tensor(out=lbtmp[:], in0=lbrow[:], in1=lbtmp[:], op=ALU.subtract),
          reads=RC, writes=[("tma", 0), ("tma", 1), ("tma", 2), ("tma", 3)])
    S.add("act", lambda e: e.activation(out=lbrow[:], in_=lbtmp[:], func=AF.Sigmoid),
          reads=[("tma", 0), ("tma", 1), ("tma", 2), ("tma", 3)], writes=[("lbrow",)])
    S.add("dve", lambda e: e.tensor_scalar(out=omlrow[:], in0=lbrow[:], scalar1=-1.0, scalar2=1.0,
                                           op0=ALU.mult, op1=ALU.add),
          reads=[("lbrow",)], writes=[("omlrow",)])
    S.add("pool", lambda e: e.memset(epscol[:], EPS), writes=[("epscol",)])
    S.add("dve", lambda e: e.tensor_copy(out=constb[:], in_=consts[:, C_ID:C_ID + 128]),
          reads=RC, writes=[("constb",)])
    RC = RC + [("epscol",)]
    RC2 = RC + [("lbrow",), ("omlrow",), ("constb",)]
    for c in range(8):
        pt, pk = ps1()
        S.add("pe", lambda e, c=c, pt=pt: e.matmul(pt[:, 0:CK], lhsT=dwkT[0:CK, c * 128:(c + 1) * 128],
                                                    rhs=consts[0:CK, C_ID:C_ID + CK], start=True, stop=True),
              reads=RC + [("tml",)], writes=pk)
        S.add("dve", lambda e, c=c, pt=pt: e.tensor_copy(out=dwk[:, c, :], in_=pt[:, 0:CK]),
              reads=pk, writes=[("dwk",)])
    RC2 = RC2 + [("dwk",)]

    ident_f = consts[:, C_ID:C_ID + 128]

    def key_of(t):
        return (names[id(t)],)

    class GR:
        cur = None

    def load_g(which):
        if GR.cur == which:
            return
        GR.cur = which
        src = {"g1": g1_in, "g2": g2_in, "gf": gf_in}[which]
        S.add("sp", lambda e: e.dma_start(out=grow[:], in_=src[0:1, :].to_broadcast([128, D])),
              writes=[("grow",)], chan=("grow",))

    def norm_to_hT(i, st, g_row, dstT, tok0):
        xk = ("xin", i)
        par = i % 2
        hbuf = hbufs[par]
        junk = hbuf
        hk = ("hbuf", 0)
        c0_, c1_, c2_ = 3 * par, 3 * par + 1, 3 * par + 2
        S.add("pool", lambda e: e.memset(stat[:, c0_:c0_ + 1], 0.0), writes=[("stat", c0_)])
        S.add("act", lambda e: e.activation(out=junk[:st, :], in_=xin[:st, i, :], func=AF.Square,
                                            accum_out=stat[:st, c0_:c0_ + 1]),
              reads=[xk, ("stat", c0_)], writes=[hk, ("stat", c0_)])
        S.add("act", lambda e: e.activation(out=stat[:st, c1_:c1_ + 1], in_=stat[:st, c0_:c0_ + 1], func=AF.Ln,
                                            scale=1.0 / D, bias=epscol[:st, 0:1]),
              reads=[("stat", c0_)] + RC, writes=[("stat", c1_)])
        S.add("act", lambda e: e.activation(out=stat[:st, c2_:c2_ + 1], in_=stat[:st, c1_:c1_ + 1], func=AF.Exp,
                                            scale=-0.5),
              reads=[("stat", c1_)], writes=[("stat", c2_)])
        S.add("dve", lambda e: e.scalar_tensor_tensor(out=hbuf[:st, :], in0=xin[:st, i, :], scalar=stat[:st, c2_:c2_ + 1],
                                                      in1=grow[:st, :], op0=ALU.mult, op1=ALU.mult),
              reads=[xk, ("stat", c2_), ("grow",)] + RC, writes=[hk])
        for half in range(2):
            pt, pk = ps1()
            ptb = pt.bitcast(BF16)
            for j in range(8):
                kc = half * 8 + j
                S.add("pe", lambda e, kc=kc, j=j, ptb=ptb: e.transpose(
                    ptb[:, j * 128:j * 128 + st], hbuf[:st, kc * 128:(kc + 1) * 128], constb[:st, :st]),
                    reads=[hk] + RC2, writes=pk)
            eng = "act" if half == 0 else "dve"
            src = ptb.rearrange("p (j t) -> p j t", t=128)[:, :, 0:st]
            dst = dstT[:, half * 8:(half + 1) * 8, tok0:tok0 + st]
            if eng == "act":
                S.add("act", lambda e, src=src, dst=dst: e.activation(out=dst, in_=src, func=AF.Copy),
                      reads=pk, writes=[key_of(dstT)])
            else:
                S.add("dve", lambda e, src=src, dst=dst: e.tensor_copy(out=dst, in_=src),
                      reads=pk, writes=[key_of(dstT)])

    def proj_fm(piece_src, piece_key, col_chunks, rhsT, ntok, consume):
        ws, wk = wload(piece_src, piece_key)
        wv = ws[:].rearrange("p (k c) -> p k c", c=256)
        for ch in range(col_chunks):
            pt, pk = ps1()
            for kc in range(KC):
                S.add("pe", lambda e, kc=kc, ch=ch, pt=pt, wv=wv: e.matmul(
                    pt[:, 0:ntok], lhsT=wv[:, kc, ch * 128:(ch + 1) * 128], rhs=rhsT[:, kc, 0:ntok],
                    start=(kc == 0), stop=(kc == KC - 1)),
                    reads=[wk, key_of(rhsT)], writes=pk)
            consume(ch, pt, pk)

    def hgrn_tm(i, st, tok0, piece0_f, piece0_v):
        for p4 in range(4):
            ws, wk = wload(wb_in[piece0_f + p4], ("wb_in", piece0_f + p4))
            wv = ws[:].rearrange("p (k c) -> p k c", c=256)
            pt, pk = ps1()
            for kc in range(KC):
                S.add("pe", lambda e, kc=kc, pt=pt, wv=wv: e.matmul(
                    pt[:st, 0:256], lhsT=hT[:, kc, tok0:tok0 + st], rhs=wv[:, kc, :],
                    start=(kc == 0), stop=(kc == KC - 1)),
                    reads=[wk, key_of(hT)], writes=pk)
            S.add("act", lambda e, pt=pt, p4=p4: e.activation(out=tma[:st, p4 * 256:(p4 + 1) * 256],
                                                               in_=pt[:st, 0:256], func=AF.Sigmoid),
                  reads=pk, writes=[("tma", p4)])
        for p4 in range(4):
            ws, wk = wload(wb_in[piece0_v + p4], ("wb_in", piece0_v + p4))
            wv = ws[:].rearrange("p (k c) -> p k c", c=256)
            pt, pk = ps1()
            for kc in range(KC):
                S.add("pe", lambda e, kc=kc, pt=pt, wv=wv: e.matmul(
                    pt[:st, 0:256], lhsT=hT[:, kc, tok0:tok0 + st], rhs=wv[:, kc, :],
                    start=(kc == 0), stop=(kc == KC - 1)),
                    reads=[wk, key_of(hT)], writes=pk)
            S.add("dve", lambda e, pt=pt, p4=p4: e.tensor_copy(out=vb[:st, p4 * 256:(p4 + 1) * 256],
                                                                in_=pt[:st, 0:256]),
                  reads=pk, writes=[("vb", p4)])
        TMA = [("tma", j) for j in range(4)]
        S.add("dve", lambda e: e.tensor_tensor(out=tma[:st, :], in0=tma[:st, :], in1=omlrow[:st, :], op=ALU.mult),
              reads=TMA + RC2, writes=TMA)
        S.add("dve", lambda e: e.tensor_tensor(out=tma[:st, :], in0=tma[:st, :], in1=lbrow[:st, :], op=ALU.add),
              reads=TMA + RC2, writes=TMA)
        S.add("act", lambda e: e.activation(out=tml[:st, :], in_=tma[:st, :], func=AF.Ln),
              reads=TMA, writes=[("tml",)])
        S.add("dve", lambda e: e.tensor_scalar(out=tmk[:st, :], in0=tma[:st, :], scalar1=-1.0, scalar2=1.0,
                                               op0=ALU.mult, op1=ALU.add),
              reads=TMA, writes=[("tmk",)])
        pp, ppk = ps2()
        for half in range(2):
            S.add("pe", lambda e, half=half, pp=pp: e.matmul(
                pp[:st, half * 512:(half + 1) * 512], lhsT=consts[:st, C_SU:C_SU + st],
                rhs=tml[:st, half * 512:(half + 1) * 512], start=True, stop=True),
                reads=[("tml",)] + RC, writes=ppk)
        S.add("act", lambda e, pp=pp: e.activation(out=tma[:st, :], in_=pp[:st, :], func=AF.Exp),
              reads=ppk + TMA, writes=TMA)
        S.add("dve", lambda e: e.tensor_tensor(out=ktb[:st, :], in0=tmk[:st, :], in1=tma[:st, :], op=ALU.mult),
              reads=TMA + [("tmk",)], writes=[("ktb",)])

    def hgrn_state_update(st):
        pp, ppk = ps2()
        for h in range(NH):
            S.add("pe", lambda e, h=h, pp=pp: e.matmul(
                pp[:, h * 128:(h + 1) * 128], lhsT=ktb[:st, h * 128:(h + 1) * 128],
                rhs=vb[:st, h * 128:(h + 1) * 128], start=True, stop=True),
                reads=[("ktb",)] + [("vb", j) for j in range(4)], writes=ppk)
        ppv = pp[:].rearrange("p (h v) -> p h v", v=128)
        for h in range(NH):
            S.add("dve", lambda e, h=h, ppv=ppv: e.scalar_tensor_tensor(
                out=Sst[:, h, :], in0=Sst[:, h, :], scalar=erg[:, h, 1:2], in1=ppv[:, h, :],
                op0=ALU.mult, op1=ALU.add),
                reads=ppk + [("erg",), ("Sst",)], writes=[("Sst",)])
        S.add("pool", lambda e: e.tensor_copy(out=Sbf[:], in_=Sst[:]), reads=[("Sst",)], writes=[("Sbf",)])

    def hgrn_decay_cols(st):
        crg = C_RG128 if st == 128 else C_RG64
        pt, pk = ps1()
        for h in range(NH):
            S.add("pe", lambda e, h=h, pt=pt: e.matmul(
                pt[:, h * 2:h * 2 + 2], lhsT=tml[:st, h * 128:(h + 1) * 128], rhs=consts[:st, crg:crg + 2],
                start=True, stop=True),
                reads=[("tml",)] + RC, writes=pk)
        S.add("act", lambda e, pt=pt: e.activation(out=erg[:].rearrange("p h c -> p (h c)"), in_=pt[:, 0:16],
                                                    func=AF.Exp),
              reads=pk, writes=[("erg",)])

    def hgrn_fm(i, st, tok0):
        cm = C_M128 if st == 128 else C_M64
        pb, pbk = ps2()
        pkT, pkTk = ps2()
        for h in range(NH):
            S.add("pe", lambda e, h=h, pb=pb: e.matmul(
                pb[:, h * 128:h * 128 + st], lhsT=tml[:st, h * 128:(h + 1) * 128], rhs=consts[:st, cm:cm + st],
                start=True, stop=True),
                reads=[("tml",)] + RC, writes=pbk)
        for h in range(NH):
            S.add("pe", lambda e, h=h, pkT=pkT: e.matmul(
                pkT[:, h * 128:h * 128 + st], lhsT=tmk[:st, h * 128:(h + 1) * 128], rhs=consts[:st, C_ID:C_ID + st],
                start=True, stop=True),
                reads=[("tmk",)] + RC, writes=pkTk)
        pbv = pb[:].rearrange("p (h t) -> p h t", t=128)[:, :, 0:st]
        pkTv = pkT[:].rearrange("p (h t) -> p h t", t=128)[:, :, 0:st]
        S.add("act", lambda e: e.activation(out=e1[:, :, 0:st], in_=pbv, func=AF.Exp),
              reads=pbk, writes=[("e1",)])
        S.add("act", lambda e: e.activation(out=e2[:, :, 0:st], in_=pbv, func=AF.Exp, scale=-1.0),
              reads=pbk, writes=[("e2",)])
        S.add("dve", lambda e: e.tensor_tensor(out=qp[:, :, 0:st], in0=qs[:, :, tok0:tok0 + st], in1=e1[:, :, 0:st],
                                               op=ALU.mult),
              reads=[("qs",), ("e1",)], writes=[("qp",)])
        S.add("dve", lambda e: e.tensor_tensor(out=kp[:, :, 0:st], in0=pkTv, in1=e2[:, :, 0:st], op=ALU.mult),
              reads=pkTk + [("e2",)], writes=[("kp",)])
        for h in range(NH):
            eng = "dve"
            S.add(eng, lambda e, h=h: e.scalar_tensor_tensor(
                out=qh[:, h, 0:st], in0=qs[:, h, tok0:tok0 + st], scalar=erg[:, h, 0:1], in1=e1[:, h, 0:st],
                op0=ALU.mult, op1=ALU.mult),
                reads=[("qs",), ("e1",), ("erg",)], writes=[("qh", h)])
        psc, psck = ps2()
        for h in range(NH):
            S.add("pe", lambda e, h=h, psc=psc: e.matmul(
                psc[:st, h * 128:h * 128 + st], lhsT=kp[:, h, 0:st], rhs=qp[:, h, 0:st], start=True, stop=True),
                reads=[("kp",), ("qp",)], writes=psck)
        pscv = psc[:].rearrange("p (h t) -> p h t", t=128)[:st, :, 0:st]
        maskv = consts[:, C_MASK:C_MASK + 1024].rearrange("p (h t) -> p h t", t=128)[:st, :, 0:st]
        S.add("dve", lambda e: e.tensor_tensor(out=stb[:st, :, 0:st], in0=pscv, in1=maskv, op=ALU.mult),
              reads=psck + RC, writes=[("stb",)])
        po, pok = ps2()
        for h in range(NH):
            S.add("pe", lambda e, h=h, po=po: e.matmul(
                po[:, h * 128:h * 128 + st], lhsT=vb[:st, h * 128:(h + 1) * 128], rhs=stb[:st, h, 0:st],
                start=True, stop=False),
                reads=[("stb",)] + [("vb", j) for j in range(4)], writes=pok)
            S.add("pe", lambda e, h=h, po=po: e.matmul(
                po[:, h * 128:h * 128 + st], lhsT=Sbf[:, h, :], rhs=qh[:, h, 0:st], start=False, stop=True),
                reads=[("Sbf",), ("qh", h)], writes=pok)
        pov = po[:].rearrange("p (h t) -> p h t", t=128)[:, :, 0:st]
        S.add("act", lambda e: e.activation(out=osq[:, :, 0:st], in_=pov, func=AF.Square),
              reads=pok, writes=[("e1",)])
        pm, pmk = ps2()
        for h2 in range(2):
            for hh in range(4):
                h = h2 * 4 + hh
                S.add("pe", lambda e, h=h, pm=pm: e.matmul(
                    pm[:, h * 128:h * 128 + st], lhsT=consts[:, C_O128:C_O128 + 128], rhs=osq[:, h, 0:st],
                    start=True, stop=True),
                    reads=[("e1",)] + RC, writes=pmk)
        pmv = pm[:].rearrange("p (h t) -> p h t", t=128)[:, :, 0:st]
        S.add("act", lambda e: e.activation(out=rstd_o[:, :, 0:st], in_=pmv, func=AF.Ln, bias=epscol[:, 0:1]),
              reads=pmk + RC, writes=[("e2",)])
        S.add("act", lambda e: e.activation(out=rstd_o[:, :, 0:st], in_=rstd_o[:, :, 0:st], func=AF.Exp, scale=-0.5),
              reads=[("e2",)], writes=[("e2",)])
        S.add("dve", lambda e: e.tensor_tensor(out=osq[:, :, 0:st], in0=pov, in1=rstd_o[:, :, 0:st], op=ALU.mult),
              reads=pok + [("e2",), ("e1",)], writes=[("e1",)])
        S.add("dve", lambda e: e.scalar_tensor_tensor(
            out=onT[:, :, tok0:tok0 + st], in0=osq[:, :, 0:st], scalar=hgcol[:, 0:1], in1=sog[:, :, tok0:tok0 + st],
            op0=ALU.mult, op1=ALU.mult),
            reads=[("e1",), ("sog",)] + RC, writes=[("onT",)])

    def load_x(i, st, src_rows):
        S.add("sp", lambda e: e.dma_start(out=xin[:st, i, :], in_=src_rows), writes=[("xin", i)], chan=("x", i))

    S.add("pool", lambda e: e.memset(Sst[:], 0.0), writes=[("Sst",)])
    S.add("pool", lambda e: e.memset(Sbf[:], 0.0), writes=[("Sbf",)])
    S.add("pool", lambda e: e.memset(histtmp[:], 0.0), writes=[("histtmp",)])

    P_F = OFF_F // 256
    P_V = OFF_V // 256
    P_Q = OFF_Q // 256
    P_OG = OFF_OG // 256
    P_CA = OFF_CA // 256
    P_CB = OFF_CB // 256
    P_GH = OFF_GH // 256
    P_GC = OFF_GC // 256

    def glu_chunks(ntok, c_list=range(8), hTb=None):
        hTb = hT if hTb is None else hTb
        S.add("pool", lambda e: e.tensor_copy(out=uc[:, :, 0:30], in_=histtmp[:]),
              reads=[("histtmp",)], writes=[("uc",)])
        for cp in range(4):
            if not any((2 * cp + ch) in c_list for ch in range(2)):
                continue
            wsa, wka = wload(wb_in[P_CA + cp], ("wb_in", P_CA + cp))
            wsb, wkb = wload(wb_in[P_CB + cp], ("wb_in", P_CB + cp))
            wva = wsa[:].rearrange("p (k c) -> p k c", c=256)
            wvb = wsb[:].rearrange("p (k c) -> p k c", c=256)
            for ch in range(2):
                c = 2 * cp + ch
                pa, pak = ps1()
                pbb, pbk = ps1()
                for kc in range(KC):
                    S.add("pe", lambda e, kc=kc, ch=ch, pa=pa, wva=wva: e.matmul(
                        pa[:, 0:ntok], lhsT=wva[:, kc, ch * 128:(ch + 1) * 128], rhs=hTb[:, kc, 0:ntok],
                        start=(kc == 0), stop=(kc == KC - 1)), reads=[wka, key_of(hTb)], writes=pak)
                for kc in range(KC):
                    S.add("pe", lambda e, kc=kc, ch=ch, pbb=pbb, wvb=wvb: e.matmul(
                        pbb[:, 0:ntok], lhsT=wvb[:, kc, ch * 128:(ch + 1) * 128], rhs=hTb[:, kc, 0:ntok],
                        start=(kc == 0), stop=(kc == KC - 1)), reads=[wkb, key_of(hTb)], writes=pbk)
                S.add("act", lambda e, pbb=pbb: e.activation(out=tmp1[:, 0:ntok], in_=pbb[:, 0:ntok], func=AF.Sigmoid),
                      reads=pbk, writes=[("sq0",)])
                S.add("dve", lambda e, c=c, pa=pa: e.tensor_tensor(out=uc[:, c, 30:30 + ntok], in0=pa[:, 0:ntok],
                                                                   in1=tmp1[:, 0:ntok], op=ALU.mult),
                      reads=pak + [("sq0",)], writes=[("uc",)])

    def shift_hist(ntok):
        S.add("pool", lambda e: e.tensor_copy(out=histtmp[:], in_=uc[:, :, ntok:ntok + 30]),
              reads=[("uc",)], writes=[("histtmp",)])

    if cfg.n_hist > 0:
        assert ARENA_KB >= 88 or NS < 4
        hTs = [hT, av("hT_B", 2 * u, [128, KC, T], BF16)]
        tmB0 = u
        sets = [
            dict(tma=tma, tml=tml, tmk=tmk, ktb=ktb, vb=vb, erg=erg, k="A"),
            dict(tma=av("tmaB", tmB0, [128, HW]), tml=av("tmlB", tmB0 + 4, [128, HW]),
                 tmk=av("tmkB", tmB0 + 8, [128, HW]), ktb=av("ktbB", tmB0 + 12, [128, HW], BF16),
                 vb=av("vbB", tmB0 + 14, [128, HW], BF16), erg=sb("ergB", [128, NH, 2]), k="B"),
        ]
        skeys = [
            dict(tma=[("tma", j) for j in range(4)], tml=[("tml",)], tmk=[("tmk",)], ktb=[("ktb",)],
                 vb=[("vb", j) for j in range(4)], erg=[("erg",)]),
            dict(tma=[("tmaB",)], tml=[("tmlB",)], tmk=[("tmkB",)], ktb=[("ktbB",)], vb=[("vbB",)],
                 erg=[("ergB",)]),
        ]
        res_w = []
        res_k = []
        hw_ev = None
        for n in range(8):
            if n < cfg.nslot:
                dst = wslots[n][:].rearrange("p (k c) -> p k c", c=256)
                kk = ("ws", n)
            else:
                dst = av(f"resw{n}", b0 + 16 + (n - cfg.nslot) * 8, [128, KC, 256], BF16)
                kk = (f"resw{n}",)
            p = (PF_ + n) if n < 4 else (PV_ + n - 4)
            hw_ev = S.add("pool", lambda e, dst=dst, p=p: e.dma_start(out=dst, in_=w_in_v[:, :, p * 256:(p + 1) * 256]),
                          writes=[kk], chan=("hw",))
            res_w.append(dst)
            res_k.append(kk)
        for kk in res_k:
            for pk_ in S._exp([kk]):
                S.lastw[pk_] = hw_ev

        def h_load(t):
            for i in range(NS):
                r0 = t * T + i * 128
                load_x(i, 128, xrows[r0:r0 + 128, :])

        def h_norm(t, hTbuf):
            load_g("g1")
            for i in range(NS):
                norm_to_hT(i, 128, None, hTbuf, i * 128)
            if t + 1 < cfg.n_hist:
                h_load(t + 1)

        def h_p1(g):
            t, i = divmod(g, NS)
            B, K_ = sets[g % 2], skeys[g % 2]
            hTb = hTs[t % 2]
            pf, pfk = ps2()
            for p4 in range(4):
                for kc in range(KC):
                    S.add("pe", lambda e, kc=kc, p4=p4, pf=pf, hTb=hTb, i=i: e.matmul(
                        pf[:, p4 * 256:(p4 + 1) * 256], lhsT=hTb[:, kc, i * 128:(i + 1) * 128], rhs=res_w[p4][:, kc, :],
                        start=(kc == 0), stop=(kc == KC - 1)),
                        reads=[res_k[p4], key_of(hTb)], writes=pfk)
            S.add("act", lambda e, pf=pf, B=B: e.activation(out=B["tma"][:, :], in_=pf[:, :], func=AF.Sigmoid),
                  reads=pfk, writes=K_["tma"])
            S.add("dve", lambda e, B=B: e.tensor_tensor(out=B["tma"][:, :], in0=B["tma"][:, :], in1=omlrow[:, :],
                                                        op=ALU.mult), reads=K_["tma"] + RC2, writes=K_["tma"])
            S.add("dve", lambda e, B=B: e.tensor_tensor(out=B["tma"][:, :], in0=B["tma"][:, :], in1=lbrow[:, :],
                                                        op=ALU.add), reads=K_["tma"] + RC2, writes=K_["tma"])
            S.add("act", lambda e, B=B: e.activation(out=B["tml"][:, :], in_=B["tma"][:, :], func=AF.Ln),
                  reads=K_["tma"], writes=K_["tml"])
            S.add("dve", lambda e, B=B: e.tensor_scalar(out=B["tmk"][:, :], in0=B["tma"][:, :], scalar1=-1.0,
                                                        scalar2=1.0, op0=ALU.mult, op1=ALU.add),
                  reads=K_["tma"], writes=K_["tmk"])

        def h_p2(g):
            t, i = divmod(g, NS)
            B, K_ = sets[g % 2], skeys[g % 2]
            hTb = hTs[t % 2]
            pv, pvk = ps2()
            for p4 in range(4):
                for kc in range(KC):
                    S.add("pe", lambda e, kc=kc, p4=p4, pv=pv, hTb=hTb, i=i: e.matmul(
                        pv[:, p4 * 256:(p4 + 1) * 256], lhsT=hTb[:, kc, i * 128:(i + 1) * 128],
                        rhs=res_w[4 + p4][:, kc, :], start=(kc == 0), stop=(kc == KC - 1)),
                        reads=[res_k[4 + p4], key_of(hTb)], writes=pvk)
            S.add("dve", lambda e, pv=pv, B=B: e.tensor_copy(out=B["vb"][:, :], in_=pv[:, :]),
                  reads=pvk, writes=K_["vb"])

        def h_c1(g):
            B, K_ = sets[g % 2], skeys[g % 2]
            pg, pgk = ps2()
            for half in range(2):
                S.add("pe", lambda e, half=half, pg=pg, B=B: e.matmul(
                    pg[:, half * 512:(half + 1) * 512], lhsT=consts[:, C_SU:C_SU + 128],
                    rhs=B["tml"][:, half * 512:(half + 1) * 512], start=True, stop=True),
                    reads=K_["tml"] + RC, writes=pgk)
            pr, prk = ps1()
            for h in range(NH):
                S.add("pe", lambda e, h=h, pr=pr, B=B: e.matmul(
                    pr[:, h * 2:h * 2 + 2], lhsT=B["tml"][:, h * 128:(h + 1) * 128],
                    rhs=consts[:, C_RG128:C_RG128 + 2], start=True, stop=True),
                    reads=K_["tml"] + RC, writes=prk)
            S.add("act", lambda e, pg=pg, B=B: e.activation(out=B["tma"][:, :], in_=pg[:, :], func=AF.Exp),
                  reads=pgk + K_["tma"], writes=K_["tma"])
            S.add("act", lambda e, pr=pr, B=B: e.activation(out=B["erg"][:].rearrange("p h c -> p (h c)"),
                                                            in_=pr[:, 0:16], func=AF.Exp),
                  reads=prk, writes=K_["erg"])
            S.add("dve", lambda e, B=B: e.tensor_tensor(out=B["ktb"][:, :], in0=B["tmk"][:, :], in1=B["tma"][:, :],
                                                        op=ALU.mult),
                  reads=K_["tma"] + K_["tmk"], writes=K_["ktb"])

        def h_c2(g):
            B, K_ = sets[g % 2], skeys[g % 2]
            pS, pSk = ps2()
            for h in range(NH):
                S.add("pe", lambda e, h=h, pS=pS, B=B: e.matmul(
                    pS[:, h * 128:(h + 1) * 128], lhsT=B["ktb"][:, h * 128:(h + 1) * 128],
                    rhs=B["vb"][:, h * 128:(h + 1) * 128], start=True, stop=True),
                    reads=K_["ktb"] + K_["vb"], writes=pSk)
            pSv = pS[:].rearrange("p (h v) -> p h v", v=128)
            for h in range(NH):
                S.add("dve", lambda e, h=h, pSv=pSv, B=B: e.scalar_tensor_tensor(
                    out=Sst[:, h, :], in0=Sst[:, h, :], scalar=B["erg"][:, h, 1:2], in1=pSv[:, h, :],
                    op0=ALU.mult, op1=ALU.add),
                    reads=pSk + K_["erg"] + [("Sst",)], writes=[("Sst",)])

        NG = cfg.n_hist * NS
        per_tile = (len(conv_q) + cfg.n_hist - 1) // max(cfg.n_hist, 1) + 1
        PS.hist = False
        h_load(0)
        h_norm(0, hTs[0])
        h_p1(0)
        h_p2(0)
        for g in range(NG):
            t, i = divmod(g, NS)
            if i == 2:
                emit_conv(per_tile)
            if i == 1 and t + 1 < cfg.n_hist:
                h_norm(t + 1, hTs[(t + 1) % 2])
            if g + 1 < NG:
                h_p1(g + 1)
            h_c1(g)
            if g + 1 < NG:
                h_p2(g + 1)
            h_c2(g)
        emit_conv(len(conv_q))
        PS.hist = False
        S.add("pool", lambda e: e.tensor_copy(out=Sbf[:], in_=Sst[:]), reads=[("Sst",)], writes=[("Sbf",)])
        glu_chunks(T, hTb=hTs[(cfg.n_hist - 1) % 2])
        shift_hist(T)
    else:
        emit_conv(len(conv_q))

    def full_tile(ntok, nsub, st, x_src_fn, y_dst_fn, emit_conv_out=None, emit_state_out=None):
        for i in range(nsub):
            load_x(i, st, x_src_fn(i))
            load_g("g1")
            norm_to_hT(i, st, None, hT, i * 128)
        for p in range(4):
            def cons_q(ch, pt, pk, p=p):
                h = 2 * p + ch
                S.add("act", lambda e: e.activation(out=qs[:, h, 0:ntok], in_=pt[:, 0:ntok], func=AF.Silu),
                      reads=pk, writes=[("qs",)])
            proj_fm(wb_in[P_Q + p], ("wb_in", P_Q + p), 2, hT, ntok, cons_q)
        for p in range(4):
            def cons_og(ch, pt, pk, p=p):
                h = 2 * p + ch
                S.add("act", lambda e: e.activation(out=sog[:, h, 0:ntok], in_=pt[:, 0:ntok], func=AF.Silu),
                      reads=pk, writes=[("sog",)])
            proj_fm(wb_in[P_OG + p], ("wb_in", P_OG + p), 2, hT, ntok, cons_og)
        for i in range(nsub):
            hgrn_tm(i, st, i * 128, P_F, P_V)
            hgrn_decay_cols(st)
            hgrn_fm(i, st, i * 128)
            hgrn_state_update(st)
        if emit_state_out is not None:
            emit_state_out()
        glu_chunks(ntok)
        def part1_chunks():
            for dp in range(4):
                wsh, wkh = wload(wb_ph[dp], ("wb_ph", dp))
                wvh = wsh[:].rearrange("p (k c) -> p k c", c=512)
                for half in range(2):
                    wsg, wkg = wload(wb_in[P_GH + dp * 2 + half], ("wb_in", P_GH + dp * 2 + half), live=(wkh,))
                    wvg = wsg[:].rearrange("p (k c) -> p k c", c=256)
                    for ch in range(2):
                        dch = dp * 4 + half * 2 + ch
                        c0 = (half * 2 + ch) * 128
                        pbh, pbhk = ps1()
                        pgh, pghk = ps1()
                        for c in range(8):
                            S.add("pe", lambda e, c=c, c0=c0, pbh=pbh, wvh=wvh: e.matmul(
                                pbh[:, 0:ntok], lhsT=wvh[:, c, c0:c0 + 128], rhs=onT[:, c, 0:ntok],
                                start=(c == 0), stop=(c == 7)), reads=[wkh, ("onT",)], writes=pbhk)
                        for kc in range(KC):
                            S.add("pe", lambda e, kc=kc, ch=ch, pgh=pgh, wvg=wvg: e.matmul(
                                pgh[:, 0:ntok], lhsT=wvg[:, kc, ch * 128:(ch + 1) * 128], rhs=hT[:, kc, 0:ntok],
                                start=(kc == 0), stop=(kc == KC - 1)), reads=[wkg, key_of(hT)], writes=pghk)
                        tg, tgk = (tmp3, ("tmp3",)) if dch % 2 == 0 else (tmp4, ("tmp4",))
                        S.add("act", lambda e, pgh=pgh, tg=tg: e.activation(out=tg[:, 0:ntok], in_=pgh[:, 0:ntok],
                                                                            func=AF.Sigmoid), reads=pghk, writes=[tgk])
                        yield (dch, pbh, pbhk, tg, tgk)

        def part1_evac(item):
            dch, pbh, pbhk, tg, tgk = item
            S.add("dve", lambda e: e.tensor_tensor(out=mT[:, dch, 0:ntok], in0=pbh[:, 0:ntok], in1=tg[:, 0:ntok],
                                                   op=ALU.mult),
                  reads=pbhk + [tgk], writes=[("mT", dch)])

        p1 = part1_chunks()
        pending = []
        ntap = 0
        for j in range(CK):
            for c in range(8):
                if j == 0:
                    S.add("dve", lambda e, c=c: e.tensor_scalar(out=dwa[:, c, 0:ntok], in0=uc[:, c, 0:ntok],
                                                                scalar1=dwk[:, c, 0:1], scalar2=dwb[:, c:c + 1],
                                                                op0=ALU.mult, op1=ALU.add),
                          reads=[("uc",)] + RC2, writes=[("dwa", c)])
                else:
                    S.add("dve", lambda e, c=c, j=j: e.scalar_tensor_tensor(
                        out=dwa[:, c, 0:ntok], in0=uc[:, c, j:j + ntok], scalar=dwk[:, c, j:j + 1],
                        in1=dwa[:, c, 0:ntok], op0=ALU.mult, op1=ALU.add),
                        reads=[("uc",), ("dwa", c)] + RC2, writes=[("dwa", c)])
                ntap += 1
                if ntap % 15 == 0:
                    while len(pending) < 2:
                        it = next(p1, None)
                        if it is None:
                            break
                        pending.append(it)
                    if pending and ntap >= 30:
                        part1_evac(pending.pop(0))
        for it in p1:
            pending.append(it)
            if len(pending) >= 2:
                part1_evac(pending.pop(0))
        for it in pending:
            part1_evac(it)
        if emit_conv_out is not None:
            emit_conv_out(ntok)
        shift_hist(ntok)
        DWA = [("dwa", c) for c in range(8)]
        pmu, pmuk = ps1()
        pm2, pm2k = ps1()
        for c in range(8):
            S.add("pe", lambda e, c=c: e.matmul(pmu[:, 0:ntok], lhsT=consts[:, C_OC:C_OC + 128], rhs=dwa[:, c, 0:ntok],
                                                start=(c == 0), stop=(c == 7)), reads=[("dwa", c)] + RC, writes=pmuk)
        for c in range(8):
            sq = sqt[0]
            sqk = ("sq0",)
            S.add("act", lambda e, c=c, sq=sq: e.activation(out=sq[:, 0:ntok], in_=dwa[:, c, 0:ntok], func=AF.Square),
                  reads=[("dwa", c)], writes=[sqk])
            S.add("pe", lambda e, c=c, sq=sq: e.matmul(pm2[:, 0:ntok], lhsT=consts[:, C_OC:C_OC + 128],
                                                       rhs=sq[:, 0:ntok], start=(c == 0), stop=(c == 7)),
                  reads=[sqk] + RC, writes=pm2k)
        S.add("act", lambda e: e.activation(out=mu[:, 0:ntok], in_=pmu[:, 0:ntok], func=AF.Copy),
              reads=pmuk, writes=[("mu",)])
        S.add("dve", lambda e: e.tensor_tensor(out=musq[:, 0:ntok], in0=mu[:, 0:ntok], in1=mu[:, 0:ntok], op=ALU.mult),
              reads=[("mu",)], writes=[("musq",)])
        S.add("dve", lambda e: e.tensor_tensor(out=musq[:, 0:ntok], in0=pm2[:, 0:ntok], in1=musq[:, 0:ntok],
                                               op=ALU.subtract),
              reads=pm2k + [("musq",)], writes=[("musq",)])
        S.add("act", lambda e: e.activation(out=rstd_c[:, 0:ntok], in_=musq[:, 0:ntok], func=AF.Ln,
                                            bias=epscol[:, 0:1]),
              reads=[("musq",)] + RC, writes=[("musq",)])
        S.add("act", lambda e: e.activation(out=rstd_c[:, 0:ntok], in_=rstd_c[:, 0:ntok], func=AF.Exp, scale=-0.5),
              reads=[("musq",)], writes=[("musq",)])
        for c in range(8):
            eng = "pool" if c % 2 else "dve"
            S.add(eng, lambda e, c=c: e.tensor_tensor(out=dwa[:, c, 0:ntok], in0=dwa[:, c, 0:ntok], in1=mu[:, 0:ntok],
                                                      op=ALU.subtract),
                  reads=[("dwa", c), ("mu",)], writes=[("dwa", c)])
            S.add(eng, lambda e, c=c: e.tensor_tensor(out=dwa[:, c, 0:ntok], in0=dwa[:, c, 0:ntok],
                                                      in1=rstd_c[:, 0:ntok], op=ALU.mult),
                  reads=[("dwa", c), ("musq",)], writes=[("dwa", c)])
            S.add("act", lambda e, c=c: e.activation(out=cT[:, c, 0:ntok], in_=dwa[:, c, 0:ntok], func=AF.Silu,
                                                     scale=lng[:, c:c + 1], bias=lnb[:, c:c + 1]),
                  reads=[("dwa", c)] + RC, writes=[("cT",)])
        for dp in range(4):
            wsc, wkc = wload(wb_pc[dp], ("wb_pc", dp))
            wvc = wsc[:].rearrange("p (k c) -> p k c", c=512)
            for half in range(2):
                wsq, wkq = wload(wb_in[P_GC + dp * 2 + half], ("wb_in", P_GC + dp * 2 + half), live=(wkc,))
                wvq = wsq[:].rearrange("p (k c) -> p k c", c=256)
                for ch in range(2):
                    dch = dp * 4 + half * 2 + ch
                    c0 = (half * 2 + ch) * 128
                    pbc, pbck = ps1()
                    pgc, pgck = ps1()
                    for c in range(8):
                        S.add("pe", lambda e, c=c, c0=c0, pbc=pbc, wvc=wvc: e.matmul(
                            pbc[:, 0:ntok], lhsT=wvc[:, c, c0:c0 + 128], rhs=cT[:, c, 0:ntok],
                            start=(c == 0), stop=(c == 7)), reads=[wkc, ("cT",)], writes=pbck)
                    for kc in range(KC):
                        S.add("pe", lambda e, kc=kc, ch=ch, pgc=pgc, wvq=wvq: e.matmul(
                            pgc[:, 0:ntok], lhsT=wvq[:, kc, ch * 128:(ch + 1) * 128], rhs=hT[:, kc, 0:ntok],
                            start=(kc == 0), stop=(kc == KC - 1)), reads=[wkq, key_of(hT)], writes=pgck)
                    tg, tgk = (tmp3, ("tmp3",)) if dch % 2 == 0 else (tmp4, ("tmp4",))
                    S.add("act", lambda e, pgc=pgc, tg=tg: e.activation(out=tg[:, 0:ntok], in_=pgc[:, 0:ntok],
                                                                        func=AF.Sigmoid), reads=pgck, writes=[tgk])
                    S.add("dve", lambda e, pbc=pbc, tg=tg: e.tensor_tensor(out=tg[:, 0:ntok], in0=pbc[:, 0:ntok],
                                                                           in1=tg[:, 0:ntok], op=ALU.mult),
                          reads=pbck + [tgk], writes=[tgk])
                    S.add("dve", lambda e, dch=dch, tg=tg: e.tensor_tensor(out=mT[:, dch, 0:ntok],
                                                                           in0=mT[:, dch, 0:ntok], in1=tg[:, 0:ntok],
                                                                           op=ALU.add),
                          reads=[tgk, ("mT", dch)], writes=[("mT", dch)])
        for dpc in range(8):
            ws, wk = wload(wb_out[dpc], ("wb_out", dpc))
            wv = ws[:].rearrange("p (k c) -> p k c", c=256)
            for i in range(nsub):
                pt, pk = ps1()
                for kc in range(KC):
                    S.add("pe", lambda e, kc=kc, i=i, pt=pt, wv=wv: e.matmul(
                        pt[:st, 0:256], lhsT=mT[:, kc, i * 128:i * 128 + st], rhs=wv[:, kc, :],
                        start=(kc == 0), stop=(kc == KC - 1)), reads=[wk, key_of(mT)], writes=pk)
                S.add("dve", lambda e, i=i, dpc=dpc, pt=pt: e.tensor_tensor(
                    out=xin[:st, i, dpc * 256:(dpc + 1) * 256], in0=xin[:st, i, dpc * 256:(dpc + 1) * 256],
                    in1=pt[:st, 0:256], op=ALU.add),
                    reads=pk + [("xin", i)], writes=[("xin", i)])
        dbg_dump("on", onT[:].rearrange("p h t -> p (h t)"), [("onT",)])
        dbg_dump("c", cT[:].rearrange("p h t -> p (h t)"), [("cT",)])
        dbg_dump("m", mT[:].rearrange("p h t -> p (h t)"), [key_of(mT)])
        dbg_dump("x1", xin[:, 0, :], [("xin", 0)])
        for i in range(nsub):
            load_g("g2")
            norm_to_hT(i, st, None, h2T, i * 128)
        for p in range(NP_FF1):
            def cons_ff(ch, pt, pk, p=p):
                fc = 2 * p + ch
                ft = ftmp[fc % 2]
                ftk = ("ftmp", fc % 2)
                S.add("act", lambda e: e.activation(out=ft[:, 0:ntok], in_=pt[:, 0:ntok], func=AF.Relu),
                      reads=pk, writes=[ftk])
                S.add("pool", lambda e: e.tensor_tensor(out=hfT[:, fc, 0:ntok], in0=ft[:, 0:ntok],
                                                        in1=ft[:, 0:ntok], op=ALU.mult),
                      reads=[ftk], writes=[("hfT", fc)])
            proj_fm(wb_ff1[p], ("wb_ff1", p), 2, h2T, ntok, cons_ff)
        for dpc in range(8):
            pts = [ps1() for _ in range(nsub)]
            for fb in range(4):
                ws, wk = wload(wb_ff2[fb * 8 + dpc], ("wb_ff2", fb * 8 + dpc))
                wv = ws[:].rearrange("p (k c) -> p k c", c=256)
                for fc in range(16):
                    for i in range(nsub):
                        pt, pk = pts[i]
                        S.add("pe", lambda e, fb=fb, fc=fc, i=i, pt=pt, wv=wv: e.matmul(
                            pt[:st, 0:256], lhsT=hfT[:, fb * 16 + fc, i * 128:i * 128 + st], rhs=wv[:, fc, :],
                            start=(fb == 0 and fc == 0), stop=(fb == 3 and fc == 15)),
                            reads=[wk, ("hfT", fb * 16 + fc)], writes=pk)
            for i in range(nsub):
                pt, pk = pts[i]
                S.add("dve", lambda e, i=i, dpc=dpc, pt=pt: e.tensor_tensor(
                    out=xin[:st, i, dpc * 256:(dpc + 1) * 256], in0=xin[:st, i, dpc * 256:(dpc + 1) * 256],
                    in1=pt[:st, 0:256], op=ALU.add),
                    reads=pk + [("xin", i)], writes=[("xin", i)])
        dbg_dump("x2", xin[:, 0, :], [("xin", 0)])
        load_g("gf")
        for i in range(nsub):
            xk = ("xin", i)
            fb_ = 8 + 3 * (i % 2)
            S.add("pool", lambda e, fb_=fb_: e.memset(stat[:, fb_:fb_ + 1], 0.0), writes=[("stat", fb_)])
            S.add("act", lambda e, i=i, fb_=fb_: e.activation(out=hbufs[i % 2][:st, :], in_=xin[:st, i, :],
                                                              func=AF.Square, accum_out=stat[:st, fb_:fb_ + 1]),
                  reads=[xk, ("stat", fb_)], writes=[("hbuf", 0), ("stat", fb_)])
            S.add("act", lambda e, fb_=fb_: e.activation(out=stat[:st, fb_ + 1:fb_ + 2], in_=stat[:st, fb_:fb_ + 1],
                                                         func=AF.Ln, scale=1.0 / D, bias=epscol[:st, 0:1]),
                  reads=[("stat", fb_)] + RC, writes=[("stat", fb_ + 1)])
            S.add("act", lambda e, fb_=fb_: e.activation(out=stat[:st, fb_ + 2:fb_ + 3], in_=stat[:st, fb_ + 1:fb_ + 2],
                                                         func=AF.Exp, scale=-0.5),
                  reads=[("stat", fb_ + 1)], writes=[("stat", fb_ + 2)])
            S.add("dve", lambda e, i=i, fb_=fb_: e.scalar_tensor_tensor(out=xin[:st, i, :], in0=xin[:st, i, :],
                                                                        scalar=stat[:st, fb_ + 2:fb_ + 3],
                                                                        in1=grow[:st, :], op0=ALU.mult, op1=ALU.mult),
                  reads=[xk, ("stat", fb_ + 2), ("grow",)] + RC, writes=[xk])
            dst = y_dst_fn(i)
            S.add("sp", lambda e, i=i, dst=dst: e.dma_start(out=dst, in_=xin[:st, i, :]),
                  reads=[xk], writes=[("yout", len(S.ops["sp"]))], chan=("x", i))

    def conv_out_emitter(dst_ap):
        def f(ntok):
            pp, ppk = ps2()
            for c in range(8):
                S.add("pe", lambda e, c=c, pp=pp: e.matmul(
                    pp[0:30, c * 128:(c + 1) * 128], lhsT=uc[:, c, ntok:ntok + 30], rhs=ident_f,
                    start=True, stop=True), reads=[("uc",)] + RC, writes=ppk)
            S.add("act", lambda e, pp=pp: e.activation(out=cbuf[0:30, :], in_=pp[0:30, :], func=AF.Copy),
                  reads=ppk, writes=[("tmk",)])
            S.add("sp", lambda e: e.dma_start(out=dst_ap, in_=cbuf[0:30, :]),
                  reads=[("tmk",)], writes=[("cout", id(dst_ap))], chan=("cb",))
        return f

    def state_out_emitter(dst_ap):
        def f():
            S.add("sp", lambda e: e.dma_start(out=dst_ap.rearrange("h d v -> d h v"), in_=Sst[:]),
                  reads=[("Sst",)], writes=[("sout", id(dst_ap))], chan=("so",))
        return f

    for t in range(cfg.n_main):
        base = (cfg.n_hist + t) * T
        last = (t == cfg.n_main - 1)
        full_tile(T, NS, 128,
                  lambda i, base=base: xrows[base + i * 128:base + (i + 1) * 128, :],
                  lambda i, t=t: y_main[t * T + i * 128:t * T + (i + 1) * 128, :],
                  emit_conv_out=conv_out_emitter(cp_out[:, :]) if last else None,
                  emit_state_out=state_out_emitter(sp_out) if last else None)

    if cfg.sample:
        S.add("sp", lambda e: e.dma_start(out=Sst[:], in_=s0_in.rearrange("h d v -> d h v")),
              reads=[], writes=[("Sst",)], chan=("so",))
        S.add("pool", lambda e: e.tensor_copy(out=Sbf[:], in_=Sst[:]), reads=[("Sst",)], writes=[("Sbf",)])
        S.add("sp", lambda e: e.dma_start(out=cbuf[0:30, :], in_=c0_in[:, :]), reads=[], writes=[("tmk",)],
              chan=("cb",))
        for c in range(8):
            pt, pk = ps1()
            S.add("pe", lambda e, c=c, pt=pt: e.matmul(pt[:, 0:30], lhsT=cbuf[0:30, c * 128:(c + 1) * 128],
                                                        rhs=consts[0:30, C_ID:C_ID + 30], start=True, stop=True),
                  reads=[("tmk",)] + RC, writes=pk)
            S.add("dve", lambda e, c=c, pt=pt: e.tensor_copy(out=histtmp[:, c, :], in_=pt[:, 0:30]),
                  reads=pk, writes=[("histtmp",)])
        full_tile(DEC_SEQ, 1, DEC_SEQ,
                  lambda i: xs_in[:, :],
                  lambda i: y_s[:, :],
                  emit_conv_out=conv_out_emitter(cs_out[:, :]),
                  emit_state_out=state_out_emitter(ss_out))

    outkeys = [k for k in S.lastw if k[0] in ("yout", "cout", "sout")]
    S.add("sp", lambda e: e.nop(), reads=outkeys)

    S.finalize()
    sems = {e: es.enter_context(nc.semaphore(f"sem_{e}")) for e in Sched.ENG if e != "sp"}
    chan_sems = {}
    for ch in S.chan_count:
        chan_sems[ch] = es.enter_context(nc.semaphore("c_" + "_".join(str(x) for x in ch)))
    with nc.Block() as block:
        @block.tensor
        def _(e):
            S.emit("pe", e, sems, chan_sems)

        @block.scalar
        def _(e):
            S.emit("act", e, sems, chan_sems)

        @block.vector
        def _(e):
            S.emit("dve", e, sems, chan_sems)

        @block.gpsimd
        def _(e):
            S.emit("pool", e, sems, chan_sems)

        @block.sync
        def _(e):
            S.emit("sp", e, sems, chan_sems)
    es.close()
    n_ins = {e: len(S.ops[e]) for e in Sched.ENG}
    return nc, n_ins


HIST_ROWS = 12800
NS_TILE = 4


def core_inputs(cfg, xrows, xs, s0, c0, shared):
    m = dict(shared)
    m["xrows"] = np.ascontiguousarray(xrows, dtype=np.float32)
    m["xs"] = np.ascontiguousarray(xs, dtype=np.float32)
    m["s0"] = np.ascontiguousarray(s0, dtype=np.float32)
    m["c0"] = np.ascontiguousarray(c0, dtype=np.float32)
    return m


def shared_inputs(norm1_g, w_in, lb_logits, hgrn_norm_g, w_proj_h, dw_kernel, dw_bias, conv_ln_g, conv_ln_b,
                  w_proj_c, w_out, norm2_g, w_ff1, w_ff2, final_norm_g):
    f = lambda a: np.ascontiguousarray(np.asarray(a), dtype=np.float32)
    return {
        "w_in": f(w_in[0]), "w_proj_h": f(w_proj_h[0]), "w_proj_c": f(w_proj_c[0]), "w_out": f(w_out[0]),
        "w_ff1": f(w_ff1[0]), "w_ff2": f(w_ff2[0]),
        "norm1_g": f(norm1_g[0]).reshape(1, D), "norm2_g": f(norm2_g[0]).reshape(1, D),
        "final_norm_g": f(final_norm_g).reshape(1, D), "lb_logits": f(lb_logits),
        "hgrn_norm_g": f(hgrn_norm_g[0]).reshape(128, 1), "dw_kernel": f(dw_kernel[0]),
        "dw_bias": f(dw_bias[0]).reshape(CW, 1), "conv_ln_g": f(conv_ln_g[0]).reshape(CW, 1),
        "conv_ln_b": f(conv_ln_b[0]).reshape(CW, 1), "consts": make_consts(),
    }


def kernel(x_prompt, x_sample, state_hgrn, state_conv, meta_tokens, norm1_g, w_in, lb_logits,
           hgrn_norm_g, w_proj_h, dw_kernel, dw_bias, conv_ln_g, conv_ln_b, w_proj_c, w_out,
           norm2_g, w_ff1, w_ff2, final_norm_g):
    x_prompt = np.asarray(x_prompt, dtype=np.float32)
    x_sample = np.asarray(x_sample, dtype=np.float32)
    state_hgrn = np.asarray(state_hgrn, dtype=np.float32)
    state_conv = np.asarray(state_conv, dtype=np.float32)
    meta = np.asarray(meta_tokens, dtype=np.float32)
    T = NS_TILE * 128
    cfg = Cfg(n_hist=HIST_ROWS // T, n_main=CHUNK_TOK // T, ns=NS_TILE)
    shared = shared_inputs(norm1_g, w_in, lb_logits, hgrn_norm_g, w_proj_h, dw_kernel, dw_bias, conv_ln_g,
                           conv_ln_b, w_proj_c, w_out, norm2_g, w_ff1, w_ff2, final_norm_g)
    in_maps = []
    for c in range(8):
        s, j = c // 4, c % 4
        start = j * CHUNK_TOK
        rows = np.zeros((HIST_ROWS + CHUNK_TOK, D), np.float32)
        rows[HIST_ROWS:] = x_prompt[s, start:start + CHUNK_TOK]
        lo = start - HIST_ROWS
        if lo >= 0:
            rows[:HIST_ROWS] = x_prompt[s, lo:start]
        else:
            nx = start
            if nx > 0:
                rows[HIST_ROWS - nx:HIST_ROWS] = x_prompt[s, 0:start]
            rows[HIST_ROWS - nx - N_META:HIST_ROWS - nx] = meta
        in_maps.append(core_inputs(cfg, rows, x_sample[c], state_hgrn[0, c], state_conv[0, c], shared))
    nc, _ = build_program(cfg)
    res = run_bass_kernel_spmd(nc, in_maps, core_ids=list(range(8)))
    R = res.results
    y_prompt = np.stack([np.concatenate([R[s * 4 + j]["y_main"] for j in range(4)], axis=0) for s in range(2)], 0)
    y_sample = np.stack([R[c]["y_s"] for c in range(8)], 0)
    new_hp = np.stack([R[3]["s_p"], R[7]["s_p"]], 0)[None]
    new_cp = np.stack([R[3]["c_p"], R[7]["c_p"]], 0)[None]
    new_hs = np.stack([R[c]["s_s"] for c in range(8)], 0)[None]
    new_cs = np.stack([R[c]["c_s"] for c in range(8)], 0)[None]
    return (y_prompt.astype(np.float32), y_sample.astype(np.float32), new_hp.astype(np.float32),
            new_cp.astype(np.float32), new_hs.astype(np.float32), new_cs.astype(np.float32))
```

```python
import contextlib
import numpy as np
import ml_dtypes
import concourse.bass as bass
import concourse.mybir as mybir
from concourse.bass_utils import run_bass_kernel_spmd

F32 = mybir.dt.float32
BF16 = mybir.dt.bfloat16
AF = mybir.ActivationFunctionType
ALU = mybir.AluOpType

D = 2048
NH = 8
HW = 1024
CW = 1024
CK = 31
DFF = 8192
NIN = 10240
EPS = 1e-6
N_META = 16
SEQ = 16384
CHUNK_TOK = 4096
DEC_SEQ = 64
KC = D // 128

OFF_Q, OFF_F, OFF_V, OFF_OG, OFF_CA, OFF_CB, OFF_GH, OFF_GC = 0, 1024, 2048, 3072, 4096, 5120, 6144, 8192

C_ID = 0
C_SU = 128
C_M128 = 256
C_M64 = 384
C_RG128 = 512
C_RG64 = 514
C_O128 = 516
C_OC = 644
C_MASK = 772
NCONST = C_MASK + 8 * 128


def make_consts():
    c = np.zeros((128, NCONST), np.float32)
    s = np.arange(128)[:, None]
    t = np.arange(128)[None, :]
    c[:, C_ID:C_ID + 128] = (s == t)
    c[:, C_SU:C_SU + 128] = (s > t)
    c[:, C_M128:C_M128 + 128] = (s <= t).astype(np.float32) - (s <= 63).astype(np.float32)
    c[:, C_M64:C_M64 + 128] = (s <= t).astype(np.float32) - (s <= 31).astype(np.float32)
    c[:, C_RG128] = (np.arange(128) <= 63)
    c[:, C_RG128 + 1] = 1.0
    c[:, C_RG64] = (np.arange(128) <= 31)
    c[:, C_RG64 + 1] = 1.0
    c[:, C_O128:C_O128 + 128] = 1.0 / 128.0
    c[:, C_OC:C_OC + 128] = 1.0 / 1024.0
    for h in range(8):
        c[:, C_MASK + h * 128:C_MASK + (h + 1) * 128] = (s <= t)
    return c


class Sched:
    ENG = ("pe", "act", "dve", "pool", "sp")

    def __init__(self):
        self.ops = {e: [] for e in self.ENG}
        self.lastw = {}
        self.readers = {}
        self.chan_count = {}
        self.alias = {}

    def _exp(self, keys):
        out = []
        for k in keys:
            a = self.alias.get(k)
            if a is None:
                out.append(k)
            else:
                out.extend(a)
        return out

    def add(self, eng, fn, reads=(), writes=(), chan=None):
        reads = self._exp(reads)
        writes = self._exp(writes)
        deps = []
        for k in reads:
            ev = self.lastw.get(k)
            if ev is not None:
                deps.append(ev)
        for k in writes:
            ev = self.lastw.get(k)
            if ev is not None:
                deps.append(ev)
            r = self.readers.get(k)
            if r:
                deps.extend((kk[0], kk[1], v) for kk, v in r.items())
        if chan is None:
            ev = ("E", eng, len(self.ops[eng]))
        else:
            c = self.chan_count.get(chan, 0)
            self.chan_count[chan] = c + 1
            ev = ("D", chan, c)
        self.ops[eng].append({"fn": fn, "deps": deps, "ev": ev, "chan": chan, "sig": False, "eng": eng})
        for k in reads:
            r = self.readers.setdefault(k, {})
            kk = (ev[0], ev[1])
            if r.get(kk, -1) < ev[2]:
                r[kk] = ev[2]
        for k in writes:
            self.lastw[k] = ev
            self.readers[k] = {}
        return ev

    def finalize(self):
        for e in self.ENG:
            known = {}
            for op in self.ops[e]:
                need = {}
                for (kind, ident, idx) in op["deps"]:
                    if kind == "E" and ident == e and e in ("pe", "sp"):
                        continue
                    kk = (kind, ident)
                    if known.get(kk, -1) >= idx:
                        continue
                    if need.get(kk, -1) < idx:
                        need[kk] = idx
                for kk, idx in need.items():
                    known[kk] = idx
                    if kk[0] == "E":
                        self.ops[kk[1]][idx]["sig"] = True
                op["need"] = need
        self.cnt = {}
        for e in self.ENG:
            c = 0
            arr = []
            for op in self.ops[e]:
                if op["sig"] and op["chan"] is None:
                    c += 1
                arr.append(c)
            self.cnt[e] = arr

    def emit(self, e, engine, sems, chan_sems):
        for op in self.ops[e]:
            for (kind, ident), idx in op["need"].items():
                if kind == "E":
                    engine.wait_ge(sems[ident], self.cnt[ident][idx])
                else:
                    engine.wait_ge(chan_sems[ident], 16 * (idx + 1))
            ins = op["fn"](engine)
            if op["chan"] is not None:
                ins.then_inc(chan_sems[op["chan"]], 16)
            elif op["sig"]:
                ins.then_inc(sems[e], 1)


class Cfg:
    def __init__(self, n_hist, n_main, ns=1, nslot=5, sample=True):
        self.n_hist = n_hist
        self.n_main = n_main
        self.ns = ns
        self.T = ns * 128
        self.nslot = nslot
        self.sample = sample
        self.rows = (n_hist + n_main) * self.T


def build_program(cfg):
    nc = bass.Bass("TRN2", target_bir_lowering=False)
    S = Sched()
    T = cfg.T
    NS = cfg.ns
    es = contextlib.ExitStack()

    def din(name, shape, dt=F32):
        return nc.dram_tensor(name, list(shape), dt, kind="ExternalInput").ap()

    def dout(name, shape, dt=F32):
        return nc.dram_tensor(name, list(shape), dt, kind="ExternalOutput").ap()

    xrows = din("xrows", [cfg.rows, D])
    xs_in = din("xs", [DEC_SEQ, D])
    s0_in = din("s0", [NH, 128, 128])
    c0_in = din("c0", [CK - 1, CW])
    w_in = din("w_in", [D, NIN])
    w_ph = din("w_proj_h", [HW, D])
    w_pc = din("w_proj_c", [CW, D])
    w_out = din("w_out", [D, D])
    w_ff1 = din("w_ff1", [D, DFF])
    w_ff2 = din("w_ff2", [DFF, D])
    g1_in = din("norm1_g", [1, D])
    g2_in = din("norm2_g", [1, D])
    gf_in = din("final_norm_g", [1, D])
    lbl_in = din("lb_logits", [2, HW])
    hg_in = din("hgrn_norm_g", [128, 1])
    dwk_in = din("dw_kernel", [CK, CW])
    dwb_in = din("dw_bias", [CW, 1])
    lng_in = din("conv_ln_g", [CW, 1])
    lnb_in = din("conv_ln_b", [CW, 1])
    consts_in = din("consts", [128, NCONST])

    y_main = dout("y_main", [cfg.n_main * T, D])
    y_s = dout("y_s", [DEC_SEQ, D])
    sp_out = dout("s_p", [NH, 128, 128])
    cp_out = dout("c_p", [CK - 1, CW])
    ss_out = dout("s_s", [NH, 128, 128])
    cs_out = dout("c_s", [CK - 1, CW])

    dbg = None
    if getattr(cfg, "debug", False):
        dbg = {"x1": dout("dbg_x1", [128, D]), "m": dout("dbg_m", [128, KC * T], BF16),
               "on": dout("dbg_on", [128, NH * T], BF16), "c": dout("dbg_c", [128, 8 * T], BF16),
               "x2": dout("dbg_x2", [128, D])}
    dbg_done = set()

    def dbg_dump(name, src_ap, rkeys):
        if dbg is None or name in dbg_done:
            return
        dbg_done.add(name)
        S.add("sp", lambda e: e.dma_start(out=dbg[name][:, :], in_=src_ap), reads=rkeys,
              writes=[("yout", "dbg" + name)], chan=("dbg", name))

    NP_IN = NIN // 256
    NP_PH = D // 512
    NP_OUT = D // 256
    NP_FF1 = DFF // 256
    NP_FF2 = 4 * (D // 256)
    wb_in = nc.dram_tensor("wb_in", [NP_IN, 128, 4096], BF16).ap()
    wb_ph = nc.dram_tensor("wb_ph", [NP_PH, 128, 4096], BF16).ap()
    wb_pc = nc.dram_tensor("wb_pc", [NP_PH, 128, 4096], BF16).ap()
    wb_out = nc.dram_tensor("wb_out", [NP_OUT, 128, 4096], BF16).ap()
    wb_ff1 = nc.dram_tensor("wb_ff1", [NP_FF1, 128, 4096], BF16).ap()
    wb_ff2 = nc.dram_tensor("wb_ff2", [NP_FF2, 128, 4096], BF16).ap()

    def sb(name, shape, dt=F32):
        return es.enter_context(nc.sbuf_tensor("s_" + name, list(shape), dt))

    consts = sb("consts", [128, NCONST])
    constb = sb("constb", [128, 128], BF16)
    grow = sb("grow", [128, D])
    lbrow = sb("lbrow", [128, HW])
    omlrow = sb("omlrow", [128, HW])
    hgcol = sb("hgcol", [128, 1])
    epscol = sb("epscol", [128, 1])
    dwk = sb("dwk", [128, 8, CK])
    dwb = sb("dwb", [128, 8])
    lng = sb("lng", [128, 8])
    lnb = sb("lnb", [128, 8])
    xin = sb("xin", [128, NS, D])
    hbuf0_ = sb("hbuf0", [128, D], BF16)
    hbufs = [hbuf0_, hbuf0_]
    stat = sb("stat", [128, 16])
    rgs = sb("rgs", [128, NH, 2])
    erg = sb("erg", [128, NH, 2])
    Sst = sb("Sst", [128, NH, 128])
    Sbf = sb("Sbf", [128, NH, 128], BF16)
    histtmp = sb("histtmp", [128, 8, 30])
    ftmp = [sb(f"ftmp{j}", [128, T]) for j in range(2)]

    u = T / 32.0
    ARENA_KB = int(np.ceil(max(3 * u + 36 + u / 2, 5 * u, 4.5 * u + 2)))
    arena = sb("arena", [128, ARENA_KB * 256])
    names = {}

    def regrange(key, lo, hi):
        S.alias[key] = [("pg", p) for p in range(int(lo // 1024), int((hi + 1023) // 1024))]

    def av(name, off_kb, shape, dt=F32):
        esz = 4 if dt == F32 else 2
        nel = int(np.prod(shape[1:]))
        lo = int(round(off_kb * 1024))
        assert lo % 32 == 0 and lo + nel * esz <= ARENA_KB * 1024, (name, lo, nel * esz)
        v = arena[:, lo // 4:(lo + nel * esz + 3) // 4]
        if dt == BF16:
            v = v.bitcast(BF16)
        if len(shape) == 3:
            v = v.rearrange("p (a b) -> p a b", b=shape[2])
        regrange((name,), lo, lo + nel * esz)
        names[id(v)] = name
        v_off[name] = lo
        return v
    v_off = {}

    hT = av("hT", 0, [128, KC, T], BF16)
    h2T = av("h2T", 0, [128, KC, T], BF16)
    qs = av("qs", u, [128, NH, T])
    sog = av("sog", 2 * u, [128, NH, T])
    b0 = 3 * u
    tma = av("tma", b0, [128, HW])
    tml = av("tml", b0 + 4, [128, HW])
    tmk = av("tmk", b0 + 8, [128, HW])
    ktb = av("ktb", b0 + 12, [128, HW], BF16)
    vb = av("vb", b0 + 14, [128, HW], BF16)
    e1 = av("e1", b0 + 16, [128, NH, 128])
    e2 = av("e2", b0 + 20, [128, NH, 128])
    qp = av("qp", b0 + 24, [128, NH, 128], BF16)
    kp = av("kp", b0 + 26, [128, NH, 128], BF16)
    qh = av("qh", b0 + 28, [128, NH, 128], BF16)
    stb = av("stb", b0 + 30, [128, NH, 128], BF16)
    onT = av("onT", b0 + 36, [128, NH, T], BF16)
    for j in range(4):
        regrange(("tma", j), v_off["tma"] + j * 1024, v_off["tma"] + (j + 1) * 1024)
        regrange(("vb", j), v_off["vb"] + j * 512, v_off["vb"] + (j + 1) * 512)
    for h in range(NH):
        S.alias[("qh", h)] = S.alias[("qh",)]
    lbtmp = tma
    dwkT = tml
    cbuf = tmk
    osq = e1
    rstd_o = e2
    uc = av("uc", u, [128, 8, 30 + T])
    dwa = av("dwa", 2 * u + 1, [128, 8, T])
    for c in range(8):
        regrange(("dwa", c), v_off["dwa"] + c * T * 4, v_off["dwa"] + (c + 1) * T * 4)
    sqt = [av("sq0", 3 * u + 1, [128, T])]
    cT = av("cT", 3 * u + 1 + u / 8, [128, 8, T], BF16)
    d0 = 3 * u + 1 + u / 8 + u / 2
    mu = av("mu", d0, [128, T])
    musq = av("musq", d0 + u / 8, [128, T])
    rstd_c = musq
    tmp1 = sqt[0]
    tmp3 = av("tmp3", d0 + 2 * u / 8, [128, T])
    tmp4 = av("tmp4", d0 + 3 * u / 8, [128, T])
    mT = av("mT", d0 + 4 * u / 8, [128, KC, T], BF16)
    assert d0 + 4 * u / 8 + u <= b0 + 36 or NS < 4
    for dch_ in range(KC):
        regrange(("mT", dch_), v_off["mT"] + dch_ * T * 2, v_off["mT"] + (dch_ + 1) * T * 2)
    hfT = av("hfT", u, [128, DFF // 128, T], BF16)
    for fc in range(DFF // 128):
        regrange(("hfT", fc), v_off["hfT"] + fc * T * 2, v_off["hfT"] + (fc + 1) * T * 2)
    wslots = [sb(f"ws{i}", [128, 4096], BF16) for i in range(cfg.nslot)]

    psum = [es.enter_context(nc.psum_tensor(f"ps{i}", [128, 1024], F32)) for i in range(4)]

    class PS:
        n1 = 0
        n2 = 0
        hist = False

    def ps1():
        if PS.hist:
            i = 6 + PS.n1 % 2
        else:
            i = PS.n1 % 8
        PS.n1 += 1
        return psum[i // 2][:, (i % 2) * 512:(i % 2) * 512 + 512], [("ps", i)]

    def ps2():
        i = PS.n2 % (3 if PS.hist else 4)
        PS.n2 += 1
        return psum[i], [("ps", 2 * i), ("ps", 2 * i + 1)]

    class WS:
        n = 0

    def wload(src_ap, src_key, live=()):
        s = WS.n % cfg.nslot
        while ("ws", s) in live:
            WS.n += 1
            s = WS.n % cfg.nslot
        WS.n += 1
        S.add("sp", lambda e, s=s, src_ap=src_ap: e.dma_start(out=wslots[s][:], in_=src_ap),
              reads=[src_key], writes=[("ws", s)], chan=("ws", s))
        return wslots[s], ("ws", s)

    conv_q = []

    def convert(name, wb, pieces, src_fn, chan):
        pieces = list(pieces)
        for n, p in enumerate(pieces):
            conv_q.append((name, wb, p, src_fn(p), chan, pieces if n == len(pieces) - 1 else None))

    def emit_conv(n):
        for _ in range(n):
            if not conv_q:
                return
            name, wb, p, src, chan, fin = conv_q.pop(0)
            ev = S.add("pool", lambda e, p=p, src=src, wb=wb: e.dma_start(
                out=wb[p].rearrange("q (a b) -> q a b", b=src.shape[-1]), in_=src),
                writes=[(name, p)], chan=("cv", chan))
            if fin is not None:
                for pp in fin:
                    S.lastw[(name, pp)] = ev

    w_in_v = w_in.rearrange("(kc p) n -> p kc n", p=128)
    w_ph_v = w_ph.rearrange("(kc p) n -> p kc n", p=128)
    w_pc_v = w_pc.rearrange("(kc p) n -> p kc n", p=128)
    w_out_v = w_out.rearrange("(kc p) n -> p kc n", p=128)
    w_ff1_v = w_ff1.rearrange("(kc p) n -> p kc n", p=128)
    w_ff2_v = w_ff2.rearrange("(fb fc p) n -> p fb fc n", p=128, fc=16)

    def cload(dst, src, eng="sp", slow=False):
        return S.add(eng, lambda e: e.dma_start(out=dst, in_=src, allow_slow_non_contiguous=slow),
                     writes=[], chan=("const",))

    cevs = []
    cevs.append(cload(consts[:], consts_in[:, :]))
    cevs.append(cload(lbrow[:], lbl_in[0:1, :].to_broadcast([128, HW])))
    cevs.append(cload(lbtmp[:], lbl_in[1:2, :].to_broadcast([128, HW])))
    cevs.append(cload(hgcol[:], hg_in[:, :]))
    cevs.append(cload(dwkT[0:CK, :], dwk_in[:, :]))
    cevs.append(cload(dwb[:], dwb_in.rearrange("(c p) o -> p (c o)", p=128), slow=True))
    cevs.append(cload(lng[:], lng_in.rearrange("(c p) o -> p (c o)", p=128), slow=True))
    cevs.append(cload(lnb[:], lnb_in.rearrange("(c p) o -> p (c o)", p=128), slow=True))
    CK_ALL = ("constk",)
    S.lastw[CK_ALL] = cevs[-1]
    RC = [CK_ALL]

    in_src = lambda p: w_in_v[:, :, p * 256:(p + 1) * 256]
    PF_, PV_, PQ_, POG_, PCA_, PCB_, PGH_, PGC_ = (o // 256 for o in (OFF_F, OFF_V, OFF_Q, OFF_OG, OFF_CA, OFF_CB,
                                                                      OFF_GH, OFF_GC))
    convert("wb_in", wb_in, list(range(PCA_, PCA_ + 4)) + list(range(PCB_, PCB_ + 4)), in_src, "in_c")
    convert("wb_in", wb_in, list(range(PQ_, PQ_ + 4)) + list(range(POG_, POG_ + 4)), in_src, "in_q")
    convert("wb_in", wb_in, list(range(PF_, PF_ + 4)) + list(range(PV_, PV_ + 4)), in_src, "in_fv")
    convert("wb_in", wb_in, list(range(PGH_, PGH_ + 8)) + list(range(PGC_, PGC_ + 8)), in_src, "in_g")
    convert("wb_ph", wb_ph, range(NP_PH), lambda p: w_ph_v[:, :, p * 512:(p + 1) * 512], "ph")
    convert("wb_pc", wb_pc, range(NP_PH), lambda p: w_pc_v[:, :, p * 512:(p + 1) * 512], "pc")
    convert("wb_out", wb_out, range(NP_OUT), lambda p: w_out_v[:, :, p * 256:(p + 1) * 256], "out")
    convert("wb_ff1", wb_ff1, range(0, 16), lambda p: w_ff1_v[:, :, p * 256:(p + 1) * 256], "ff1a")
    convert("wb_ff1", wb_ff1, range(16, 32), lambda p: w_ff1_v[:, :, p * 256:(p + 1) * 256], "ff1b")
    ff2_src = lambda p: w_ff2_v[:, p // 8, :, (p % 8) * 256:(p % 8) * 256 + 256]
    convert("wb_ff2", wb_ff2, [fb * 8 + d for d in range(0, 4) for fb in range(4)], ff2_src, "ff2a")
    convert("wb_ff2", wb_ff2, [fb * 8 + d for d in range(4, 8) for fb in range(4)], ff2_src, "ff2b")

    S.add("dve", lambda e: e.tensor_tensor(out=lbtmp[:], in0=lbrow[:], in1=lbtmp[:], op=ALU.subtract),
          reads=RC, writes=[("tma", 0), ("tma", 1), ("tma", 2), ("tma", 3)])
    S.add("act", lambda e: e.activation(out=lbrow[:], in_=lbtmp[:], func=AF.Sigmoid),
          reads=[("tma", 0), ("tma", 1), ("tma", 2), ("tma", 3)], writes=[("lbrow",)])
    S.add("dve", lambda e: e.tensor_scalar(out=omlrow[:], in0=lbrow[:], scalar1=-1.0, scalar2=1.0,
                                           op0=ALU.mult, op1=ALU.add),
          reads=[("lbrow",)], writes=[("omlrow",)])
    S.add("pool", lambda e: e.memset(epscol[:], EPS), writes=[("epscol",)])
    S.add("dve", lambda e: e.tensor_copy(out=constb[:], in_=consts[:, C_ID:C_ID + 128]),
          reads=RC, writes=[("constb",)])
    RC = RC + [("epscol",)]
    RC2 = RC + [("lbrow",), ("omlrow",), ("constb",)]
    for c in range(8):
        pt, pk = ps1()
        S.add("pe", lambda e, c=c, pt=pt: e.matmul(pt[:, 0:CK], lhsT=dwkT[0:CK, c * 128:(c + 1) * 128],
                                                    rhs=consts[0:CK, C_ID:C_ID + CK], start=True, stop=True),
              reads=RC + [("tml",)], writes=pk)
        S.add("dve", lambda e, c=c, pt=pt: e.tensor_copy(out=dwk[:, c, :], in_=pt[:, 0:CK]),
              reads=pk, writes=[("dwk",)])
    RC2 = RC2 + [("dwk",)]

    ident_f = consts[:, C_ID:C_ID + 128]

    def key_of(t):
        return (names[id(t)],)

    class GR:
        cur = None

    def load_g(which):
        if GR.cur == which:
            return
        GR.cur = which
        src = {"g1": g1_in, "g2": g2_in, "gf": gf_in}[which]
        S.add("sp", lambda e: e.dma_start(out=grow[:], in_=src[0:1, :].to_broadcast([128, D])),
              writes=[("grow",)], chan=("grow",))

    def norm_to_hT(i, st, g_row, dstT, tok0):
        xk = ("xin", i)
        par = i % 2
        hbuf = hbufs[par]
        junk = hbuf
        hk = ("hbuf", 0)
        c0_, c1_, c2_ = 3 * par, 3 * par + 1, 3 * par + 2
        S.add("pool", lambda e: e.memset(stat[:, c0_:c0_ + 1], 0.0), writes=[("stat", c0_)])
        S.add("act", lambda e: e.activation(out=junk[:st, :], in_=xin[:st, i, :], func=AF.Square,
                                            accum_out=stat[:st, c0_:c0_ + 1]),
              reads=[xk, ("stat", c0_)], writes=[hk, ("stat", c0_)])
        S.add("act", lambda e: e.activation(out=stat[:st, c1_:c1_ + 1], in_=stat[:st, c0_:c0_ + 1], func=AF.Ln,
                                            scale=1.0 / D, bias=epscol[:st, 0:1]),
              reads=[("stat", c0_)] + RC, writes=[("stat", c1_)])
        S.add("act", lambda e: e.activation(out=stat[:st, c2_:c2_ + 1], in_=stat[:st, c1_:c1_ + 1], func=AF.Exp,
                                            scale=-0.5),
              reads=[("stat", c1_)], writes=[("stat", c2_)])
        S.add("dve", lambda e: e.scalar_tensor_tensor(out=hbuf[:st, :], in0=xin[:st, i, :], scalar=stat[:st, c2_:c2_ + 1],
                                                      in1=grow[:st, :], op0=ALU.mult, op1=ALU.mult),
              reads=[xk, ("stat", c2_), ("grow",)] + RC, writes=[hk])
        for half in range(2):
            pt, pk = ps1()
            ptb = pt.bitcast(BF16)
            for j in range(8):
                kc = half * 8 + j
                S.add("pe", lambda e, kc=kc, j=j, ptb=ptb: e.transpose(
                    ptb[:, j * 128:j * 128 + st], hbuf[:st, kc * 128:(kc + 1) * 128], constb[:st, :st]),
                    reads=[hk] + RC2, writes=pk)
            eng = "act" if half == 0 else "dve"
            src = ptb.rearrange("p (j t) -> p j t", t=128)[:, :, 0:st]
            dst = dstT[:, half * 8:(half + 1) * 8, tok0:tok0 + st]
            if eng == "act":
                S.add("act", lambda e, src=src, dst=dst: e.activation(out=dst, in_=src, func=AF.Copy),
                      reads=pk, writes=[key_of(dstT)])
            else:
                S.add("dve", lambda e, src=src, dst=dst: e.tensor_copy(out=dst, in_=src),
                      reads=pk, writes=[key_of(dstT)])

    def proj_fm(piece_src, piece_key, col_chunks, rhsT, ntok, consume):
        ws, wk = wload(piece_src, piece_key)
        wv = ws[:].rearrange("p (k c) -> p k c", c=256)
        for ch in range(col_chunks):
            pt, pk = ps1()
            for kc in range(KC):
                S.add("pe", lambda e, kc=kc, ch=ch, pt=pt, wv=wv: e.matmul(
                    pt[:, 0:ntok], lhsT=wv[:, kc, ch * 128:(ch + 1) * 128], rhs=rhsT[:, kc, 0:ntok],
                    start=(kc == 0), stop=(kc == KC - 1)),
                    reads=[wk, key_of(rhsT)], writes=pk)
            consume(ch, pt, pk)

    def hgrn_tm(i, st, tok0, piece0_f, piece0_v):
        for p4 in range(4):
            ws, wk = wload(wb_in[piece0_f + p4], ("wb_in", piece0_f + p4))
            wv = ws[:].rearrange("p (k c) -> p k c", c=256)
            pt, pk = ps1()
            for kc in range(KC):
                S.add("pe", lambda e, kc=kc, pt=pt, wv=wv: e.matmul(
                    pt[:st, 0:256], lhsT=hT[:, kc, tok0:tok0 + st], rhs=wv[:, kc, :],
                    start=(kc == 0), stop=(kc == KC - 1)),
                    reads=[wk, key_of(hT)], writes=pk)
            S.add("act", lambda e, pt=pt, p4=p4: e.activation(out=tma[:st, p4 * 256:(p4 + 1) * 256],
                                                               in_=pt[:st, 0:256], func=AF.Sigmoid),
                  reads=pk, writes=[("tma", p4)])
        for p4 in range(4):
            ws, wk = wload(wb_in[piece0_v + p4], ("wb_in", piece0_v + p4))
            wv = ws[:].rearrange("p (k c) -> p k c", c=256)
            pt, pk = ps1()
            for kc in range(KC):
                S.add("pe", lambda e, kc=kc, pt=pt, wv=wv: e.matmul(
                    pt[:st, 0:256], lhsT=hT[:, kc, tok0:tok0 + st], rhs=wv[:, kc, :],
                    start=(kc == 0), stop=(kc == KC - 1)),
                    reads=[wk, key_of(hT)], writes=pk)
            S.add("dve", lambda e, pt=pt, p4=p4: e.tensor_copy(out=vb[:st, p4 * 256:(p4 + 1) * 256],
                                                                in_=pt[:st, 0:256]),
                  reads=pk, writes=[("vb", p4)])
        TMA = [("tma", j) for j in range(4)]
        S.add("dve", lambda e: e.tensor_tensor(out=tma[:st, :], in0=tma[:st, :], in1=omlrow[:st, :], op=ALU.mult),
              reads=TMA + RC2, writes=TMA)
        S.add("dve", lambda e: e.tensor_tensor(out=tma[:st, :], in0=tma[:st, :], in1=lbrow[:st, :], op=ALU.add),
              reads=TMA + RC2, writes=TMA)
        S.add("act", lambda e: e.activation(out=tml[:st, :], in_=tma[:st, :], func=AF.Ln),
              reads=TMA, writes=[("tml",)])
        S.add("dve", lambda e: e.tensor_scalar(out=tmk[:st, :], in0=tma[:st, :], scalar1=-1.0, scalar2=1.0,
                                               op0=ALU.mult, op1=ALU.add),
              reads=TMA, writes=[("tmk",)])
        pp, ppk = ps2()
        for half in range(2):
            S.add("pe", lambda e, half=half, pp=pp: e.matmul(
                pp[:st, half * 512:(half + 1) * 512], lhsT=consts[:st, C_SU:C_SU + st],
                rhs=tml[:st, half * 512:(half + 1) * 512], start=True, stop=True),
                reads=[("tml",)] + RC, writes=ppk)
        S.add("act", lambda e, pp=pp: e.activation(out=tma[:st, :], in_=pp[:st, :], func=AF.Exp),
              reads=ppk + TMA, writes=TMA)
        S.add("dve", lambda e: e.tensor_tensor(out=ktb[:st, :], in0=tmk[:st, :], in1=tma[:st, :], op=ALU.mult),
              reads=TMA + [("tmk",)], writes=[("ktb",)])

    def hgrn_state_update(st):
        pp, ppk = ps2()
        for h in range(NH):
            S.add("pe", lambda e, h=h, pp=pp: e.matmul(
                pp[:, h * 128:(h + 1) * 128], lhsT=ktb[:st, h * 128:(h + 1) * 128],
                rhs=vb[:st, h * 128:(h + 1) * 128], start=True, stop=True),
                reads=[("ktb",)] + [("vb", j) for j in range(4)], writes=ppk)
        ppv = pp[:].rearrange("p (h v) -> p h v", v=128)
        for h in range(NH):
            S.add("dve", lambda e, h=h, ppv=ppv: e.scalar_tensor_tensor(
                out=Sst[:, h, :], in0=Sst[:, h, :], scalar=erg[:, h, 1:2], in1=ppv[:, h, :],
                op0=ALU.mult, op1=ALU.add),
                reads=ppk + [("erg",), ("Sst",)], writes=[("Sst",)])
        S.add("act", lambda e: e.activation(out=Sbf[:], in_=Sst[:], func=AF.Copy), reads=[("Sst",)],
              writes=[("Sbf",)])

    def hgrn_decay_cols(st):
        crg = C_RG128 if st == 128 else C_RG64
        pt, pk = ps1()
        for h in range(NH):
            S.add("pe", lambda e, h=h, pt=pt: e.matmul(
                pt[:, h * 2:h * 2 + 2], lhsT=tml[:st, h * 128:(h + 1) * 128], rhs=consts[:st, crg:crg + 2],
                start=True, stop=True),
                reads=[("tml",)] + RC, writes=pk)
        S.add("act", lambda e, pt=pt: e.activation(out=erg[:].rearrange("p h c -> p (h c)"), in_=pt[:, 0:16],
                                                    func=AF.Exp),
              reads=pk, writes=[("erg",)])

    def hgrn_fm(i, st, tok0):
        cm = C_M128 if st == 128 else C_M64
        pb, pbk = ps2()
        pkT, pkTk = ps2()
        for h in range(NH):
            S.add("pe", lambda e, h=h, pb=pb: e.matmul(
                pb[:, h * 128:h * 128 + st], lhsT=tml[:st, h * 128:(h + 1) * 128], rhs=consts[:st, cm:cm + st],
                start=True, stop=True),
                reads=[("tml",)] + RC, writes=pbk)
        for h in range(NH):
            S.add("pe", lambda e, h=h, pkT=pkT: e.matmul(
                pkT[:, h * 128:h * 128 + st], lhsT=tmk[:st, h * 128:(h + 1) * 128], rhs=consts[:st, C_ID:C_ID + st],
                start=True, stop=True),
                reads=[("tmk",)] + RC, writes=pkTk)
        pbv = pb[:].rearrange("p (h t) -> p h t", t=128)[:, :, 0:st]
        pkTv = pkT[:].rearrange("p (h t) -> p h t", t=128)[:, :, 0:st]
        S.add("act", lambda e: e.activation(out=e1[:, :, 0:st], in_=pbv, func=AF.Exp),
              reads=pbk, writes=[("e1",)])
        S.add("act", lambda e: e.activation(out=e2[:, :, 0:st], in_=pbv, func=AF.Exp, scale=-1.0),
              reads=pbk, writes=[("e2",)])
        S.add("dve", lambda e: e.tensor_tensor(out=qp[:, :, 0:st], in0=qs[:, :, tok0:tok0 + st], in1=e1[:, :, 0:st],
                                               op=ALU.mult),
              reads=[("qs",), ("e1",)], writes=[("qp",)])
        S.add("dve", lambda e: e.tensor_tensor(out=kp[:, :, 0:st], in0=pkTv, in1=e2[:, :, 0:st], op=ALU.mult),
              reads=pkTk + [("e2",)], writes=[("kp",)])
        for h in range(NH):
            eng = "dve"
            S.add(eng, lambda e, h=h: e.scalar_tensor_tensor(
                out=qh[:, h, 0:st], in0=qs[:, h, tok0:tok0 + st], scalar=erg[:, h, 0:1], in1=e1[:, h, 0:st],
                op0=ALU.mult, op1=ALU.mult),
                reads=[("qs",), ("e1",), ("erg",)], writes=[("qh", h)])
        psc, psck = ps2()
        for h in range(NH):
            S.add("pe", lambda e, h=h, psc=psc: e.matmul(
                psc[:st, h * 128:h * 128 + st], lhsT=kp[:, h, 0:st], rhs=qp[:, h, 0:st], start=True, stop=True),
                reads=[("kp",), ("qp",)], writes=psck)
        pscv = psc[:].rearrange("p (h t) -> p h t", t=128)[:st, :, 0:st]
        maskv = consts[:, C_MASK:C_MASK + 1024].rearrange("p (h t) -> p h t", t=128)[:st, :, 0:st]
        S.add("dve", lambda e: e.tensor_tensor(out=stb[:st, :, 0:st], in0=pscv, in1=maskv, op=ALU.mult),
              reads=psck + RC, writes=[("stb",)])
        po, pok = ps2()
        for h in range(NH):
            S.add("pe", lambda e, h=h, po=po: e.matmul(
                po[:, h * 128:h * 128 + st], lhsT=vb[:st, h * 128:(h + 1) * 128], rhs=stb[:st, h, 0:st],
                start=True, stop=False),
                reads=[("stb",)] + [("vb", j) for j in range(4)], writes=pok)
            S.add("pe", lambda e, h=h, po=po: e.matmul(
                po[:, h * 128:h * 128 + st], lhsT=Sbf[:, h, :], rhs=qh[:, h, 0:st], start=False, stop=True),
                reads=[("Sbf",), ("qh", h)], writes=pok)
        pov = po[:].rearrange("p (h t) -> p h t", t=128)[:, :, 0:st]
        S.add("act", lambda e: e.activation(out=osq[:, :, 0:st], in_=pov, func=AF.Square),
              reads=pok, writes=[("e1",)])
        pm, pmk = ps2()
        for h2 in range(2):
            for hh in range(4):
                h = h2 * 4 + hh
                S.add("pe", lambda e, h=h, pm=pm: e.matmul(
                    pm[:, h * 128:h * 128 + st], lhsT=consts[:, C_O128:C_O128 + 128], rhs=osq[:, h, 0:st],
                    start=True, stop=True),
                    reads=[("e1",)] + RC, writes=pmk)
        pmv = pm[:].rearrange("p (h t) -> p h t", t=128)[:, :, 0:st]
        S.add("act", lambda e: e.activation(out=rstd_o[:, :, 0:st], in_=pmv, func=AF.Ln, bias=epscol[:, 0:1]),
              reads=pmk + RC, writes=[("e2",)])
        S.add("act", lambda e: e.activation(out=rstd_o[:, :, 0:st], in_=rstd_o[:, :, 0:st], func=AF.Exp, scale=-0.5),
              reads=[("e2",)], writes=[("e2",)])
        S.add("dve", lambda e: e.tensor_tensor(out=osq[:, :, 0:st], in0=pov, in1=rstd_o[:, :, 0:st], op=ALU.mult),
              reads=pok + [("e2",), ("e1",)], writes=[("e1",)])
        S.add("dve", lambda e: e.scalar_tensor_tensor(
            out=onT[:, :, tok0:tok0 + st], in0=osq[:, :, 0:st], scalar=hgcol[:, 0:1], in1=sog[:, :, tok0:tok0 + st],
            op0=ALU.mult, op1=ALU.mult),
            reads=[("e1",), ("sog",)] + RC, writes=[("onT",)])

    def load_x(i, st, src_rows):
        S.add("sp", lambda e: e.dma_start(out=xin[:st, i, :], in_=src_rows), writes=[("xin", i)], chan=("x", i))

    S.add("pool", lambda e: e.memset(Sst[:], 0.0), writes=[("Sst",)])
    S.add("pool", lambda e: e.memset(Sbf[:], 0.0), writes=[("Sbf",)])
    S.add("pool", lambda e: e.memset(histtmp[:], 0.0), writes=[("histtmp",)])

    P_F = OFF_F // 256
    P_V = OFF_V // 256
    P_Q = OFF_Q // 256
    P_OG = OFF_OG // 256
    P_CA = OFF_CA // 256
    P_CB = OFF_CB // 256
    P_GH = OFF_GH // 256
    P_GC = OFF_GC // 256

    def glu_chunks(ntok, c_list=range(8), hTb=None):
        hTb = hT if hTb is None else hTb
        S.add("pool", lambda e: e.tensor_copy(out=uc[:, :, 0:30], in_=histtmp[:]),
              reads=[("histtmp",)], writes=[("uc",)])
        for cp in range(4):
            if not any((2 * cp + ch) in c_list for ch in range(2)):
                continue
            wsa, wka = wload(wb_in[P_CA + cp], ("wb_in", P_CA + cp))
            wsb, wkb = wload(wb_in[P_CB + cp], ("wb_in", P_CB + cp))
            wva = wsa[:].rearrange("p (k c) -> p k c", c=256)
            wvb = wsb[:].rearrange("p (k c) -> p k c", c=256)
            for ch in range(2):
                c = 2 * cp + ch
                pa, pak = ps1()
                pbb, pbk = ps1()
                for kc in range(KC):
                    S.add("pe", lambda e, kc=kc, ch=ch, pa=pa, wva=wva: e.matmul(
                        pa[:, 0:ntok], lhsT=wva[:, kc, ch * 128:(ch + 1) * 128], rhs=hTb[:, kc, 0:ntok],
                        start=(kc == 0), stop=(kc == KC - 1)), reads=[wka, key_of(hTb)], writes=pak)
                for kc in range(KC):
                    S.add("pe", lambda e, kc=kc, ch=ch, pbb=pbb, wvb=wvb: e.matmul(
                        pbb[:, 0:ntok], lhsT=wvb[:, kc, ch * 128:(ch + 1) * 128], rhs=hTb[:, kc, 0:ntok],
                        start=(kc == 0), stop=(kc == KC - 1)), reads=[wkb, key_of(hTb)], writes=pbk)
                S.add("act", lambda e, pbb=pbb: e.activation(out=tmp1[:, 0:ntok], in_=pbb[:, 0:ntok], func=AF.Sigmoid),
                      reads=pbk, writes=[("sq0",)])
                S.add("dve", lambda e, c=c, pa=pa: e.tensor_tensor(out=uc[:, c, 30:30 + ntok], in0=pa[:, 0:ntok],
                                                                   in1=tmp1[:, 0:ntok], op=ALU.mult),
                      reads=pak + [("sq0",)], writes=[("uc",)])

    def shift_hist(ntok):
        S.add("pool", lambda e: e.tensor_copy(out=histtmp[:], in_=uc[:, :, ntok:ntok + 30]),
              reads=[("uc",)], writes=[("histtmp",)])

    if cfg.n_hist > 0:
        assert ARENA_KB >= 88 or NS < 4
        hTs = [hT, av("hT_B", 2 * u, [128, KC, T], BF16)]
        tmB0 = u
        sets = [
            dict(tma=tma, tml=tml, tmk=tmk, ktb=ktb, vb=vb, erg=erg, k="A"),
            dict(tma=av("tmaB", tmB0, [128, HW]), tml=av("tmlB", tmB0 + 4, [128, HW]),
                 tmk=av("tmkB", tmB0 + 8, [128, HW]), ktb=av("ktbB", tmB0 + 12, [128, HW], BF16),
                 vb=av("vbB", tmB0 + 14, [128, HW], BF16), erg=sb("ergB", [128, NH, 2]), k="B"),
        ]
        skeys = [
            dict(tma=[("tma", j) for j in range(4)], tml=[("tml",)], tmk=[("tmk",)], ktb=[("ktb",)],
                 vb=[("vb", j) for j in range(4)], erg=[("erg",)]),
            dict(tma=[("tmaB",)], tml=[("tmlB",)], tmk=[("tmkB",)], ktb=[("ktbB",)], vb=[("vbB",)],
                 erg=[("ergB",)]),
        ]
        res_w = []
        res_k = []
        hw_ev = None
        for n in range(8):
            if n < cfg.nslot:
                dst = wslots[n][:].rearrange("p (k c) -> p k c", c=256)
                kk = ("ws", n)
            else:
                dst = av(f"resw{n}", b0 + 16 + (n - cfg.nslot) * 8, [128, KC, 256], BF16)
                kk = (f"resw{n}",)
            p = (PF_ + n) if n < 4 else (PV_ + n - 4)
            hw_ev = S.add("pool", lambda e, dst=dst, p=p: e.dma_start(out=dst, in_=w_in_v[:, :, p * 256:(p + 1) * 256]),
                          writes=[kk], chan=("hw",))
            res_w.append(dst)
            res_k.append(kk)
        for kk in res_k:
            for pk_ in S._exp([kk]):
                S.lastw[pk_] = hw_ev

        def h_load(t):
            for i in range(NS):
                r0 = t * T + i * 128
                load_x(i, 128, xrows[r0:r0 + 128, :])

        def h_norm(t, hTbuf):
            load_g("g1")
            for i in range(NS):
                norm_to_hT(i, 128, None, hTbuf, i * 128)
            if t + 1 < cfg.n_hist:
                h_load(t + 1)

        def h_p1(g):
            t, i = divmod(g, NS)
            B, K_ = sets[g % 2], skeys[g % 2]
            hTb = hTs[t % 2]
            pf, pfk = ps2()
            for p4 in range(4):
                for kc in range(KC):
                    S.add("pe", lambda e, kc=kc, p4=p4, pf=pf, hTb=hTb, i=i: e.matmul(
                        pf[:, p4 * 256:(p4 + 1) * 256], lhsT=hTb[:, kc, i * 128:(i + 1) * 128], rhs=res_w[p4][:, kc, :],
                        start=(kc == 0), stop=(kc == KC - 1)),
                        reads=[res_k[p4], key_of(hTb)], writes=pfk)
            S.add("act", lambda e, pf=pf, B=B: e.activation(out=B["tma"][:, :], in_=pf[:, :], func=AF.Sigmoid),
                  reads=pfk, writes=K_["tma"])
            S.add("dve", lambda e, B=B: e.tensor_tensor(out=B["tma"][:, :], in0=B["tma"][:, :], in1=omlrow[:, :],
                                                        op=ALU.mult), reads=K_["tma"] + RC2, writes=K_["tma"])
            S.add("dve", lambda e, B=B: e.tensor_tensor(out=B["tma"][:, :], in0=B["tma"][:, :], in1=lbrow[:, :],
                                                        op=ALU.add), reads=K_["tma"] + RC2, writes=K_["tma"])
            S.add("act", lambda e, B=B: e.activation(out=B["tml"][:, :], in_=B["tma"][:, :], func=AF.Ln),
                  reads=K_["tma"], writes=K_["tml"])
            S.add("dve", lambda e, B=B: e.tensor_scalar(out=B["tmk"][:, :], in0=B["tma"][:, :], scalar1=-1.0,
                                                        scalar2=1.0, op0=ALU.mult, op1=ALU.add),
                  reads=K_["tma"], writes=K_["tmk"])

        def h_p2(g):
            t, i = divmod(g, NS)
            B, K_ = sets[g % 2], skeys[g % 2]
            hTb = hTs[t % 2]
            pv, pvk = ps2()
            for p4 in range(4):
                for kc in range(KC):
                    S.add("pe", lambda e, kc=kc, p4=p4, pv=pv, hTb=hTb, i=i: e.matmul(
                        pv[:, p4 * 256:(p4 + 1) * 256], lhsT=hTb[:, kc, i * 128:(i + 1) * 128],
                        rhs=res_w[4 + p4][:, kc, :], start=(kc == 0), stop=(kc == KC - 1)),
                        reads=[res_k[4 + p4], key_of(hTb)], writes=pvk)
            S.add("dve", lambda e, pv=pv, B=B: e.tensor_copy(out=B["vb"][:, :], in_=pv[:, :]),
                  reads=pvk, writes=K_["vb"])

        def h_c1(g):
            B, K_ = sets[g % 2], skeys[g % 2]
            pg, pgk = ps2()
            for half in range(2):
                S.add("pe", lambda e, half=half, pg=pg, B=B: e.matmul(
                    pg[:, half * 512:(half + 1) * 512], lhsT=consts[:, C_SU:C_SU + 128],
                    rhs=B["tml"][:, half * 512:(half + 1) * 512], start=True, stop=True),
                    reads=K_["tml"] + RC, writes=pgk)
            pr, prk = ps1()
            for h in range(NH):
                S.add("pe", lambda e, h=h, pr=pr, B=B: e.matmul(
                    pr[:, h * 2:h * 2 + 2], lhsT=B["tml"][:, h * 128:(h + 1) * 128],
                    rhs=consts[:, C_RG128:C_RG128 + 2], start=True, stop=True),
                    reads=K_["tml"] + RC, writes=prk)
            S.add("act", lambda e, pg=pg, B=B: e.activation(out=B["tma"][:, :], in_=pg[:, :], func=AF.Exp),
                  reads=pgk + K_["tma"], writes=K_["tma"])
            S.add("act", lambda e, pr=pr, B=B: e.activation(out=B["erg"][:].rearrange("p h c -> p (h c)"),
                                                            in_=pr[:, 0:16], func=AF.Exp),
                  reads=prk, writes=K_["erg"])
            S.add("dve", lambda e, B=B: e.tensor_tensor(out=B["ktb"][:, :], in0=B["tmk"][:, :], in1=B["tma"][:, :],
                                                        op=ALU.mult),
                  reads=K_["tma"] + K_["tmk"], writes=K_["ktb"])

        def h_c2(g):
            B, K_ = sets[g % 2], skeys[g % 2]
            pS, pSk = ps2()
            for h in range(NH):
                S.add("pe", lambda e, h=h, pS=pS, B=B: e.matmul(
                    pS[:, h * 128:(h + 1) * 128], lhsT=B["ktb"][:, h * 128:(h + 1) * 128],
                    rhs=B["vb"][:, h * 128:(h + 1) * 128], start=True, stop=True),
                    reads=K_["ktb"] + K_["vb"], writes=pSk)
            pSv = pS[:].rearrange("p (h v) -> p h v", v=128)
            for h in range(NH):
                S.add("dve", lambda e, h=h, pSv=pSv, B=B: e.scalar_tensor_tensor(
                    out=Sst[:, h, :], in0=Sst[:, h, :], scalar=B["erg"][:, h, 1:2], in1=pSv[:, h, :],
                    op0=ALU.mult, op1=ALU.add),
                    reads=pSk + K_["erg"] + [("Sst",)], writes=[("Sst",)])

        NG = cfg.n_hist * NS
        per_tile = (len(conv_q) + cfg.n_hist - 1) // max(cfg.n_hist, 1) + 1
        PS.hist = False
        h_load(0)
        h_norm(0, hTs[0])
        h_p1(0)
        h_p2(0)
        for g in range(NG):
            t, i = divmod(g, NS)
            if i == 2:
                emit_conv(per_tile)
            if i == 1 and t + 1 < cfg.n_hist:
                h_norm(t + 1, hTs[(t + 1) % 2])
            if g + 1 < NG:
                h_p1(g + 1)
            h_c1(g)
            if g + 1 < NG:
                h_p2(g + 1)
            h_c2(g)
        emit_conv(len(conv_q))
        PS.hist = False
        S.add("pool", lambda e: e.tensor_copy(out=Sbf[:], in_=Sst[:]), reads=[("Sst",)], writes=[("Sbf",)])
        glu_chunks(T, hTb=hTs[(cfg.n_hist - 1) % 2])
        shift_hist(T)
    else:
        emit_conv(len(conv_q))

    def full_tile(ntok, nsub, st, x_src_fn, y_dst_fn, emit_conv_out=None, emit_state_out=None):
        for i in range(nsub):
            load_x(i, st, x_src_fn(i))
            load_g("g1")
            norm_to_hT(i, st, None, hT, i * 128)
        for p in range(4):
            def cons_q(ch, pt, pk, p=p):
                h = 2 * p + ch
                S.add("act", lambda e: e.activation(out=qs[:, h, 0:ntok], in_=pt[:, 0:ntok], func=AF.Silu),
                      reads=pk, writes=[("qs",)])
            proj_fm(wb_in[P_Q + p], ("wb_in", P_Q + p), 2, hT, ntok, cons_q)
        for p in range(4):
            def cons_og(ch, pt, pk, p=p):
                h = 2 * p + ch
                S.add("act", lambda e: e.activation(out=sog[:, h, 0:ntok], in_=pt[:, 0:ntok], func=AF.Silu),
                      reads=pk, writes=[("sog",)])
            proj_fm(wb_in[P_OG + p], ("wb_in", P_OG + p), 2, hT, ntok, cons_og)
        for i in range(nsub):
            hgrn_tm(i, st, i * 128, P_F, P_V)
            hgrn_decay_cols(st)
            hgrn_fm(i, st, i * 128)
            hgrn_state_update(st)
        if emit_state_out is not None:
            emit_state_out()
        glu_chunks(ntok)
        def part1_chunks():
            for dp in range(4):
                wsh, wkh = wload(wb_ph[dp], ("wb_ph", dp))
                wvh = wsh[:].rearrange("p (k c) -> p k c", c=512)
                for half in range(2):
                    wsg, wkg = wload(wb_in[P_GH + dp * 2 + half], ("wb_in", P_GH + dp * 2 + half), live=(wkh,))
                    wvg = wsg[:].rearrange("p (k c) -> p k c", c=256)
                    for ch in range(2):
                        dch = dp * 4 + half * 2 + ch
                        c0 = (half * 2 + ch) * 128
                        pbh, pbhk = ps1()
                        pgh, pghk = ps1()
                        for c in range(8):
                            S.add("pe", lambda e, c=c, c0=c0, pbh=pbh, wvh=wvh: e.matmul(
                                pbh[:, 0:ntok], lhsT=wvh[:, c, c0:c0 + 128], rhs=onT[:, c, 0:ntok],
                                start=(c == 0), stop=(c == 7)), reads=[wkh, ("onT",)], writes=pbhk)
                        for kc in range(KC):
                            S.add("pe", lambda e, kc=kc, ch=ch, pgh=pgh, wvg=wvg: e.matmul(
                                pgh[:, 0:ntok], lhsT=wvg[:, kc, ch * 128:(ch + 1) * 128], rhs=hT[:, kc, 0:ntok],
                                start=(kc == 0), stop=(kc == KC - 1)), reads=[wkg, key_of(hT)], writes=pghk)
                        tg, tgk = (tmp3, ("tmp3",)) if dch % 2 == 0 else (tmp4, ("tmp4",))
                        S.add("act", lambda e, pgh=pgh, tg=tg: e.activation(out=tg[:, 0:ntok], in_=pgh[:, 0:ntok],
                                                                            func=AF.Sigmoid), reads=pghk, writes=[tgk])
                        yield (dch, pbh, pbhk, tg, tgk)

        def part1_evac(item):
            dch, pbh, pbhk, tg, tgk = item
            S.add("dve", lambda e: e.tensor_tensor(out=mT[:, dch, 0:ntok], in0=pbh[:, 0:ntok], in1=tg[:, 0:ntok],
                                                   op=ALU.mult),
                  reads=pbhk + [tgk], writes=[("mT", dch)])

        p1 = part1_chunks()
        pending = []
        ntap = 0
        for j in range(CK):
            for c in range(8):
                if j == 0:
                    S.add("dve", lambda e, c=c: e.tensor_scalar(out=dwa[:, c, 0:ntok], in0=uc[:, c, 0:ntok],
                                                                scalar1=dwk[:, c, 0:1], scalar2=dwb[:, c:c + 1],
                                                                op0=ALU.mult, op1=ALU.add),
                          reads=[("uc",)] + RC2, writes=[("dwa", c)])
                else:
                    S.add("dve", lambda e, c=c, j=j: e.scalar_tensor_tensor(
                        out=dwa[:, c, 0:ntok], in0=uc[:, c, j:j + ntok], scalar=dwk[:, c, j:j + 1],
                        in1=dwa[:, c, 0:ntok], op0=ALU.mult, op1=ALU.add),
                        reads=[("uc",), ("dwa", c)] + RC2, writes=[("dwa", c)])
                ntap += 1
                if ntap % 15 == 0:
                    while len(pending) < 2:
                        it = next(p1, None)
                        if it is None:
                            break
                        pending.append(it)
                    if pending and ntap >= 30:
                        part1_evac(pending.pop(0))
        for it in p1:
            pending.append(it)
            if len(pending) >= 2:
                part1_evac(pending.pop(0))
        for it in pending:
            part1_evac(it)
        if emit_conv_out is not None:
            emit_conv_out(ntok)
        shift_hist(ntok)
        DWA = [("dwa", c) for c in range(8)]
        pmu, pmuk = ps1()
        pm2, pm2k = ps1()
        for c in range(8):
            S.add("pe", lambda e, c=c: e.matmul(pmu[:, 0:ntok], lhsT=consts[:, C_OC:C_OC + 128], rhs=dwa[:, c, 0:ntok],
                                                start=(c == 0), stop=(c == 7)), reads=[("dwa", c)] + RC, writes=pmuk)
        for c in range(8):
            sq = sqt[0]
            sqk = ("sq0",)
            S.add("act", lambda e, c=c, sq=sq: e.activation(out=sq[:, 0:ntok], in_=dwa[:, c, 0:ntok], func=AF.Square),
                  reads=[("dwa", c)], writes=[sqk])
            S.add("pe", lambda e, c=c, sq=sq: e.matmul(pm2[:, 0:ntok], lhsT=consts[:, C_OC:C_OC + 128],
                                                       rhs=sq[:, 0:ntok], start=(c == 0), stop=(c == 7)),
                  reads=[sqk] + RC, writes=pm2k)
        S.add("act", lambda e: e.activation(out=mu[:, 0:ntok], in_=pmu[:, 0:ntok], func=AF.Copy),
              reads=pmuk, writes=[("mu",)])
        S.add("dve", lambda e: e.tensor_tensor(out=musq[:, 0:ntok], in0=mu[:, 0:ntok], in1=mu[:, 0:ntok], op=ALU.mult),
              reads=[("mu",)], writes=[("musq",)])
        S.add("dve", lambda e: e.tensor_tensor(out=musq[:, 0:ntok], in0=pm2[:, 0:ntok], in1=musq[:, 0:ntok],
                                               op=ALU.subtract),
              reads=pm2k + [("musq",)], writes=[("musq",)])
        S.add("act", lambda e: e.activation(out=rstd_c[:, 0:ntok], in_=musq[:, 0:ntok], func=AF.Ln,
                                            bias=epscol[:, 0:1]),
              reads=[("musq",)] + RC, writes=[("musq",)])
        S.add("act", lambda e: e.activation(out=rstd_c[:, 0:ntok], in_=rstd_c[:, 0:ntok], func=AF.Exp, scale=-0.5),
              reads=[("musq",)], writes=[("musq",)])
        for c in range(8):
            eng = "dve"
            S.add(eng, lambda e, c=c: e.tensor_tensor(out=dwa[:, c, 0:ntok], in0=dwa[:, c, 0:ntok], in1=mu[:, 0:ntok],
                                                      op=ALU.subtract),
                  reads=[("dwa", c), ("mu",)], writes=[("dwa", c)])
            S.add(eng, lambda e, c=c: e.tensor_tensor(out=dwa[:, c, 0:ntok], in0=dwa[:, c, 0:ntok],
                                                      in1=rstd_c[:, 0:ntok], op=ALU.mult),
                  reads=[("dwa", c), ("musq",)], writes=[("dwa", c)])
            S.add("act", lambda e, c=c: e.activation(out=cT[:, c, 0:ntok], in_=dwa[:, c, 0:ntok], func=AF.Silu,
                                                     scale=lng[:, c:c + 1], bias=lnb[:, c:c + 1]),
                  reads=[("dwa", c)] + RC, writes=[("cT",)])
        for dp in range(4):
            wsc, wkc = wload(wb_pc[dp], ("wb_pc", dp))
            wvc = wsc[:].rearrange("p (k c) -> p k c", c=512)
            for half in range(2):
                wsq, wkq = wload(wb_in[P_GC + dp * 2 + half], ("wb_in", P_GC + dp * 2 + half), live=(wkc,))
                wvq = wsq[:].rearrange("p (k c) -> p k c", c=256)
                for ch in range(2):
                    dch = dp * 4 + half * 2 + ch
                    c0 = (half * 2 + ch) * 128
                    pbc, pbck = ps1()
                    pgc, pgck = ps1()
                    for c in range(8):
                        S.add("pe", lambda e, c=c, c0=c0, pbc=pbc, wvc=wvc: e.matmul(
                            pbc[:, 0:ntok], lhsT=wvc[:, c, c0:c0 + 128], rhs=cT[:, c, 0:ntok],
                            start=(c == 0), stop=(c == 7)), reads=[wkc, ("cT",)], writes=pbck)
                    for kc in range(KC):
                        S.add("pe", lambda e, kc=kc, ch=ch, pgc=pgc, wvq=wvq: e.matmul(
                            pgc[:, 0:ntok], lhsT=wvq[:, kc, ch * 128:(ch + 1) * 128], rhs=hT[:, kc, 0:ntok],
                            start=(kc == 0), stop=(kc == KC - 1)), reads=[wkq, key_of(hT)], writes=pgck)
                    tg, tgk = (tmp3, ("tmp3",)) if dch % 2 == 0 else (tmp4, ("tmp4",))
                    S.add("act", lambda e, pgc=pgc, tg=tg: e.activation(out=tg[:, 0:ntok], in_=pgc[:, 0:ntok],
                                                                        func=AF.Sigmoid), reads=pgck, writes=[tgk])
                    S.add("dve", lambda e, pbc=pbc, tg=tg: e.tensor_tensor(out=tg[:, 0:ntok], in0=pbc[:, 0:ntok],
                                                                           in1=tg[:, 0:ntok], op=ALU.mult),
                          reads=pbck + [tgk], writes=[tgk])
                    S.add("dve", lambda e, dch=dch, tg=tg: e.tensor_tensor(out=mT[:, dch, 0:ntok],
                                                                           in0=mT[:, dch, 0:ntok], in1=tg[:, 0:ntok],
                                                                           op=ALU.add),
                          reads=[tgk, ("mT", dch)], writes=[("mT", dch)])
        for dpc in range(8):
            ws, wk = wload(wb_out[dpc], ("wb_out", dpc))
            wv = ws[:].rearrange("p (k c) -> p k c", c=256)
            for i in range(nsub):
                pt, pk = ps1()
                for kc in range(KC):
                    S.add("pe", lambda e, kc=kc, i=i, pt=pt, wv=wv: e.matmul(
                        pt[:st, 0:256], lhsT=mT[:, kc, i * 128:i * 128 + st], rhs=wv[:, kc, :],
                        start=(kc == 0), stop=(kc == KC - 1)), reads=[wk, key_of(mT)], writes=pk)
                S.add("dve", lambda e, i=i, dpc=dpc, pt=pt: e.tensor_tensor(
                    out=xin[:st, i, dpc * 256:(dpc + 1) * 256], in0=xin[:st, i, dpc * 256:(dpc + 1) * 256],
                    in1=pt[:st, 0:256], op=ALU.add),
                    reads=pk + [("xin", i)], writes=[("xin", i)])
        dbg_dump("on", onT[:].rearrange("p h t -> p (h t)"), [("onT",)])
        dbg_dump("c", cT[:].rearrange("p h t -> p (h t)"), [("cT",)])
        dbg_dump("m", mT[:].rearrange("p h t -> p (h t)"), [key_of(mT)])
        dbg_dump("x1", xin[:, 0, :], [("xin", 0)])
        for i in range(nsub):
            load_g("g2")
            norm_to_hT(i, st, None, h2T, i * 128)
        for p in range(NP_FF1):
            def cons_ff(ch, pt, pk, p=p):
                fc = 2 * p + ch
                ft = ftmp[fc % 2]
                ftk = ("ftmp", fc % 2)
                S.add("act", lambda e: e.activation(out=ft[:, 0:ntok], in_=pt[:, 0:ntok], func=AF.Relu),
                      reads=pk, writes=[ftk])
                S.add("pool", lambda e: e.tensor_tensor(out=hfT[:, fc, 0:ntok], in0=ft[:, 0:ntok],
                                                        in1=ft[:, 0:ntok], op=ALU.mult),
                      reads=[ftk], writes=[("hfT", fc)])
            proj_fm(wb_ff1[p], ("wb_ff1", p), 2, h2T, ntok, cons_ff)
        for dpc in range(8):
            pts = [ps1() for _ in range(nsub)]
            for fb in range(4):
                ws, wk = wload(wb_ff2[fb * 8 + dpc], ("wb_ff2", fb * 8 + dpc))
                wv = ws[:].rearrange("p (k c) -> p k c", c=256)
                for fc in range(16):
                    for i in range(nsub):
                        pt, pk = pts[i]
                        S.add("pe", lambda e, fb=fb, fc=fc, i=i, pt=pt, wv=wv: e.matmul(
                            pt[:st, 0:256], lhsT=hfT[:, fb * 16 + fc, i * 128:i * 128 + st], rhs=wv[:, fc, :],
                            start=(fb == 0 and fc == 0), stop=(fb == 3 and fc == 15)),
                            reads=[wk, ("hfT", fb * 16 + fc)], writes=pk)
            for i in range(nsub):
                pt, pk = pts[i]
                S.add("dve", lambda e, i=i, dpc=dpc, pt=pt: e.tensor_tensor(
                    out=xin[:st, i, dpc * 256:(dpc + 1) * 256], in0=xin[:st, i, dpc * 256:(dpc + 1) * 256],
                    in1=pt[:st, 0:256], op=ALU.add),
                    reads=pk + [("xin", i)], writes=[("xin", i)])
        dbg_dump("x2", xin[:, 0, :], [("xin", 0)])
        load_g("gf")
        for i in range(nsub):
            xk = ("xin", i)
            fb_ = 8 + 3 * (i % 2)
            S.add("pool", lambda e, fb_=fb_: e.memset(stat[:, fb_:fb_ + 1], 0.0), writes=[("stat", fb_)])
            S.add("act", lambda e, i=i, fb_=fb_: e.activation(out=hbufs[i % 2][:st, :], in_=xin[:st, i, :],
                                                              func=AF.Square, accum_out=stat[:st, fb_:fb_ + 1]),
                  reads=[xk, ("stat", fb_)], writes=[("hbuf", 0), ("stat", fb_)])
            S.add("act", lambda e, fb_=fb_: e.activation(out=stat[:st, fb_ + 1:fb_ + 2], in_=stat[:st, fb_:fb_ + 1],
                                                         func=AF.Ln, scale=1.0 / D, bias=epscol[:st, 0:1]),
                  reads=[("stat", fb_)] + RC, writes=[("stat", fb_ + 1)])
            S.add("act", lambda e, fb_=fb_: e.activation(out=stat[:st, fb_ + 2:fb_ + 3], in_=stat[:st, fb_ + 1:fb_ + 2],
                                                         func=AF.Exp, scale=-0.5),
                  reads=[("stat", fb_ + 1)], writes=[("stat", fb_ + 2)])
            S.add("dve", lambda e, i=i, fb_=fb_: e.scalar_tensor_tensor(out=xin[:st, i, :], in0=xin[:st, i, :],
                                                                        scalar=stat[:st, fb_ + 2:fb_ + 3],
                                                                        in1=grow[:st, :], op0=ALU.mult, op1=ALU.mult),
                  reads=[xk, ("stat", fb_ + 2), ("grow",)] + RC, writes=[xk])
            dst = y_dst_fn(i)
            S.add("sp", lambda e, i=i, dst=dst: e.dma_start(out=dst, in_=xin[:st, i, :]),
                  reads=[xk], writes=[("yout", len(S.ops["sp"]))], chan=("x", i))

    def conv_out_emitter(dst_ap):
        def f(ntok):
            pp, ppk = ps2()
            for c in range(8):
                S.add("pe", lambda e, c=c, pp=pp: e.matmul(
                    pp[0:30, c * 128:(c + 1) * 128], lhsT=uc[:, c, ntok:ntok + 30], rhs=ident_f,
                    start=True, stop=True), reads=[("uc",)] + RC, writes=ppk)
            S.add("act", lambda e, pp=pp: e.activation(out=cbuf[0:30, :], in_=pp[0:30, :], func=AF.Copy),
                  reads=ppk, writes=[("tmk",)])
            S.add("sp", lambda e: e.dma_start(out=dst_ap, in_=cbuf[0:30, :]),
                  reads=[("tmk",)], writes=[("cout", id(dst_ap))], chan=("cb",))
        return f

    def state_out_emitter(dst_ap):
        def f():
            S.add("sp", lambda e: e.dma_start(out=dst_ap.rearrange("h d v -> d h v"), in_=Sst[:]),
                  reads=[("Sst",)], writes=[("sout", id(dst_ap))], chan=("so",))
        return f

    for t in range(cfg.n_main):
        base = (cfg.n_hist + t) * T
        last = (t == cfg.n_main - 1)
        full_tile(T, NS, 128,
                  lambda i, base=base: xrows[base + i * 128:base + (i + 1) * 128, :],
                  lambda i, t=t: y_main[t * T + i * 128:t * T + (i + 1) * 128, :],
                  emit_conv_out=conv_out_emitter(cp_out[:, :]) if last else None,
                  emit_state_out=state_out_emitter(sp_out) if last else None)

    if cfg.sample:
        S.add("sp", lambda e: e.dma_start(out=Sst[:], in_=s0_in.rearrange("h d v -> d h v")),
              reads=[], writes=[("Sst",)], chan=("so",))
        S.add("pool", lambda e: e.tensor_copy(out=Sbf[:], in_=Sst[:]), reads=[("Sst",)], writes=[("Sbf",)])
        S.add("sp", lambda e: e.dma_start(out=cbuf[0:30, :], in_=c0_in[:, :]), reads=[], writes=[("tmk",)],
              chan=("cb",))
        for c in range(8):
            pt, pk = ps1()
            S.add("pe", lambda e, c=c, pt=pt: e.matmul(pt[:, 0:30], lhsT=cbuf[0:30, c * 128:(c + 1) * 128],
                                                        rhs=consts[0:30, C_ID:C_ID + 30], start=True, stop=True),
                  reads=[("tmk",)] + RC, writes=pk)
            S.add("dve", lambda e, c=c, pt=pt: e.tensor_copy(out=histtmp[:, c, :], in_=pt[:, 0:30]),
                  reads=pk, writes=[("histtmp",)])
        full_tile(DEC_SEQ, 1, DEC_SEQ,
                  lambda i: xs_in[:, :],
                  lambda i: y_s[:, :],
                  emit_conv_out=conv_out_emitter(cs_out[:, :]),
                  emit_state_out=state_out_emitter(ss_out))

    outkeys = [k for k in S.lastw if k[0] in ("yout", "cout", "sout")]
    S.add("sp", lambda e: e.nop(), reads=outkeys)

    S.finalize()
    sems = {e: es.enter_context(nc.semaphore(f"sem_{e}")) for e in Sched.ENG if e != "sp"}
    chan_sems = {}
    for ch in S.chan_count:
        chan_sems[ch] = es.enter_context(nc.semaphore("c_" + "_".join(str(x) for x in ch)))
    with nc.Block() as block:
        @block.tensor
        def _(e):
            S.emit("pe", e, sems, chan_sems)

        @block.scalar
        def _(e):
            S.emit("act", e, sems, chan_sems)

        @block.vector
        def _(e):
            S.emit("dve", e, sems, chan_sems)

        @block.gpsimd
        def _(e):
            S.emit("pool", e, sems, chan_sems)

        @block.sync
        def _(e):
            S.emit("sp", e, sems, chan_sems)
    es.close()
    n_ins = {e: len(S.ops[e]) for e in Sched.ENG}
    return nc, n_ins


HIST_ROWS = 12800
NS_TILE = 4


def core_inputs(cfg, xrows, xs, s0, c0, shared):
    m = dict(shared)
    m["xrows"] = np.ascontiguousarray(xrows, dtype=np.float32)
    m["xs"] = np.ascontiguousarray(xs, dtype=np.float32)
    m["s0"] = np.ascontiguousarray(s0, dtype=np.float32)
    m["c0"] = np.ascontiguousarray(c0, dtype=np.float32)
    return m


def shared_inputs(norm1_g, w_in, lb_logits, hgrn_norm_g, w_proj_h, dw_kernel, dw_bias, conv_ln_g, conv_ln_b,
                  w_proj_c, w_out, norm2_g, w_ff1, w_ff2, final_norm_g):
    f = lambda a: np.ascontiguousarray(np.asarray(a), dtype=np.float32)
    return {
        "w_in": f(w_in[0]), "w_proj_h": f(w_proj_h[0]), "w_proj_c": f(w_proj_c[0]), "w_out": f(w_out[0]),
        "w_ff1": f(w_ff1[0]), "w_ff2": f(w_ff2[0]),
        "norm1_g": f(norm1_g[0]).reshape(1, D), "norm2_g": f(norm2_g[0]).reshape(1, D),
        "final_norm_g": f(final_norm_g).reshape(1, D), "lb_logits": f(lb_logits),
        "hgrn_norm_g": f(hgrn_norm_g[0]).reshape(128, 1), "dw_kernel": f(dw_kernel[0]),
        "dw_bias": f(dw_bias[0]).reshape(CW, 1), "conv_ln_g": f(conv_ln_g[0]).reshape(CW, 1),
        "conv_ln_b": f(conv_ln_b[0]).reshape(CW, 1), "consts": make_consts(),
    }


def kernel(x_prompt, x_sample, state_hgrn, state_conv, meta_tokens, norm1_g, w_in, lb_logits,
           hgrn_norm_g, w_proj_h, dw_kernel, dw_bias, conv_ln_g, conv_ln_b, w_proj_c, w_out,
           norm2_g, w_ff1, w_ff2, final_norm_g):
    x_prompt = np.asarray(x_prompt, dtype=np.float32)
    x_sample = np.asarray(x_sample, dtype=np.float32)
    state_hgrn = np.asarray(state_hgrn, dtype=np.float32)
    state_conv = np.asarray(state_conv, dtype=np.float32)
    meta = np.asarray(meta_tokens, dtype=np.float32)
    T = NS_TILE * 128
    cfg = Cfg(n_hist=HIST_ROWS // T, n_main=CHUNK_TOK // T, ns=NS_TILE)
    shared = shared_inputs(norm1_g, w_in, lb_logits, hgrn_norm_g, w_proj_h, dw_kernel, dw_bias, conv_ln_g,
                           conv_ln_b, w_proj_c, w_out, norm2_g, w_ff1, w_ff2, final_norm_g)
    in_maps = []
    for c in range(8):
        s, j = c // 4, c % 4
        start = j * CHUNK_TOK
        rows = np.zeros((HIST_ROWS + CHUNK_TOK, D), np.float32)
        rows[HIST_ROWS:] = x_prompt[s, start:start + CHUNK_TOK]
        lo = start - HIST_ROWS
        if lo >= 0:
            rows[:HIST_ROWS] = x_prompt[s, lo:start]
        else:
            nx = start
            if nx > 0:
                rows[HIST_ROWS - nx:HIST_ROWS] = x_prompt[s, 0:start]
            rows[HIST_ROWS - nx - N_META:HIST_ROWS - nx] = meta
        in_maps.append(core_inputs(cfg, rows, x_sample[c], state_hgrn[0, c], state_conv[0, c], shared))
    nc, _ = build_program(cfg)
    res = run_bass_kernel_spmd(nc, in_maps, core_ids=list(range(8)))
    R = res.results
    y_prompt = np.stack([np.concatenate([R[s * 4 + j]["y_main"] for j in range(4)], axis=0) for s in range(2)], 0)
    y_sample = np.stack([R[c]["y_s"] for c in range(8)], 0)
    new_hp = np.stack([R[3]["s_p"], R[7]["s_p"]], 0)[None]
    new_cp = np.stack([R[3]["c_p"], R[7]["c_p"]], 0)[None]
    new_hs = np.stack([R[c]["s_s"] for c in range(8)], 0)[None]
    new_cs = np.stack([R[c]["c_s"] for c in range(8)], 0)[None]
    return (y_prompt.astype(np.float32), y_sample.astype(np.float32), new_hp.astype(np.float32),
            new_cp.astype(np.float32), new_hs.astype(np.float32), new_cs.astype(np.float32))
```
